# Optimizing a Trainium2 kernel written in Bass

```python
import math
import jax
import jax.numpy as jnp
from jax import lax
import numpy as np

D_MODEL = 1024
BATCH = 2
SEQ = 16384
DEPTH = 2

GRID_W = 64
CTX_LEN = 256
N_MIXERS = 2
N_MLA_LAYERS = (DEPTH + 1) // 2
N_GDN_LAYERS = DEPTH // 2
DEEPNORM_ALPHA = (2.0 * DEPTH) ** 0.25
DEEPNORM_BETA = (8.0 * DEPTH) ** -0.25
N_MOD = 6
LN_EPS = 1e-5
RMS_EPS = 1e-6

MLA_HEADS = D_MODEL // 128
MLA_Q_RANK = D_MODEL // 2
MLA_KV_RANK = D_MODEL // 4
MLA_NOPE = 128
MLA_ROPE = 64
MLA_V = 128
MLA_IN_W = MLA_Q_RANK + MLA_KV_RANK + MLA_ROPE
MLA_SCALE = (MLA_NOPE + MLA_ROPE) ** -0.5
Q_BLOCK = 128
ROPE_THETA = 10000.0

GDN_K_HEADS = D_MODEL // 128
GDN_V_HEADS = 2 * GDN_K_HEADS
GDN_DK = 128
GDN_DV = 128
GDN_KW = GDN_K_HEADS * GDN_DK
GDN_VW = GDN_V_HEADS * GDN_DV
GDN_QKV_W = 2 * GDN_KW + GDN_VW
GDN_IN_W = GDN_QKV_W + GDN_VW + 4 * GDN_V_HEADS
GDN_CONV = 5
CHUNK = 64

D_FF = int(math.ceil(8 * D_MODEL / 3 / 256)) * 256

kernel_name = 'hybrid_mla_gdn_dit_prefix_trunk'


def layer_norm(x, g, b):
    xf = x.astype(jnp.float32)
    mu = jnp.mean(xf, axis=-1, keepdims=True)
    var = jnp.mean(jnp.square(xf - mu), axis=-1, keepdims=True)
    return ((xf - mu) * lax.rsqrt(var + LN_EPS) * g + b).astype(x.dtype)


def rms_norm(x, g):
    xf = x.astype(jnp.float32)
    y = xf * lax.rsqrt(jnp.mean(jnp.square(xf), axis=-1, keepdims=True) + RMS_EPS) * g
    return y.astype(x.dtype)


def l2_norm(x):
    xf = x.astype(jnp.float32)
    return (xf * lax.rsqrt(jnp.sum(jnp.square(xf), axis=-1, keepdims=True) + RMS_EPS)).astype(x.dtype)


def modulate(h, shift, scale):
    return h * (1 + scale) + shift


def axial_rope_tables(n_tokens):
    rows = n_tokens // GRID_W
    row = jnp.broadcast_to(jnp.arange(rows)[:, None], (rows, GRID_W)).reshape(-1).astype(jnp.float32)
    col = jnp.broadcast_to(jnp.arange(GRID_W)[None, :], (rows, GRID_W)).reshape(-1).astype(jnp.float32)
    n_freq = MLA_ROPE // 4
    axis_dim = MLA_ROPE // 2
    inv_freq = ROPE_THETA ** (-(2.0 * jnp.arange(n_freq, dtype=jnp.float32)) / axis_dim)
    ang = jnp.concatenate([row[:, None] * inv_freq, col[:, None] * inv_freq], axis=-1)
    return jnp.cos(ang), jnp.sin(ang)


def apply_axial_rope(u, cos, sin):
    B, T, H, _ = u.shape
    f = MLA_ROPE // 4
    u = u.reshape(B, T, H, 2, 2, f)
    u1, u2 = u[..., 0, :], u[..., 1, :]
    cs = cos.reshape(1, T, 1, 2, f).astype(u.dtype)
    sn = sin.reshape(1, T, 1, 2, f).astype(u.dtype)
    out = jnp.stack([u1 * cs - u2 * sn, u2 * cs + u1 * sn], axis=-2)
    return out.reshape(B, T, H, MLA_ROPE)


def softmax_attend(q, k, v):
    s = jnp.einsum('bqhd,bkhd->bhqk', q, k).astype(jnp.float32) * MLA_SCALE
    p = jax.nn.softmax(s, axis=-1).astype(v.dtype)
    return jnp.einsum('bhqk,bkhd->bqhd', p, v)


def blockwise_attend(q, k, v):
    B, T, H, dq = q.shape
    qb = jnp.moveaxis(q.reshape(B, T // Q_BLOCK, Q_BLOCK, H, dq), 1, 0)
    o = lax.map(lambda blk: softmax_attend(blk, k, v), qb)
    return jnp.moveaxis(o, 0, 1).reshape(B, T, H, v.shape[-1])


def mla_project(h, w_in, q_norm, kv_norm, w_q_up, w_kv_up, rope):
    B, T, _ = h.shape
    lat = h @ w_in
    cq = rms_norm(lat[..., :MLA_Q_RANK], q_norm)
    ckv = rms_norm(lat[..., MLA_Q_RANK:MLA_Q_RANK + MLA_KV_RANK], kv_norm)
    k_rope = lat[..., MLA_Q_RANK + MLA_KV_RANK:][:, :, None, :]
    q = (cq @ w_q_up).reshape(B, T, MLA_HEADS, MLA_NOPE + MLA_ROPE)
    kv = (ckv @ w_kv_up).reshape(B, T, MLA_HEADS, MLA_NOPE + MLA_V)
    q_nope, q_rope = q[..., :MLA_NOPE], q[..., MLA_NOPE:]
    k_nope, v = kv[..., :MLA_NOPE], kv[..., MLA_NOPE:]
    if rope is not None:
        cos, sin = rope
        q_rope = apply_axial_rope(q_rope, cos, sin)
        k_rope = apply_axial_rope(k_rope, cos, sin)
    k_rope = jnp.broadcast_to(k_rope, (B, T, MLA_HEADS, MLA_ROPE))
    q = jnp.concatenate([q_nope, q_rope], axis=-1)
    k = jnp.concatenate([k_nope, k_rope], axis=-1)
    return q, k, v


def mla_mixer(h_ctx, h_lat, w_in, q_norm, kv_norm, w_q_up, w_kv_up, w_out, rope):
    q_c, k_c, v_c = mla_project(h_ctx, w_in, q_norm, kv_norm, w_q_up, w_kv_up, None)
    q_l, k_l, v_l = mla_project(h_lat, w_in, q_norm, kv_norm, w_q_up, w_kv_up, rope)
    o_c = softmax_attend(q_c, k_c, v_c)
    o_l = blockwise_attend(q_l, jnp.concatenate([k_c, k_l], axis=1), jnp.concatenate([v_c, v_l], axis=1))

    def out(o):
        B, T = o.shape[:2]
        return o.reshape(B, T, MLA_HEADS * MLA_V) @ w_out

    return out(o_c), out(o_l)


def short_conv(u, w):
    out = lax.conv_general_dilated(u, w[:, None, :], window_strides=(1,),
                                   padding=[(GDN_CONV // 2, GDN_CONV // 2)],
                                   dimension_numbers=('NWC', 'WIO', 'NWC'),
                                   feature_group_count=u.shape[-1])
    return jax.nn.silu(out)


def gated_delta_chunked(q, k, v, beta, g, s0):
    B, T, H, dk = q.shape
    dv = v.shape[-1]
    n = T // CHUNK
    f32 = jnp.float32

    def chunks(a):
        return jnp.moveaxis(a.astype(f32).reshape((B, n, CHUNK) + a.shape[2:]), 3, 1)

    qc, kc, vc, bc, gc = chunks(q), chunks(k), chunks(v), chunks(beta), chunks(g)
    gcum = jnp.cumsum(gc, axis=-1)
    idx = jnp.arange(CHUNK)
    incl = idx[:, None] >= idx[None, :]
    strict = idx[:, None] > idx[None, :]
    diff = gcum[..., :, None] - gcum[..., None, :]
    decay = jnp.where(incl, jnp.exp(jnp.where(incl, diff, 0.0)), 0.0)
    kb = kc * bc[..., None]
    lmat = jnp.where(strict, jnp.einsum('bhnid,bhnjd->bhnij', kb, kc) * decay, 0.0)
    a = lmat + jnp.eye(CHUNK, dtype=f32)
    rhs = jnp.concatenate([vc * bc[..., None], kb * jnp.exp(gcum)[..., None]], axis=-1)
    sol = lax.linalg.triangular_solve(a, rhs, left_side=True, lower=True, unit_diagonal=True)
    u, w = sol[..., :dv], sol[..., dv:]
    attn = jnp.einsum('bhnid,bhnjd->bhnij', qc, kc) * decay
    q_dec = qc * jnp.exp(gcum)[..., None]
    k_dec = kc * jnp.exp(gcum[..., -1:] - gcum)[..., None]
    g_last = jnp.exp(gcum[..., -1])
    xs = tuple(jnp.moveaxis(t, 2, 0) for t in (w, u, q_dec, k_dec, attn, g_last))

    def step(S, inp):
        w_i, u_i, qd_i, kd_i, a_i, gl_i = inp
        v_new = u_i - jnp.einsum('bhck,bhkv->bhcv', w_i, S)
        o_i = jnp.einsum('bhck,bhkv->bhcv', qd_i, S) + jnp.einsum('bhij,bhjv->bhiv', a_i, v_new)
        S = S * gl_i[..., None, None] + jnp.einsum('bhck,bhcv->bhkv', kd_i, v_new)
        return S, o_i

    s_final, o = lax.scan(step, s0, xs)
    o = jnp.transpose(o, (1, 0, 3, 2, 4)).reshape(B, T, H, dv)
    return o.astype(v.dtype), s_final


def gdn_project(h, w_in, conv_w, a_log, dt_bias):
    B, T, _ = h.shape
    proj = h @ w_in
    qkv = short_conv(proj[..., :GDN_QKV_W], conv_w)
    rep = GDN_V_HEADS // GDN_K_HEADS
    q = l2_norm(qkv[..., :GDN_KW].reshape(B, T, GDN_K_HEADS, GDN_DK)) * (GDN_DK ** -0.5)
    k = l2_norm(qkv[..., GDN_KW:2 * GDN_KW].reshape(B, T, GDN_K_HEADS, GDN_DK))
    q = jnp.repeat(q, rep, axis=2)
    k = jnp.repeat(k, rep, axis=2)
    v = qkv[..., 2 * GDN_KW:].reshape(B, T, GDN_V_HEADS, GDN_DV)
    z = proj[..., GDN_QKV_W:GDN_QKV_W + GDN_VW].reshape(B, T, GDN_V_HEADS, GDN_DV)
    ba = proj[..., GDN_QKV_W + GDN_VW:].astype(jnp.float32).reshape(B, T, 2, 2, GDN_V_HEADS)
    beta = jax.nn.sigmoid(ba[..., 0, :])
    g = -jnp.exp(a_log.astype(jnp.float32)) * jax.nn.softplus(ba[..., 1, :] + dt_bias.astype(jnp.float32))
    return q, k, v, z, beta, g


def gdn_mixer(h_ctx, h_lat, w_in, conv_w, a_log, dt_bias, norm_w, w_out):
    ctx_t = gdn_project(h_ctx, w_in, conv_w, a_log, dt_bias)
    lat_t = gdn_project(h_lat, w_in, conv_w, a_log, dt_bias)
    s0 = jnp.zeros((h_lat.shape[0], GDN_V_HEADS, GDN_DK, GDN_DV), jnp.float32)

    def one_direction(d):
        flip = (lambda a: jnp.flip(a, axis=1)) if d == 1 else (lambda a: a)

        def run(t, s_init):
            q, k, v, _, beta, g = t
            o, s_fin = gated_delta_chunked(flip(q), flip(k), flip(v), flip(beta[:, :, d]), flip(g[:, :, d]), s_init)
            return flip(o), s_fin

        o_ctx, s_ctx = run(ctx_t, s0)
        o_lat, _ = run(lat_t, s_ctx)
        return o_ctx, o_lat

    oc_f, ol_f = one_direction(0)
    oc_b, ol_b = one_direction(1)

    def finish(o, z):
        B, T = o.shape[:2]
        y = rms_norm(o, norm_w) * jax.nn.silu(z)
        return y.reshape(B, T, GDN_VW) @ w_out

    return finish(oc_f + oc_b, ctx_t[3]), finish(ol_f + ol_b, lat_t[3])


def swiglu(h, w_in, w_out):
    gate, up = jnp.split(h @ w_in, 2, axis=-1)
    return (jax.nn.silu(gate) * up) @ w_out


def setup_inputs(seed: int = 0) -> dict:
    key = jax.random.key(seed)
    ks = jax.random.split(key, 24)
    f32 = jnp.float32

    def nrm(i, shape, std=1.0):
        return std * jax.random.normal(ks[i], shape, f32)

    D = D_MODEL
    nA, nB = N_MLA_LAYERS, N_GDN_LAYERS
    a_log = jnp.log(jax.random.uniform(ks[20], (nB, 2, GDN_V_HEADS), f32, 1.0, 16.0))
    dt = jnp.exp(jax.random.uniform(ks[21], (nB, 2, GDN_V_HEADS), f32, math.log(1e-3), math.log(1e-1)))
    dt_bias = dt + jnp.log(-jnp.expm1(-dt))
    return {
        'x': nrm(0, (BATCH, SEQ, D)),
        'c': nrm(1, (BATCH, D)),
        'ctx': nrm(2, (BATCH, CTX_LEN, D)),
        'c_ctx': nrm(3, (D,)),
        'w_mod': nrm(4, (DEPTH, D, N_MOD * D), D ** -0.5),
        'b_mod': nrm(5, (DEPTH, N_MOD * D), 0.01),
        'ln1_g': 1.0 + nrm(6, (DEPTH, D), 0.02),
        'ln1_b': nrm(7, (DEPTH, D), 0.01),
        'ln2_g': 1.0 + nrm(8, (DEPTH, D), 0.02),
        'ln2_b': nrm(9, (DEPTH, D), 0.01),
        'w_ffn_in': nrm(10, (DEPTH, D, 2 * D_FF), D ** -0.5),
        'w_ffn_out': nrm(11, (DEPTH, D_FF, D), DEEPNORM_BETA * D_FF ** -0.5),
        'mla_w_in': nrm(12, (nA, D, MLA_IN_W), D ** -0.5),
        'mla_q_norm': 1.0 + nrm(13, (nA, MLA_Q_RANK), 0.02),
        'mla_kv_norm': 1.0 + nrm(14, (nA, MLA_KV_RANK), 0.02),
        'mla_w_q_up': nrm(15, (nA, MLA_Q_RANK, MLA_HEADS * (MLA_NOPE + MLA_ROPE)), MLA_Q_RANK ** -0.5),
        'mla_w_kv_up': nrm(16, (nA, MLA_KV_RANK, MLA_HEADS * (MLA_NOPE + MLA_V)), MLA_KV_RANK ** -0.5),
        'mla_w_out': nrm(17, (nA, MLA_HEADS * MLA_V, D), DEEPNORM_BETA * (MLA_HEADS * MLA_V) ** -0.5),
        'gdn_w_in': nrm(18, (nB, D, GDN_IN_W), D ** -0.5),
        'gdn_conv': nrm(19, (nB, GDN_CONV, GDN_QKV_W), GDN_CONV ** -0.5),
        'gdn_a_log': a_log,
        'gdn_dt_bias': dt_bias,
        'gdn_norm': 1.0 + nrm(22, (nB, GDN_DV), 0.02),
        'gdn_w_out': nrm(23, (nB, GDN_VW, D), DEEPNORM_BETA * GDN_VW ** -0.5),
    }


def reference(x, c, ctx, c_ctx, w_mod, b_mod, ln1_g, ln1_b, ln2_g, ln2_b, w_ffn_in, w_ffn_out,
              mla_w_in, mla_q_norm, mla_kv_norm, mla_w_q_up, mla_w_kv_up, mla_w_out,
              gdn_w_in, gdn_conv, gdn_a_log, gdn_dt_bias, gdn_norm, gdn_w_out):
    rope = axial_rope_tables(x.shape[1])
    h_lat, h_ctx = x, ctx
    for i in range(DEPTH):
        last = i == DEPTH - 1
        mod_lat = jnp.split((jax.nn.silu(c) @ w_mod[i] + b_mod[i])[:, None, :], N_MOD, axis=-1)
        mod_ctx = jnp.split(jax.nn.silu(c_ctx) @ w_mod[i] + b_mod[i], N_MOD, axis=-1)
        u_ctx = modulate(h_ctx, mod_ctx[0], mod_ctx[1])
        u_lat = modulate(h_lat, mod_lat[0], mod_lat[1])
        j = i // N_MIXERS
        if i % N_MIXERS == 0:
            y_ctx, y_lat = mla_mixer(u_ctx, u_lat, mla_w_in[j], mla_q_norm[j], mla_kv_norm[j],
                                     mla_w_q_up[j], mla_w_kv_up[j], mla_w_out[j], rope)
        else:
            y_ctx, y_lat = gdn_mixer(u_ctx, u_lat, gdn_w_in[j], gdn_conv[j], gdn_a_log[j],
                                     gdn_dt_bias[j], gdn_norm[j], gdn_w_out[j])
        h_lat = layer_norm(DEEPNORM_ALPHA * h_lat + mod_lat[2] * y_lat, ln1_g[i], ln1_b[i])
        f_lat = swiglu(modulate(h_lat, mod_lat[3], mod_lat[4]), w_ffn_in[i], w_ffn_out[i])
        h_lat = layer_norm(DEEPNORM_ALPHA * h_lat + mod_lat[5] * f_lat, ln2_g[i], ln2_b[i])
        if not last:
            h_ctx = layer_norm(DEEPNORM_ALPHA * h_ctx + mod_ctx[2] * y_ctx, ln1_g[i], ln1_b[i])
            f_ctx = swiglu(modulate(h_ctx, mod_ctx[3], mod_ctx[4]), w_ffn_in[i], w_ffn_out[i])
            h_ctx = layer_norm(DEEPNORM_ALPHA * h_ctx + mod_ctx[5] * f_ctx, ln2_g[i], ln2_b[i])
    return h_lat
```

```python
import math
import numpy as np
import ml_dtypes
from contextlib import ExitStack
import concourse.bass as bass
import concourse.mybir as mybir
from concourse.bass_utils import run_bass_kernel_spmd

F32 = mybir.dt.float32
BF16 = mybir.dt.bfloat16
ALU = mybir.AluOpType
AF = mybir.ActivationFunctionType

D = 1024
SEQ = 16384
CTX = 256
NT = SEQ + CTX
NQ = SEQ // 4
NO = NQ + CTX
DFF = 2816
ALPHA = (2.0 * 2) ** 0.25
LN_EPS = 1e-5
RMS_EPS = 1e-6
MLA_SCALE = 192 ** -0.5


class Tok:
    __slots__ = ("w", "r", "sem")

    def __init__(self):
        self.w = {}
        self.r = {}
        self.sem = None


class B:
    def __init__(self, ap, tok=None, sb=True):
        self.ap = ap
        self.tok = tok if tok is not None else Tok()
        self.sb = sb

    def __getitem__(self, idx):
        return B(self.ap[idx], self.tok, self.sb)

    def re(self, pat, **kw):
        return B(self.ap.rearrange(pat, **kw), self.tok, self.sb)


class Ring:
    def __init__(self, bufs):
        self.bufs = bufs
        self.i = 0

    def next(self):
        b = self.bufs[self.i % len(self.bufs)]
        self.i += 1
        return b


class FW:
    ENG = ("pe", "act", "dve", "pool", "sp")

    def __init__(self, nc, stack):
        self.nc = nc
        self.outer = stack
        self.stack = stack
        self.lists = {e: [] for e in self.ENG}
        self.sems = {}
        self.cnt = {}
        self.seen = {e: {} for e in self.ENG}
        self.free_sems = []
        self.phase_sems = []
        self.nsem = 0
        self.uid = 0
        self.consts = {}
        self.ekey = {}
        for e in ("pe", "act", "dve", "pool"):
            self.ekey[e] = self._newsem("E_" + e)

    def _newsem(self, key):
        self.sems[key] = self.outer.enter_context(self.nc.semaphore(key))
        self.cnt[key] = 0
        self.nsem += 1
        return key

    def getsem(self):
        if self.free_sems:
            k = self.free_sems.pop()
        else:
            k = self._newsem("S%d" % self.nsem)
        self.phase_sems.append(k)
        return k

    def const(self, val):
        key = float(val)
        if key not in self.consts:
            assert self.stack is self.outer, "create consts before phases"
            saved = self.stack
            self.stack = self.outer
            c = self.sb("const", [128, 1], F32)
            self.stack = saved
            self.memset("dve", c, key)
            self.consts[key] = c
        return self.consts[key]

    def name(self, n):
        self.uid += 1
        return "%s_%d" % (n, self.uid)

    def sb(self, name, shape, dt):
        return B(self.stack.enter_context(self.nc.sbuf_tensor(self.name(name), list(shape), dt))[:])

    def ps(self, name, shape, dt=F32):
        return B(self.stack.enter_context(self.nc.psum_tensor(self.name(name), list(shape), dt))[:])

    def dram(self, name, shape, dt, kind="Internal"):
        return B(self.nc.dram_tensor(name, list(shape), dt, kind=kind).ap(), sb=False)

    def ring(self, name, n, shape, dt, ps=False):
        return Ring([(self.ps if ps else self.sb)(name, shape, dt) for _ in range(n)])

    def _waits(self, eng, reads, writes):
        deps = {}
        for t in reads:
            for k, v in t.w.items():
                deps[k] = max(deps.get(k, 0), v)
        for t in writes:
            for k, v in t.w.items():
                deps[k] = max(deps.get(k, 0), v)
            for k, v in t.r.items():
                deps[k] = max(deps.get(k, 0), v)
        out = []
        seen = self.seen[eng]
        for k, v in deps.items():
            if eng == "pe" and k.startswith("E_pe"):
                continue
            if seen.get(k, 0) >= v:
                continue
            seen[k] = v
            out.append((k, v))
        return out

    def op(self, eng, fn, reads=(), writes=()):
        reads = [b.tok for b in reads if isinstance(b, B)]
        writes = [b.tok for b in writes]
        waits = self._waits(eng, reads, writes)
        key = self.ekey[eng]
        if self.cnt[key] >= 6000:
            key = self.ekey[eng] = self._newsem("E_%s_%d" % (eng, self.nsem))
        self.cnt[key] += 1
        v = self.cnt[key]
        self.lists[eng].append((waits, fn, key, 1))
        for t in reads:
            t.r[key] = v
        for t in writes:
            t.w = {key: v}
            t.r = {}

    def dma(self, q, out, in_, **kw):
        sbside = out if out.sb else in_
        t = sbside.tok
        if t.sem is None:
            t.sem = self.getsem()
        key = t.sem
        reads = [in_.tok]
        writes = [out.tok]
        waits = self._waits(q, reads, writes)
        self.cnt[key] += 16
        v = self.cnt[key]
        oa, ia = out.ap, in_.ap
        self.lists[q].append((waits, lambda e: e.dma_start(out=oa, in_=ia, **kw), key, 16))
        for tk in reads:
            tk.r[key] = v
        for tk in writes:
            tk.w = dict(tk.w)
            tk.w[key] = v
            tk.r = {}

    def barrier(self):
        snap = {k: v for k, v in self.cnt.items() if v > 0}
        for e in self.ENG:
            seen = self.seen[e]
            waits = []
            for k, v in snap.items():
                if e == "pe" and k.startswith("E_pe"):
                    continue
                if seen.get(k, 0) >= v:
                    continue
                seen[k] = v
                waits.append((k, v))
            if waits:
                self.lists[e].append((waits, None, None, 0))

    class _Phase:
        def __init__(self, fw):
            self.fw = fw

        def __enter__(self):
            self.st = ExitStack()
            self.st.__enter__()
            self.prev = self.fw.stack
            self.fw.stack = self.st
            self.fw.phase_sems = []
            return self

        def __exit__(self, *a):
            self.fw.barrier()
            self.fw.free_sems.extend(self.fw.phase_sems)
            self.fw.phase_sems = []
            self.fw.stack = self.prev
            return self.st.__exit__(*a)

    def phase(self):
        return FW._Phase(self)

    def finish(self):
        self.barrier()
        sems = self.sems
        lists = self.lists
        with self.nc.Block() as block:
            def run(e, lst):
                for waits, fn, key, inc in lst:
                    for k, v in waits:
                        e.wait_ge(sems[k], v)
                    if fn is not None:
                        ins = fn(e)
                        if ins is not None and key is not None:
                            ins.then_inc(sems[key], inc)

            @block.sync
            def _(e):
                run(e, lists["sp"])

            @block.tensor
            def _(e):
                run(e, lists["pe"])

            @block.scalar
            def _(e):
                run(e, lists["act"])

            @block.vector
            def _(e):
                run(e, lists["dve"])

            @block.gpsimd
            def _(e):
                run(e, lists["pool"])

    def mm(self, out, lhsT, rhs, start=True, stop=True):
        o, l, r = out.ap, lhsT.ap, rhs.ap
        self.op("pe", lambda e: e.matmul(o, l, r, start=start, stop=stop), reads=[lhsT, rhs], writes=[out])

    def act(self, out, in_, func, bias=0.0, scale=1.0):
        o, i = out.ap, in_.ap
        b = bias.ap if isinstance(bias, B) else bias
        self.op("act", lambda e: e.activation(o, i, func, bias=b, scale=scale), reads=[in_, bias], writes=[out])

    def ts(self, eng, out, in0, s1, s2, op0, op1=None):
        o, i = out.ap, in0.ap
        a1 = s1.ap if isinstance(s1, B) else s1
        a2 = s2.ap if isinstance(s2, B) else s2
        if op1 is None:
            self.op(eng, lambda e: e.tensor_scalar(o, i, a1, None, op0), reads=[in0, s1], writes=[out])
        else:
            self.op(eng, lambda e: e.tensor_scalar(o, i, a1, a2, op0, op1), reads=[in0, s1, s2], writes=[out])

    def tt(self, eng, out, in0, in1, op):
        o, a, b = out.ap, in0.ap, in1.ap
        self.op(eng, lambda e: e.tensor_tensor(o, a, b, op), reads=[in0, in1], writes=[out])

    def stt(self, eng, out, in0, s, in1, op0, op1):
        o, a, b = out.ap, in0.ap, in1.ap
        sa = s.ap if isinstance(s, B) else s
        self.op(eng, lambda e: e.scalar_tensor_tensor(o, a, sa, b, op0, op1), reads=[in0, s, in1], writes=[out])

    def cp(self, eng, out, in_):
        o, i = out.ap, in_.ap
        if eng == "act":
            self.op(eng, lambda e: e.copy(o, i), reads=[in_], writes=[out])
        else:
            self.op(eng, lambda e: e.tensor_copy(o, i), reads=[in_], writes=[out])

    def memset(self, eng, out, val):
        o = out.ap
        self.op(eng, lambda e: e.memset(o, val), writes=[out])

    def recip(self, out, in_):
        o, i = out.ap, in_.ap
        self.op("dve", lambda e: e.reciprocal(o, i), reads=[in_], writes=[out])


class _Pre:
    def __init__(self, fw, pre):
        self._fw = fw
        self._pre = pre

    def __getattr__(self, k):
        return getattr(self._fw, k)

    def dram(self, name, shape, dt, kind="Internal"):
        return self._fw.dram(self._pre + name, shape, dt, kind)


def conv_weight(fw, src, dst, K, N, eng_cycle=("dve", "pool", "act")):
    kc = K // 128
    with fw.phase():
        CB = 2048
        fr = fw.ring("wcf", 2, [128, CB], F32)
        br = fw.ring("wcb", 2, [128, CB], BF16)
        i = 0
        for k in range(kc):
            for c0 in range(0, N, CB):
                n = min(CB, N - c0)
                f = fr.next()
                b = br.next()
                fw.dma("sp", f[:, :n], src[k * 128:(k + 1) * 128, c0:c0 + n])
                fw.cp(eng_cycle[i % len(eng_cycle)], b[:, :n], f[:, :n])
                fw.dma("pool", dst[:, k, c0:c0 + n], b[:, :n])
                i += 1


def conv_weight_gen(fw, jobs, fr, br, engs=("dve", "pool")):
    CB = 2048
    i = 0
    for (src, dst, K, N) in jobs:
        for k in range(K // 128):
            for c0 in range(0, N, CB):
                n = min(CB, N - c0)
                f = fr.next()
                b = br.next()
                fw.dma("sp", f[:, :n], src[k * 128:(k + 1) * 128, c0:c0 + n])
                fw.cp(engs[i % len(engs)], b[:, :n], f[:, :n])
                fw.dma("pool", dst[:, k, c0:c0 + n], b[:, :n])
                i += 1
                yield


def mod_phase(fw, cc, wmod, bmodT, modT):
    with fw.phase():
        ccs = fw.sb("ccs", [128, 8, 2], F32)
        bm = fw.sb("bm", [128, 48], F32)
        fw.dma("sp", ccs, cc)
        fw.dma("sp", bm, bmodT)
        fw.act(ccs, ccs, AF.Silu)
        modp = fw.ps("modp", [128, 48, 2], F32)
        wr = fw.ring("wm", 2, [128, 8, 1024], F32)
        wv = wmod.re("(k p) o -> p k o", p=128)
        for blk in range(6):
            w = wr.next()
            fw.dma("sp", w, wv[:, :, blk * 1024:(blk + 1) * 1024])
            for o8 in range(8):
                oc = blk * 8 + o8
                for k in range(8):
                    fw.mm(modp[:, oc, :], w[:, k, o8 * 128:(o8 + 1) * 128], ccs[:, k, :], k == 0, k == 7)
        for j in range(2):
            fw.tt("dve", modT[:, :, j], modp[:, :, j], bm, ALU.add)


def layer_norm_fm(fw, r, n, onesm, pstat, tmp, outs):
    sq, mean, rstd, d = tmp
    fw.tt("pool", sq[:, :, :n], r[:, :, :n], r[:, :, :n], ALU.mult)
    pm = pstat[:, 0, :n]
    pq = pstat[:, 1, :n]
    for oc in range(8):
        fw.mm(pm, onesm, r[:, oc, :n], oc == 0, oc == 7)
    for oc in range(8):
        fw.mm(pq, onesm, sq[:, oc, :n], oc == 0, oc == 7)
    fw.cp("act", mean[:, :n], pm)
    fw.tt("dve", rstd[:, :n], mean[:, :n], mean[:, :n], ALU.mult)
    fw.tt("dve", rstd[:, :n], pq, rstd[:, :n], ALU.subtract)
    fw.act(rstd[:, :n], rstd[:, :n], AF.Ln, bias=fw.const(LN_EPS), scale=1.0)
    fw.act(rstd[:, :n], rstd[:, :n], AF.Exp, scale=-0.5)
    for oc in range(8):
        fw.tt("dve", d[:, oc, :n], r[:, oc, :n], mean[:, :n], ALU.subtract)
        fw.tt("pool", d[:, oc, :n], d[:, oc, :n], rstd[:, :n], ALU.mult)
        for (dst, gf, bf, eng) in outs:
            fw.ts(eng, dst[:, oc, :n], d[:, oc, :n], gf(oc), bf(oc), ALU.mult, ALU.add)


def post_phase(fw, mixT, KH, resT, wo_b, wfi_b, wfo_b, modT, lnT, outT, chunks, mixload=None, ydirect=None):
    with fw.phase():
        N = 256
        if ydirect is None:
            wo = fw.sb("wo", [128, KH, 1024], BF16)
        wfi = fw.sb("wfi", [128, 8, 2 * DFF], BF16)
        wfor = fw.ring("wfo", 3, [128, 1024], BF16)
        if ydirect is None:
            fw.dma("sp", wo, wo_b)
        for k in range(8):
            fw.dma("sp", wfi[:, k, :], wfi_b[:, k, :])
        onesm = fw.sb("onesm", [128, 128], F32)
        fw.memset("dve", onesm, 1.0 / 1024)
        G2 = fw.sb("G2", [128, 8, 2], F32)
        B2 = fw.sb("B2", [128, 8, 2], F32)
        for j in range(2):
            fw.ts("dve", G2[:, :, j], modT[:, 32:40, j], 1.0, None, ALU.add)
            fw.tt("dve", B2[:, :, j], G2[:, :, j], lnT[:, 1, :], ALU.mult)
            fw.tt("dve", B2[:, :, j], B2[:, :, j], modT[:, 24:32, j], ALU.add)
            fw.tt("dve", G2[:, :, j], G2[:, :, j], lnT[:, 0, :], ALU.mult)
        mixv = mixT.re("(h p) t -> p h t", p=128) if (mixload is None and ydirect is None) else None
        ydv = [y_.re("(k p) t -> p k t", p=128) for y_ in ydirect] if ydirect is not None else None
        resv = resT.re("(k p) t -> p k t", p=128)
        outv = outT.re("(k p) t -> p k t", p=128)
        mr = fw.ring("mx", 2, [128, KH, N], BF16) if ydirect is None else fw.ring("yd", 2, [128, 8, N], F32)
        xr = fw.ring("xs", 1, [128, 8, N], F32)
        rr = fw.ring("r", 1, [128, 8, N], F32)
        h1r = fw.ring("h1", 1, [128, 8, N], F32)
        u2r = fw.ring("u2", 1, [128, 8, N], BF16)
        dd = fw.sb("dd", [128, 8, N], F32)
        sq = dd
        mean = fw.sb("mean", [128, N], F32)
        rstd = fw.sb("rstd", [128, N], F32)
        sgr = fw.ring("sg", 3, [128, N], F32)
        ar = fw.ring("a", 3, [128, N], BF16)
        pfa = fw.ps("pfa", [128, 8, N], F32)
        pgu = fw.ring("pgu", 2, [128, 2, N], F32, ps=True)
        py = fw.ps("py", [128, 2, N], F32)
        pstat = fw.ps("pst", [128, 2, N], F32)
        tmp = (sq, mean, rstd, dd)
        for (t0, n, j) in chunks:
            ms = mr.next()
            xs = xr.next()
            if ydirect is not None:
                for i4 in range(4):
                    fw.dma("sp", ms[:, 2 * i4:2 * i4 + 2, :n], ydv[i4][:, :, t0:t0 + n])
            elif mixload is None:
                fw.dma("sp", ms[:, :, :n], mixv[:, :, t0:t0 + n])
            else:
                mixload(ms, t0, n)
            fw.dma("sp", xs[:, :, :n], resv[:, :, t0:t0 + n])
            fw.ts("pool", xs[:, :, :n], xs[:, :, :n], ALPHA, None, ALU.mult)
            r = rr.next()
            for oc in range(8):
                if ydirect is not None:
                    p = ms[:, oc, :n]
                else:
                    p = py[:, oc % 2, :n]
                    for h in range(KH):
                        fw.mm(p, wo[:, h, oc * 128:(oc + 1) * 128], ms[:, h, :n], h == 0, h == KH - 1)
                fw.stt("dve", r[:, oc, :n], p, modT[:, 16 + oc, j:j + 1], xs[:, oc, :n], ALU.mult, ALU.add)
            h1 = h1r.next()
            u2 = u2r.next()
            layer_norm_fm(fw, r, n, onesm, pstat, tmp, [
                (h1, lambda oc: lnT[:, 0, oc:oc + 1], lambda oc: lnT[:, 1, oc:oc + 1], "dve"),
                (u2, lambda oc: G2[:, oc, j:j + 1], lambda oc: B2[:, oc, j:j + 1], "pool"),
            ])
            def gateup(m):
                pp = pgu.next()
                for k in range(8):
                    fw.mm(pp[:, 0, :n], wfi[:, k, m * 128:(m + 1) * 128], u2[:, k, :n], k == 0, k == 7)
                for k in range(8):
                    fw.mm(pp[:, 1, :n], wfi[:, k, DFF + m * 128:DFF + (m + 1) * 128], u2[:, k, :n], k == 0, k == 7)
                sg = sgr.next()
                a = ar.next()
                fw.act(sg[:, :n], pp[:, 0, :n], AF.Silu)
                fw.tt("dve", a[:, :n], sg[:, :n], pp[:, 1, :n], ALU.mult)
                return a

            def down(m, a):
                wf = wfor.next()
                fw.dma("sp", wf, wfo_b[:, m, :])
                for oc in range(8):
                    fw.mm(pfa[:, oc, :n], wf[:, oc * 128:(oc + 1) * 128], a[:, :n], m == 0 and oc % 2 == 0, m == 21)

            prev = gateup(0)
            for m in range(1, 22):
                cur = gateup(m)
                down(m - 1, prev)
                prev = cur
            down(21, prev)
            fw.ts("pool", h1[:, :, :n], h1[:, :, :n], ALPHA, None, ALU.mult)
            r2 = rr.next()
            for oc in range(8):
                fw.stt("dve", r2[:, oc, :n], pfa[:, oc, :n], modT[:, 40 + oc, j:j + 1], h1[:, oc, :n], ALU.mult, ALU.add)
            h2 = xs
            layer_norm_fm(fw, r2, n, onesm, pstat, tmp, [
                (h2, lambda oc: lnT[:, 2, oc:oc + 1], lambda oc: lnT[:, 3, oc:oc + 1], "dve"),
            ])
            fw.dma("pool", outv[:, :, t0:t0 + n], h2[:, :, :n])


def build_l0(stop_after=None, dbg=False):
    nc = bass.Bass("TRN2", target_bir_lowering=False)
    with ExitStack() as st:
        fw = FW(nc, st)
        emit_l0(fw, stop_after=stop_after, dbg=dbg)
        fw.finish()
    return nc


def emit_l0(fw, pre="", outT=None, stop_after=None, dbg=False, extra_jobs=()):
    if True:
        _d = fw.dram
        fw = _Pre(fw, pre)
        EI = "ExternalInput"
        SK = "ExternalOutput" if dbg else "Internal"
        xT = fw.dram("xT", [D, NT], F32, EI)
        xTo = fw.dram("xTo", [D, NO], F32, EI)
        cc = fw.dram("cc", [128, 8, 2], F32, EI)
        wmod = fw.dram("wmod", [D, 6 * D], F32, EI)
        bmodT = fw.dram("bmodT", [128, 48], F32, EI)
        lnTd = fw.dram("lnT", [128, 4, 8], F32, EI)
        w_in = fw.dram("w_in", [D, 896], F32, EI)
        qnT = fw.dram("qnT", [128, 4], F32, EI)
        kvnT = fw.dram("kvnT", [128, 2], F32, EI)
        w_q = fw.dram("w_q", [512, 3072], F32, EI)
        w_kv = fw.dram("w_kv", [256, 2048], F32, EI)
        w_o = fw.dram("w_o", [D, D], F32, EI)
        w_fi = fw.dram("w_fi", [D, 2 * DFF], F32, EI)
        w_fo = fw.dram("w_fo", [DFF, D], F32, EI)
        ropeK = fw.dram("ropeK", [64, 2, NT], F32, EI)
        ropeQ = fw.dram("ropeQ", [128, 2, NO], F32, EI)
        if outT is None:
            outT = fw.dram("outT", [D, NO], F32, "ExternalOutput")
        w_in_b = fw.dram("w_in_b", [128, 8, 896], BF16)
        w_q_b = fw.dram("w_q_b", [128, 4, 3072], BF16)
        w_kv_b = fw.dram("w_kv_b", [128, 2, 2048], BF16)
        w_o_b = fw.dram("w_o_b", [128, 8, D], BF16)
        w_fi_b = fw.dram("w_fi_b", [128, 8, 2 * DFF], BF16)
        w_fo_b = fw.dram("w_fo_b", [128, 22, D], BF16)
        KT = fw.dram("KT", [8, 128, NT], BF16, SK)
        VV = fw.dram("VV", [8, 128, NT // 128, 128], BF16, SK)
        KR = fw.dram("KR", [64, NT], BF16, SK)
        QN = fw.dram("QN", [8, 128, NO], BF16, SK)
        QR = fw.dram("QR", [8, 128, NO], BF16, SK)
        OT = fw.dram("OT", [D, NO], BF16, SK)

        modT = fw.sb("modT", [128, 48, 2], F32)
        lnT = fw.sb("lnTs", [128, 4, 8], F32)
        sc0 = fw.sb("sc0", [128, 8, 2], F32)
        fw.dma("sp", lnT, lnTd)
        fw.const(LN_EPS)
        fw.const(RMS_EPS)

        mod_phase(fw, cc, wmod, bmodT, modT)
        for j in range(2):
            fw.ts("dve", sc0[:, :, j], modT[:, 8:16, j], 1.0, None, ALU.add)
        conv_weight(fw, w_in, w_in_b, D, 896)
        conv_weight(fw, w_q, w_q_b, 512, 3072)
        conv_weight(fw, w_kv, w_kv_b, 256, 2048)
        jobs = [(w_o, w_o_b, D, D), (w_fi, w_fi_b, D, 2 * DFF), (w_fo, w_fo_b, DFF, D)] + list(extra_jobs)

        with fw.phase():
            win = fw.sb("win", [128, 8, 384], BF16)
            wkv = fw.sb("wkv", [128, 2, 2048], BF16)
            fw.dma("sp", win, w_in_b[:, :, 512:896])
            fw.dma("sp", wkv, w_kv_b)
            kvn = fw.sb("kvn", [128, 2], F32)
            fw.dma("sp", kvn, kvnT)
            ones = fw.sb("ones", [128, 128], BF16)
            fw.memset("dve", ones, 1.0)
            xr = fw.ring("xr", 2, [128, 8, 512], F32)
            ur = fw.ring("ur", 2, [128, 8, 512], BF16)
            cr = fw.ring("ckv", 2, [128, 2, 512], BF16)
            sqb = fw.sb("sqb", [128, 2, 512], BF16)
            rs = fw.sb("rs", [128, 512], F32)
            tr = fw.ring("tbl", 2, [64, 2, 512], F32)
            t1 = fw.sb("t1", [64, 512], F32)
            t2 = fw.sb("t2", [64, 512], F32)
            krr = fw.ring("krs", 2, [64, 512], BF16)
            ksr = fw.ring("kst", 3, [128, 512], BF16)
            vsr = fw.ring("vst", 3, [128, 512], BF16)
            pl = fw.ps("pl", [128, 2, 512], F32)
            pr = fw.ps("pr", [64, 2, 512], F32)
            pss = fw.ps("pss", [128, 512], F32)
            pkv = fw.ring("pkv", 3, [128, 512], F32, ps=True)
            xv = xT.re("(k p) t -> p k t", p=128)
            chunks = [(0, 256, 1)] + [(256 + 512 * i, 512, 0) for i in range(32)]
            ei = 0
            for (t0, n, j) in chunks:
                xs = xr.next()
                fw.dma("sp", xs[:, :, :n], xv[:, :, t0:t0 + n])
                tb = tr.next()
                fw.dma("sp", tb[:, :, :n], ropeK[:, :, t0:t0 + n])
                us = ur.next()
                for k in range(8):
                    fw.ts("dve" if k % 2 == 0 else "pool", us[:, k, :n], xs[:, k, :n], sc0[:, k, j:j + 1],
                          modT[:, k, j:j + 1], ALU.mult, ALU.add)
                for m in range(2):
                    for k in range(8):
                        fw.mm(pl[:, m, :n], win[:, k, m * 128:(m + 1) * 128], us[:, k, :n], k == 0, k == 7)
                for m in range(2):
                    for k in range(8):
                        fw.mm(pr[:, m, :n], win[:, k, 256 + m * 64:256 + (m + 1) * 64], us[:, k, :n], k == 0, k == 7)
                for m in range(2):
                    fw.act(sqb[:, m, :n], pl[:, m, :n], AF.Square)
                for m in range(2):
                    fw.mm(pss[:, :n], ones, sqb[:, m, :n], m == 0, m == 1)
                fw.act(rs[:, :n], pss[:, :n], AF.Ln, bias=fw.const(RMS_EPS), scale=1.0 / 256)
                fw.act(rs[:, :n], rs[:, :n], AF.Exp, scale=-0.5)
                cs = cr.next()
                for m in range(2):
                    fw.stt("dve", cs[:, m, :n], pl[:, m, :n], kvn[:, m:m + 1], rs[:, :n], ALU.mult, ALU.mult)
                fw.tt("dve", t1[:, :n], pr[:, 0, :n], tb[:, 0, :n], ALU.mult)
                fw.tt("dve", t2[:, :n], pr[:, 1, :n], tb[:, 1, :n], ALU.mult)
                krs = krr.next()
                fw.tt("pool", krs[:, :n], t1[:, :n], t2[:, :n], ALU.add)
                fw.dma("pool", KR[:, t0:t0 + n], krs[:, :n])
                for h in range(8):
                    pk = pkv.next()
                    for m in range(2):
                        fw.mm(pk[:, :n], wkv[:, m, h * 256:h * 256 + 128], cs[:, m, :n], m == 0, m == 1)
                    ks = ksr.next()
                    fw.cp("act" if ei % 2 == 0 else "dve", ks[:, :n], pk[:, :n])
                    ei += 1
                    fw.dma("pool", KT[h][:, t0:t0 + n], ks[:, :n])
                    pv = pkv.next()
                    for tq in range(n // 128):
                        for m in range(2):
                            fw.mm(pv[:, tq * 128:(tq + 1) * 128], cs[:, m, tq * 128:(tq + 1) * 128],
                                  wkv[:, m, h * 256 + 128:h * 256 + 256], m == 0, m == 1)
                    vs = vsr.next()
                    fw.cp("act" if ei % 2 == 0 else "dve", vs[:, :n], pv[:, :n])
                    ei += 1
                    fw.dma("pool", VV[h][:, t0 // 128:(t0 + n) // 128, :], vs[:, :n].re("p (t d) -> p t d", d=128))
        if stop_after == "P1":
            return

        with fw.phase():
            win = fw.sb("winq", [128, 8, 512], BF16)
            wq = fw.sb("wq", [128, 4, 3072], BF16)
            fw.dma("sp", win, w_in_b[:, :, 0:512])
            fw.dma("sp", wq, w_q_b)
            qn = fw.sb("qn", [128, 4], F32)
            fw.dma("sp", qn, qnT)
            ones = fw.sb("ones", [128, 128], BF16)
            fw.memset("dve", ones, 1.0)
            xr = fw.ring("xr", 2, [128, 8, 512], F32)
            ur = fw.ring("ur", 2, [128, 8, 512], BF16)
            cr = fw.ring("cq", 2, [128, 4, 512], BF16)
            sqb = fw.sb("sqb", [128, 4, 512], BF16)
            rs = fw.sb("rs", [128, 512], F32)
            tr = fw.ring("tbl", 2, [128, 2, 512], F32)
            t1 = fw.sb("t1", [128, 512], F32)
            t2 = fw.sb("t2", [128, 512], F32)
            qnr = fw.ring("qns", 3, [128, 512], BF16)
            qrr = fw.ring("qrs", 3, [128, 512], BF16)
            pq = fw.ps("pq", [128, 4, 512], F32)
            pss = fw.ps("pss", [128, 512], F32)
            pqo = fw.ring("pqo", 3, [128, 512], F32, ps=True)
            xv = xTo.re("(k p) t -> p k t", p=128)
            chunks = [(0, 256, 1)] + [(256 + 512 * i, 512, 0) for i in range(8)]
            ei = 0
            for (t0, n, j) in chunks:
                xs = xr.next()
                fw.dma("sp", xs[:, :, :n], xv[:, :, t0:t0 + n])
                tb = tr.next()
                fw.dma("sp", tb[:, :, :n], ropeQ[:, :, t0:t0 + n])
                us = ur.next()
                for k in range(8):
                    fw.ts("dve" if k % 2 == 0 else "pool", us[:, k, :n], xs[:, k, :n], sc0[:, k, j:j + 1],
                          modT[:, k, j:j + 1], ALU.mult, ALU.add)
                for m in range(4):
                    for k in range(8):
                        fw.mm(pq[:, m, :n], win[:, k, m * 128:(m + 1) * 128], us[:, k, :n], k == 0, k == 7)
                for m in range(4):
                    fw.act(sqb[:, m, :n], pq[:, m, :n], AF.Square)
                for m in range(4):
                    fw.mm(pss[:, :n], ones, sqb[:, m, :n], m == 0, m == 3)
                fw.act(rs[:, :n], pss[:, :n], AF.Ln, bias=fw.const(RMS_EPS), scale=1.0 / 512)
                fw.act(rs[:, :n], rs[:, :n], AF.Exp, scale=-0.5)
                cs = cr.next()
                for m in range(4):
                    fw.stt("dve", cs[:, m, :n], pq[:, m, :n], qn[:, m:m + 1], rs[:, :n], ALU.mult, ALU.mult)
                for h in range(8):
                    pn = pqo.next()
                    for k in range(4):
                        fw.mm(pn[:, :n], wq[:, k, h * 384:h * 384 + 128], cs[:, k, :n], k == 0, k == 3)
                    qs = qnr.next()
                    fw.cp("act", qs[:, :n], pn[:, :n])
                    fw.dma("pool", QN[h][:, t0:t0 + n], qs[:, :n])
                    pa = pqo.next()
                    for k in range(4):
                        fw.mm(pa[:, :n], wq[:, k, h * 384 + 128:h * 384 + 256], cs[:, k, :n], k == 0, k == 3)
                    pb = pqo.next()
                    for k in range(4):
                        fw.mm(pb[:, :n], wq[:, k, h * 384 + 256:h * 384 + 384], cs[:, k, :n], k == 0, k == 3)
                    fw.tt("dve", t1[:, :n], pa[:, :n], tb[:, 0, :n], ALU.mult)
                    fw.tt("dve", t2[:, :n], pb[:, :n], tb[:, 1, :n], ALU.mult)
                    qr_ = qrr.next()
                    fw.tt("pool", qr_[:, :n], t1[:, :n], t2[:, :n], ALU.add)
                    fw.dma("pool", QR[h][:, t0:t0 + n], qr_[:, :n])
        if stop_after == "P1b":
            return

        with fw.phase():
            NKT = NT // 128
            HALF = NKT // 2
            krp = fw.sb("krp", [128, HALF * 128], BF16)
            fw.dma("sp", krp[0:64, :], KR[:, 0:HALF * 128])
            fw.dma("sp", krp[64:128, :], KR[:, HALF * 128:NT])
            onesf = fw.sb("onesf", [128, 128], F32)
            fw.memset("dve", onesf, 1.0)
            kring = fw.ring("kb", 2, [128, NT], BF16)
            vring = fw.ring("vb", 2, [128, NKT, 128], BF16)
            qnr = fw.ring("qnb", 2, [128, 512], BF16)
            qrr = fw.ring("qrb", 2, [128, 512], BF16)
            ptr = fw.ring("pt", 7, [128, 512], BF16)
            acc0r = fw.ring("acc0", 2, [128, 512], F32)
            acc1r = fw.ring("acc1", 2, [128, 512], F32)
            rec = fw.sb("rec", [128, 512], F32)
            otr = fw.ring("ots", 2, [128, 512], BF16)
            pss_ = fw.ring("ps_s", 5, [128, 512], F32, ps=True)
            pso = fw.ring("ps_o", 2, [128, 512], F32, ps=True)
            psum_ = fw.ps("ps_sum", [128, 512], F32)
            qchunks = [(0, 256, 2)] + [(256 + 512 * i, 512, NKT) for i in range(8)]
            cfr = fw.ring("cfr", 2, [128, 2048], F32)
            cbr = fw.ring("cbr", 2, [128, 2048], BF16)
            cgen = conv_weight_gen(fw, jobs, cfr, cbr)
            for h in range(8):
                kb = kring.next()
                vb = vring.next()
                for c in range(5):
                    fw.dma("sp", kb[:, c * 3328:(c + 1) * 3328], KT[h][:, c * 3328:(c + 1) * 3328])
                for c in range(5):
                    fw.dma("sp", vb[:, c * 26:(c + 1) * 26, :], VV[h][:, c * 26:(c + 1) * 26, :])
                for (t0, n, nk) in qchunks:
                    qnb = qnr.next()
                    qrb = qrr.next()
                    fw.dma("sp", qnb[:, :n], QN[h][:, t0:t0 + n])
                    fw.dma("sp", qrb[:, :n], QR[h][:, t0:t0 + n])
                    po = pso.next()
                    acc0 = acc0r.next()
                    acc1 = acc1r.next()

                    def qk(jt):
                        ps = pss_.next()
                        fw.mm(ps[:, :n], kb[:, jt * 128:(jt + 1) * 128], qnb[:, :n], True, False)
                        hf, jj = jt // HALF, jt % HALF
                        fw.mm(ps[:, :n], krp[hf * 64:(hf + 1) * 64, jj * 128:(jj + 1) * 128],
                              qrb[hf * 64:(hf + 1) * 64, :n], False, True)
                        p = ptr.next()
                        fw.act(p[:, :n], ps[:, :n], AF.Exp, scale=MLA_SCALE)
                        if jt % 4 == 3 and nk >= 4:
                            if jt == 3:
                                fw.cp("pool", acc1[:, :n], p[:, :n])
                            else:
                                fw.tt("pool", acc1[:, :n], acc1[:, :n], p[:, :n], ALU.add)
                        else:
                            if jt == 0:
                                fw.cp("dve", acc0[:, :n], p[:, :n])
                            else:
                                fw.tt("dve", acc0[:, :n], acc0[:, :n], p[:, :n], ALU.add)
                        return p

                    def pv(jt, p):
                        fw.mm(po[:, :n], vb[:, jt, :], p[:, :n], jt == 0, jt == nk - 1)

                    LA = 3
                    pend = []
                    for jt in range(nk):
                        pend.append((jt, qk(jt)))
                        if len(pend) > LA:
                            j0_, p0_ = pend.pop(0)
                            pv(j0_, p0_)
                    for (j0_, p0_) in pend:
                        pv(j0_, p0_)
                    fw.mm(psum_[:, :n], onesf, acc0[:, :n], True, nk < 4)
                    if nk >= 4:
                        fw.mm(psum_[:, :n], onesf, acc1[:, :n], False, True)
                    fw.recip(rec[:, :n], psum_[:, :n])
                    ots = otr.next()
                    fw.tt("dve", ots[:, :n], po[:, :n], rec[:, :n], ALU.mult)
                    fw.dma("pool", OT[h * 128:(h + 1) * 128, t0:t0 + n], ots[:, :n])
                    for _ in range(2):
                        next(cgen, None)
            for _ in cgen:
                pass
        if stop_after == "P2":
            return

        chunks = [(0, 256, 1)] + [(256 + 256 * i, 256, 0) for i in range(16)]
        post_phase(fw, OT, 8, xTo, w_o_b, w_fi_b, w_fo_b, modT, lnT, outT, chunks)


def _fm(v, nchunk):
    return np.ascontiguousarray(np.asarray(v, np.float32).reshape(nchunk, 128).T)


def _rope_tables():
    rows = SEQ // 64
    row = np.repeat(np.arange(rows, dtype=np.float32), 64)
    col = np.tile(np.arange(64, dtype=np.float32), rows)
    inv = (np.float32(10000.0) ** (-(2.0 * np.arange(16, dtype=np.float32)) / np.float32(32))).astype(np.float32)
    ang = np.concatenate([row[:, None] * inv, col[:, None] * inv], axis=-1).astype(np.float32)
    cos, sin = np.cos(ang).astype(np.float32), np.sin(ang).astype(np.float32)
    r = np.arange(64)
    a, half, f = r // 32, (r % 32) // 16, r % 16
    cosT = cos[:, a * 16 + f].T
    sinT = (sin[:, a * 16 + f] * np.where(half == 0, -1.0, 1.0).astype(np.float32)).T
    tab = np.zeros((64, 2, NT), np.float32)
    tab[:, 0, :CTX] = 1.0
    tab[:, 0, CTX:] = cosT
    tab[:, 1, CTX:] = sinT
    return tab


_SWAP = np.array([(r // 32) * 32 + (1 - (r % 32) // 16) * 16 + r % 16 for r in range(64)])


def prep_l0(inp, b, qtr, tab):
    x, c, ctx, c_ctx = inp["x"], inp["c"], inp["ctx"], inp["c_ctx"]
    allx = np.concatenate([ctx[b], x[b]], axis=0)
    own = np.concatenate([np.arange(CTX), CTX + qtr * NQ + np.arange(NQ)])
    w_in = inp["mla_w_in"][0]
    w_in_ext = np.concatenate([w_in, w_in[:, 768 + _SWAP]], axis=1)
    wq = inp["mla_w_q_up"][0].reshape(512, 8, 192)
    rope_cols = wq[:, :, 128:]
    wq_ext = np.concatenate([wq[:, :, :128], rope_cols, rope_cols, rope_cols[:, :, _SWAP], rope_cols[:, :, _SWAP]], axis=2)
    tq = tab[:, :, own]
    return {
        "xT": np.ascontiguousarray(allx.T),
        "xTo": np.ascontiguousarray(allx[own].T),
        "cc": np.ascontiguousarray(np.stack([_fm(c[b], 8), _fm(c_ctx, 8)], axis=-1)),
        "wmod": np.ascontiguousarray(inp["w_mod"][0]),
        "bmodT": _fm(inp["b_mod"][0], 48),
        "lnT": np.ascontiguousarray(np.stack([_fm(inp[k][0], 8) for k in ("ln1_g", "ln1_b", "ln2_g", "ln2_b")], axis=1)),
        "w_in": np.ascontiguousarray(w_in_ext),
        "qnT": _fm(inp["mla_q_norm"][0], 4),
        "kvnT": _fm(inp["mla_kv_norm"][0], 2),
        "w_q": np.ascontiguousarray(wq_ext.reshape(512, 3072)),
        "w_kv": np.ascontiguousarray(inp["mla_w_kv_up"][0]),
        "w_o": np.ascontiguousarray(inp["mla_w_out"][0]),
        "w_fi": np.ascontiguousarray(inp["w_ffn_in"][0]),
        "w_fo": np.ascontiguousarray(inp["w_ffn_out"][0]),
        "ropeK": tab,
        "ropeQ": np.ascontiguousarray(np.concatenate([tq, tq], axis=0)),
    }


NTP = NT + 8
NEG = -30000.0


def _pc(t):
    return t + 2 if t < CTX else t + 6


def build_l1(stop_after=None, dbg=False, nlat=SEQ):
    nc = bass.Bass("TRN2", target_bir_lowering=False)
    with ExitStack() as st:
        fw = FW(nc, st)
        emit_l1(fw, stop_after=stop_after, dbg=dbg, nlat=nlat)
        fw.finish()
    return nc


def emit_l1(fw, pre="", hsrc=None, ydst=None, stop_after=None, dbg=False, nlat=SEQ, pdst=None, ext=None):
    NT = CTX + nlat
    NTP = NT + 8
    if True:
        fw = _Pre(fw, pre)
        EI = "ExternalInput"
        SK = "ExternalOutput" if dbg else "Internal"
        if hsrc is None:
            hT = fw.dram("hT", [D, NT], F32, EI)
            xv_ = hT.re("(k p) t -> p k t", p=128)
            hsrc = lambda t0, n: xv_[:, :, t0:t0 + n]
        cc = fw.dram("cc", [128, 8, 2], F32, EI)
        wmod = fw.dram("wmod", [D, 6 * D], F32, EI)
        bmodT = fw.dram("bmodT", [128, 48], F32, EI)
        w_g = fw.dram("w_g", [D, 1552], F32, EI) if ext is None else ext["w_g"]
        convT = fw.dram("convT", [128, 8, 5], F32, EI)
        abT = fw.dram("abT", [16, 2], F32, EI)
        normT = fw.dram("normT", [128, 1], F32, EI)
        cst = fw.dram("cst", [128, 384], F32, EI)
        if ydst is None and pdst is None:
            yT = fw.dram("yT", [512, NT], BF16, "ExternalOutput")
            ydst = lambda hv, t0, n: yT[hv * 128:(hv + 1) * 128, t0:t0 + n]
        if pdst is not None:
            w_op = fw.dram("w_op", [512, D], F32, EI) if ext is None else ext["w_op"]
            w_op_b = fw.dram("w_op_b", [128, 4, D], BF16) if ext is None else ext["w_op_b"]
        w_g_b = fw.dram("w_g_b", [128, 8, 1552], BF16) if ext is None else ext["w_g_b"]
        PR = fw.dram("PR", [8, 128, NTP], BF16)
        ZS = fw.dram("ZS", [4, 128, NT], BF16, SK)
        BG = fw.dram("BG", [16, 2, NT], F32, SK)
        QKV = fw.dram("QKV", [8, 128, NT], BF16, SK)
        OD = fw.dram("OD", [2, 4, 128, NT], F32, SK)

        modT = fw.sb("modT", [128, 48, 2], F32)
        sc0 = fw.sb("sc0", [128, 8, 2], F32)
        cs = fw.sb("cst", [128, 384], F32)
        fw.dma("sp", cs, cst)
        identf = cs[:, 0:128]
        fw.const(RMS_EPS)
        fw.const(1.0)
        identb = fw.sb("identb", [128, 128], BF16)
        fw.cp("dve", identb, identf)
        onesf = fw.sb("onesf", [128, 128], F32)
        fw.memset("dve", onesf, 1.0)

        mod_phase(fw, cc, wmod, bmodT, modT)
        for j in range(2):
            fw.ts("dve", sc0[:, :, j], modT[:, 8:16, j], 1.0, None, ALU.add)
        if ext is None:
            conv_weight(fw, w_g, w_g_b, D, 1552)
            if pdst is not None:
                conv_weight(fw, w_op, w_op_b, 512, D)

        chunks = [(0, 256, 1)] + [(256 + 512 * i, 512, 0) for i in range(nlat // 512)]
        with fw.phase():
            wg = fw.sb("wg", [128, 8, 1552], BF16)
            fw.dma("sp", wg, w_g_b)
            ab = fw.sb("ab", [16, 2], F32)
            fw.dma("sp", ab, abT)
            negA = fw.sb("negA", [16, 1], F32)
            fw.act(negA, ab[:, 1:2], AF.Exp)
            fw.ts("dve", negA, negA, -1.0, None, ALU.mult)
            zt = fw.sb("zt", [128, 4], BF16)
            fw.memset("dve", zt, 0.0)
            for ch in range(8):
                fw.dma("pool", PR[ch][:, 0:2], zt[:, 0:2])
                fw.dma("pool", PR[ch][:, 258:262], zt[:, 0:4])
                fw.dma("pool", PR[ch][:, NTP - 2:NTP], zt[:, 0:2])
            xr = fw.ring("xr", 2, [128, 8, 512], F32)
            ur = fw.ring("ur", 2, [128, 8, 512], BF16)
            sr = fw.ring("st", 3, [128, 512], BF16)
            zr = fw.ring("zs", 3, [128, 512], BF16)
            bgs = fw.ring("bgs", 2, [16, 2, 512], F32)
            et = fw.sb("et", [16, 512], F32)
            pp = fw.ring("pp", 6, [128, 512], F32, ps=True)
            pba = fw.ps("pba", [16, 512], F32)
            ei = 0
            for (t0, n, j) in chunks:
                xs = xr.next()
                src_ = hsrc(t0, n)
                if isinstance(src_, list):
                    for k in range(8):
                        fw.dma("sp", xs[:, k, :n], src_[k])
                else:
                    fw.dma("sp", xs[:, :, :n], src_)
                us = ur.next()
                for k in range(8):
                    fw.ts("dve" if k % 2 == 0 else "pool", us[:, k, :n], xs[:, k, :n], sc0[:, k, j:j + 1],
                          modT[:, k, j:j + 1], ALU.mult, ALU.add)
                for mt in range(12):
                    p = pp.next()
                    for k in range(8):
                        fw.mm(p[:, :n], wg[:, k, mt * 128:(mt + 1) * 128], us[:, k, :n], k == 0, k == 7)
                    if mt < 8:
                        s = sr.next()
                        fw.cp("act" if ei % 2 == 0 else "dve", s[:, :n], p[:, :n])
                        ei += 1
                        fw.dma("pool", PR[mt][:, _pc(t0):_pc(t0) + n], s[:, :n])
                    else:
                        z = zr.next()
                        fw.act(z[:, :n], p[:, :n], AF.Silu)
                        fw.dma("pool", ZS[mt - 8][:, t0:t0 + n], z[:, :n])
                for k in range(8):
                    fw.mm(pba[:, :n], wg[:, k, 1536:1552], us[:, k, :n], k == 0, k == 7)
                bg = bgs.next()
                fw.act(bg[:, 0, :n], pba[:, :n], AF.Sigmoid)
                fw.act(et[:, :n], pba[:, :n], AF.Exp, bias=ab[:, 0:1])
                fw.act(et[:, :n], et[:, :n], AF.Ln, bias=fw.const(1.0)[0:16, :])
                fw.ts("dve", bg[:, 1, :n], et[:, :n], negA, None, ALU.mult)
                fw.dma("pool", BG[:, :, t0:t0 + n], bg[:, :, :n])
        with fw.phase():
            cw = fw.sb("cw", [128, 8, 5], F32)
            fw.dma("sp", cw, convT)
            dg = fw.sb("dg", [128, 40, 128], BF16)
            for ch in range(8):
                for jj in range(5):
                    fw.ts("dve" if (ch + jj) % 2 == 0 else "pool", dg[:, ch * 5 + jj, :], identb, cw[:, ch, jj:jj + 1], None, ALU.mult)
            pr = fw.ring("prr", 4, [128, 516], BF16)
            a1 = fw.ring("a1", 3, [128, 512], F32)
            sq = fw.sb("sq", [128, 512], F32)
            rs = fw.sb("rs", [128, 512], F32)
            ob = fw.ring("ob", 3, [128, 512], BF16)
            pss = fw.ring("pss", 2, [128, 512], F32, ps=True)
            pcv = fw.ring("pcv", 3, [128, 512], F32, ps=True)
            for (t0, n, j) in chunks:
                for ch in range(8):
                    x = pr.next()
                    fw.dma("sp", x[:, :n + 4], PR[ch][:, _pc(t0) - 2:_pc(t0) + n + 2])
                    pc_ = pcv.next()
                    for jj in range(5):
                        fw.mm(pc_[:, :n], dg[:, ch * 5 + jj, :], x[:, jj:jj + n], jj == 0, jj == 4)
                    s = a1.next()
                    fw.act(s[:, :n], pc_[:, :n], AF.Silu)
                    o = ob.next()
                    if ch < 4:
                        fw.tt("pool", sq[:, :n], s[:, :n], s[:, :n], ALU.mult)
                        ps_ = pss.next()
                        fw.mm(ps_[:, :n], onesf, sq[:, :n], True, True)
                        fw.act(rs[:, :n], ps_[:, :n], AF.Ln, bias=fw.const(RMS_EPS))
                        fw.act(rs[:, :n], rs[:, :n], AF.Exp, scale=-0.5)
                        if ch < 2:
                            fw.stt("dve", o[:, :n], s[:, :n], 128 ** -0.5, rs[:, :n], ALU.mult, ALU.mult)
                        else:
                            fw.tt("dve", o[:, :n], s[:, :n], rs[:, :n], ALU.mult)
                    else:
                        fw.cp("pool", o[:, :n], s[:, :n])
                    fw.dma("pool", QKV[ch][:, t0:t0 + n], o[:, :n])
        if stop_after == "G1":
            return

        with fw.phase():
            tri = [cs[0:64, 128:192], cs[0:64, 192:256]]
            negS = [cs[0:64, 256:320], cs[0:64, 320:384]]
            id64 = cs[0:64, 0:64]
            GM = 8
            def mk_shared(i):
                return dict(
                    fm=[fw.sb("fm%d" % i, [128, GM * 64], BF16) for _ in range(8)],
                    kt=[fw.sb("kt%d" % i, [64, GM, 128], BF16) for _ in range(2)],
                    vt=[fw.sb("vt%d" % i, [64, GM, 128], BF16) for _ in range(4)],
                    kk=[fw.sb("kk%d" % i, [64, GM, 64], F32) for _ in range(2)],
                    at=[fw.sb("at%d" % i, [64, GM, 64], F32) for _ in range(2)],
                    bgf=fw.sb("bgf%d" % i, [16, 2, GM * 64], F32),
                    bT=fw.sb("bT%d" % i, [64, GM, 16], F32),
                    gT=fw.sb("gT%d" % i, [64, GM, 16], F32),
                    gc=[fw.sb("gc%d_%d" % (i, d), [64, GM, 16], F32) for d in range(2)],
                )
            shared = [mk_shared(0), mk_shared(1)]
            prob = {}
            for hv in range(4):
                for d in range(2):
                    prob[(hv, d)] = dict(
                        U=fw.sb("U", [64, GM, 128], BF16), Kd=fw.sb("Kd", [64, GM, 128], BF16),
                        WT=fw.sb("WT", [128, GM, 64], BF16), QdT=fw.sb("QdT", [128, GM, 64], BF16),
                        AT=fw.sb("ATb", [64, GM, 64], BF16), gl=fw.sb("gl", [128, GM], F32),
                        S=fw.sb("S", [128, 128], F32), Sb=fw.sb("Sb", [128, 128], BF16),
                        oo=fw.sb("oo", [128, GM, 64], F32))
                    fw.memset("dve", prob[(hv, d)]["S"], 0.0)
                    fw.memset("pool", prob[(hv, d)]["Sb"], 0.0)
            T = {nm: fw.sb(nm, [64, GM, 64], F32) for nm in ("E1", "E2", "diff", "m1", "DmS", "DmST")}
            NSET = 3
            TS = [{nm: fw.sb(nm, [64, GM, 64], F32) for nm in ("Tt", "X", "Y", "X2", "Y2")} for _ in range(NSET)]
            TtB = fw.sb("TtB", [64, GM, 64], BF16)
            Rv = fw.sb("Rv", [64, GM, 128], BF16)
            Rk = fw.sb("Rk", [64, GM, 128], BF16)
            eg = fw.sb("eg", [128, GM, 64], F32)
            c64 = {nm: fw.sb(nm, [64, GM], F32) for nm in ("egc", "kco", "kdc")}
            vnr = fw.ring("vn", 4, [64, 128], BF16)
            pf = fw.ring("pf", 6, [128, 512], F32, ps=True)
            pb = fw.ring("pb", 2, [128, 1024], BF16, ps=True)

            def b3(x, G, n):
                return B(x.ap.unsqueeze(2).to_broadcast([x.ap.shape[0], G, n]), x.tok)

            def m3(x, G):
                return B(x.ap.unsqueeze(1).to_broadcast([x.ap.shape[0], G, x.ap.shape[1]]), x.tok)

            def prep_group(sh, c0, G):
                t0, n = c0 * 64, G * 64
                for ch in range(8):
                    fw.dma("sp", sh["fm"][ch][:, :n], QKV[ch][:, t0:t0 + n])
                fw.dma("sp", sh["bgf"][:, :, :n], BG[:, :, t0:t0 + n])
                for idx, (src, dst) in enumerate([(2, sh["kt"][0]), (3, sh["kt"][1])] +
                                                 [(4 + v, sh["vt"][v]) for v in range(4)]):
                    p = pb.next()
                    pv = p[0:64, 0:G * 128].re("p (g d) -> p g d", d=128)
                    for g in range(G):
                        fw.op("pe", (lambda o, i: (lambda e: e.transpose(o, i, identb.ap)))(pv[:, g, :].ap, sh["fm"][src][:, g * 64:(g + 1) * 64].ap),
                              reads=[sh["fm"][src], identb], writes=[p])
                    fw.cp("act" if idx % 2 == 0 else "dve", dst[:, :G, :], pv)
                for pl_, dst in ((0, sh["bT"]), (1, sh["gT"])):
                    p = pf.next()
                    pv = p[0:64, 0:G * 16].re("p (g r) -> p g r", r=16)
                    for g in range(G):
                        fw.op("pe", (lambda o, i: (lambda e: e.transpose(o, i, identf[0:16, 0:16].ap)))(pv[:, g, :].ap, sh["bgf"][:, pl_, g * 64:(g + 1) * 64].ap),
                              reads=[sh["bgf"], cs], writes=[p])
                    fw.cp("dve", dst[:, :G, :], pv)
                for d in range(2):
                    p = pf.next()
                    fw.mm(p[0:64, 0:G * 16], tri[d], sh["gT"][:, :G, :].re("p g r -> p (g r)"), True, True)
                    fw.cp("act", sh["gc"][d][:, :G, :], p[0:64, 0:G * 16].re("p (g r) -> p g r", r=16))
                for kh in range(2):
                    KT, QT = sh["fm"][2 + kh], sh["fm"][kh]
                    p = pf.next()
                    for g in range(G):
                        fw.mm(p[0:64, g * 64:(g + 1) * 64], KT[:, g * 64:(g + 1) * 64], KT[:, g * 64:(g + 1) * 64], True, True)
                    fw.cp("act", sh["kk"][kh][:, :G, :], p[0:64, 0:n].re("p (g j) -> p g j", j=64))
                    p = pf.next()
                    for g in range(G):
                        fw.mm(p[0:64, g * 64:(g + 1) * 64], KT[:, g * 64:(g + 1) * 64], QT[:, g * 64:(g + 1) * 64], True, True)
                    fw.cp("dve", sh["at"][kh][:, :G, :], p[0:64, 0:n].re("p (g j) -> p g j", j=64))

            def prep_problem(sh, G, hv, d, ts):
                P = prob[(hv, d)]
                kh = hv // 2
                n = G * 64
                beta = sh["bT"][:, :G, d * 8 + hv]
                gc = sh["gc"][d][:, :G, d * 8 + 4 + hv]
                t = {k: v[:, :G, :] for k, v in T.items()}
                t.update({k: v[:, :G, :] for k, v in TS[ts].items()})
                idG = m3(id64, G)
                fw.tt("dve", t["E1"], idG, b3(gc, G, 64), ALU.mult)
                fw.tt("pool", t["E2"], idG, b3(beta, G, 64), ALU.mult)
                pg = pf.next()
                fw.mm(pg[:, :n], onesf[0:64, :], t["E1"].re("p g j -> p (g j)"), True, True)
                pbt = pf.next()
                fw.mm(pbt[0:64, :n], onesf[0:64, 0:64], t["E2"].re("p g j -> p (g j)"), True, True)
                gcrow = pg[0:64, :n].re("p (g j) -> p g j", j=64)
                brow = pbt[0:64, :n].re("p (g j) -> p g j", j=64)
                last = 63 if d == 0 else 0
                fw.stt("dve", t["diff"], gcrow, -1.0, b3(gc, G, 64), ALU.mult, ALU.add)
                fw.stt("dve", t["m1"], t["diff"], 0.0, m3(negS[d], G), ALU.min, ALU.add)
                fw.act(t["DmS"], t["m1"], AF.Exp)
                fw.ts("pool", t["m1"], t["diff"], -1.0, 0.0, ALU.mult, ALU.min)
                fw.tt("pool", t["m1"], t["m1"], m3(negS[1 - d], G), ALU.add)
                fw.act(t["DmST"], t["m1"], AF.Exp)
                fw.stt("dve", t["X"], sh["kk"][kh][:, :G, :], -1.0, t["DmS"], ALU.mult, ALU.mult)
                fw.tt("dve", t["X"], t["X"], b3(beta, G, 64), ALU.mult)
                fw.tt("pool", t["Y"], sh["kk"][kh][:, :G, :], t["DmST"], ALU.mult)
                fw.stt("dve", t["Y"], t["Y"], -1.0, brow, ALU.mult, ALU.mult)
                fw.tt("pool", t["m1"], t["DmST"], idG, ALU.add)
                fw.tt("pool", P["AT"][:, :G, :], sh["at"][kh][:, :G, :], t["m1"], ALU.mult)
                fw.tt("pool", t["Tt"], t["Y"], idG, ALU.add)
                fw.act(eg[:, :G, :], pg[:, :n].re("p (g j) -> p g j", j=64), AF.Exp)
                fw.cp("dve", P["gl"][:, :G], eg[:, :G, last])
                fw.tt("dve", c64["kdc"][:, :G], gcrow[:, :, last], gc, ALU.subtract)
                fw.act(c64["kdc"][:, :G], c64["kdc"][:, :G], AF.Exp)
                fw.tt("pool", P["QdT"][:, :G, :], sh["fm"][kh][:, :n].re("p (g j) -> p g j", j=64), eg[:, :G, :], ALU.mult)
                fw.tt("pool", P["Kd"][:, :G, :], sh["kt"][kh][:, :G, :], b3(c64["kdc"][:, :G], G, 128), ALU.mult)
                X, Y, X2, Y2 = t["X"], t["Y"], t["X2"], t["Y2"]
                for lv in range(5):
                    px = pf.next()
                    for g in range(G):
                        fw.mm(px[0:64, g * 64:(g + 1) * 64], Y[:, g, :], X[:, g, :], True, True)
                    fw.cp("act", X2, px[0:64, :n].re("p (g j) -> p g j", j=64))
                    if lv < 4:
                        py_ = pf.next()
                        for g in range(G):
                            fw.mm(py_[0:64, g * 64:(g + 1) * 64], X[:, g, :], Y[:, g, :], True, True)
                        fw.cp("dve", Y2, py_[0:64, :n].re("p (g j) -> p g j", j=64))
                    ptt = pf.next()
                    for g in range(G):
                        fw.mm(ptt[0:64, g * 64:(g + 1) * 64], X2[:, g, :], t["Tt"][:, g, :], True, True)
                    fw.tt("dve", t["Tt"], t["Tt"], ptt[0:64, :n].re("p (g j) -> p g j", j=64), ALU.add)
                    X, X2 = X2, X
                    Y, Y2 = Y2, Y
                    yield
                fw.act(c64["egc"][:, :G], gc, AF.Exp)
                fw.tt("dve", c64["kco"][:, :G], c64["egc"][:, :G], beta, ALU.mult)
                fw.tt("dve", Rv[:, :G, :], sh["vt"][hv][:, :G, :], b3(beta, G, 128), ALU.mult)
                fw.tt("dve", Rk[:, :G, :], sh["kt"][kh][:, :G, :], b3(c64["kco"][:, :G], G, 128), ALU.mult)
                fw.cp("pool", TtB[:, :G, :], t["Tt"])
                for hf in range(0, G, 4):
                    g1 = min(G, hf + 4)
                    pu = pf.next()
                    for g in range(hf, g1):
                        fw.mm(pu[0:64, (g - hf) * 128:(g - hf + 1) * 128], TtB[:, g, :], Rv[:, g, :], True, True)
                    fw.cp("act", P["U"][:, hf:g1, :], pu[0:64, 0:(g1 - hf) * 128].re("p (g j) -> p g j", j=128))
                pw = pf.next()
                for g in range(G):
                    fw.mm(pw[:, g * 64:(g + 1) * 64], Rk[:, g, :], TtB[:, g, :], True, True)
                fw.cp("dve", P["WT"][:, :G, :], pw[:, :n].re("p (g j) -> p g j", j=64))

            def scan_step(hv, d, g):
                P = prob[(hv, d)]
                p1 = pf.next()
                fw.mm(p1[0:64, 0:128], P["WT"][:, g, :], P["Sb"], True, True)
                vn = vnr.next()
                fw.tt("dve", vn, P["U"][:, g, :], p1[0:64, 0:128], ALU.subtract)
                po = pf.next()
                fw.mm(po[:, 0:64], P["Sb"], P["QdT"][:, g, :], True, False)
                fw.mm(po[:, 0:64], vn, P["AT"][:, g, :], False, True)
                fw.cp("act", P["oo"][:, g, :], po[:, 0:64])
                ps_ = pf.next()
                fw.mm(ps_[:, 0:128], P["Kd"][:, g, :], vn, True, True)
                fw.stt("dve", P["S"], P["S"], P["gl"][:, g:g + 1], ps_[:, 0:128], ALU.mult, ALU.add)
                fw.cp("pool", P["Sb"], P["S"])

            groups = [(0, 4)] + [(4 + 8 * k, 8) for k in range(nlat // 512)]
            NG = len(groups)
            for it in range(NG):
                gf = groups[it]
                gb = groups[0] if it == 0 else groups[NG - it]
                todo = [(gf, (0,)), (gb, (1,))] if gf != gb else [(gf, (0, 1))]
                plist = []
                for si, ((c0, G), dirs) in enumerate(todo):
                    sh = shared[si]
                    prep_group(sh, c0, G)
                    for hv in range(4):
                        for d in dirs:
                            plist.append((sh, G, hv, d))
                for i0_ in range(0, len(plist), NSET):
                    gens = [prep_problem(*a, ts) for ts, a in enumerate(plist[i0_:i0_ + NSET])]
                    while gens:
                        for g_ in list(gens):
                            try:
                                next(g_)
                            except StopIteration:
                                gens.remove(g_)
                for s in range(8):
                    for hv in range(4):
                        for d in range(2):
                            c0, G = gf if d == 0 else gb
                            if s >= G:
                                continue
                            g = s if d == 0 else G - 1 - s
                            scan_step(hv, d, g)
                for d, (c0, G) in ((0, gf), (1, gb)):
                    for hv in range(4):
                        fw.dma("pool", OD[d][hv][:, c0 * 64:(c0 + G) * 64],
                               prob[(hv, d)]["oo"][:, :G, :].re("p g j -> p (g j)"))
        if stop_after == "G2":
            return

        with fw.phase():
            nw = fw.sb("nw", [128, 1], F32)
            fw.dma("sp", nw, normT)
            ofr = fw.ring("of", 2, [128, 512], F32)
            obr = fw.ring("obb", 2, [128, 512], F32)
            zr = fw.ring("zz", 2, [128, 512], BF16)
            sq = fw.sb("sq", [128, 512], F32)
            rs = fw.sb("rs", [128, 512], F32)
            yr = fw.ring("yy", 8, [128, 512], BF16)
            pss = fw.ring("pss", 2, [128, 512], F32, ps=True)
            if pdst is not None:
                wop = fw.sb("wop", [128, 4, D], BF16)
                fw.dma("sp", wop, w_op_b)
                ppr = fw.ring("ppr", 3, [128, 512], F32, ps=True)
                pst = fw.ring("pst", 3, [128, 512], F32)
            for (t0, n, j) in chunks:
                ys = []
                for hv in range(4):
                    a, b_, z = ofr.next(), obr.next(), zr.next()
                    fw.dma("sp", a[:, :n], OD[0][hv][:, t0:t0 + n])
                    fw.dma("sp", b_[:, :n], OD[1][hv][:, t0:t0 + n])
                    fw.dma("sp", z[:, :n], ZS[hv][:, t0:t0 + n])
                    fw.tt("dve", a[:, :n], a[:, :n], b_[:, :n], ALU.add)
                    fw.tt("pool", sq[:, :n], a[:, :n], a[:, :n], ALU.mult)
                    p = pss.next()
                    fw.mm(p[:, :n], onesf, sq[:, :n], True, True)
                    fw.act(rs[:, :n], p[:, :n], AF.Ln, bias=fw.const(RMS_EPS), scale=1.0 / 128)
                    fw.act(rs[:, :n], rs[:, :n], AF.Exp, scale=-0.5)
                    fw.stt("dve", a[:, :n], a[:, :n], nw[:, 0:1], rs[:, :n], ALU.mult, ALU.mult)
                    y = yr.next()
                    fw.tt("pool", y[:, :n], a[:, :n], z[:, :n], ALU.mult)
                    ys.append(y)
                    if ydst is not None:
                        fw.dma("pool", ydst(hv, t0, n), y[:, :n])
                if pdst is not None and t0 >= CTX:
                    for oc in range(8):
                        pp_ = ppr.next()
                        for hv in range(4):
                            fw.mm(pp_[:, :n], wop[:, hv, oc * 128:(oc + 1) * 128], ys[hv][:, :n], hv == 0, hv == 3)
                        sg_ = pst.next()
                        fw.cp("act" if oc % 2 == 0 else "dve", sg_[:, :n], pp_[:, :n])
                        fw.dma("pool", pdst(oc, t0, n), sg_[:, :n])
        return modT


def build_l2():
    nc = bass.Bass("TRN2", target_bir_lowering=False)
    with ExitStack() as st:
        fw = FW(nc, st)
        emit_l2(fw)
        fw.finish()
    return nc


def emit_l2(fw, pre="", mixT=None, resT=None, modT=None, mixload=None, ydirect=None, ext=None):
    if True:
        fw = _Pre(fw, pre)
        EI = "ExternalInput"
        if mixT is None and ydirect is None:
            mixT = fw.dram("mixT", [2048, NQ], BF16, EI)
            resT = fw.dram("resT", [D, NQ], F32, EI)
        if modT is None:
            cc = fw.dram("cc", [128, 8, 2], F32, EI)
            wmod = fw.dram("wmod", [D, 6 * D], F32, EI)
            bmodT = fw.dram("bmodT", [128, 48], F32, EI)
        lnTd = fw.dram("lnT", [128, 4, 8], F32, EI)
        if ydirect is None:
            w_o = fw.dram("w_o", [2048, D], F32, EI)
        if ext is None:
            w_fi = fw.dram("w_fi", [D, 2 * DFF], F32, EI)
            w_fo = fw.dram("w_fo", [DFF, D], F32, EI)
        outT = fw.dram("outT", [D, NQ], F32, "ExternalOutput")
        w_o_b = fw.dram("w_o_b", [128, 16, D], BF16) if ydirect is None else None
        w_fi_b = fw.dram("w_fi_b", [128, 8, 2 * DFF], BF16) if ext is None else ext["w_fi_b"]
        w_fo_b = fw.dram("w_fo_b", [128, 22, D], BF16) if ext is None else ext["w_fo_b"]
        have_mod = modT is not None
        if not have_mod:
            modT = fw.sb("modT", [128, 48, 2], F32)
        lnT = fw.sb("lnTs", [128, 4, 8], F32)
        fw.dma("sp", lnT, lnTd)
        fw.const(LN_EPS)
        if not have_mod:
            mod_phase(fw, cc, wmod, bmodT, modT)
        if ydirect is None:
            conv_weight(fw, w_o, w_o_b, 2048, D)
        if ext is None:
            conv_weight(fw, w_fi, w_fi_b, D, 2 * DFF)
            conv_weight(fw, w_fo, w_fo_b, DFF, D)
        chunks = [(256 * i, 256, 0) for i in range(16)]
        post_phase(fw, mixT, 16, resT, w_o_b, w_fi_b, w_fo_b, modT, lnT, outT, chunks, mixload=mixload, ydirect=ydirect)


def _l1_cols(hg):
    q = np.arange(2 * hg * 128, (2 * hg + 2) * 128)
    k = 1024 + q
    v = 2048 + np.arange(4 * hg * 128, (4 * hg + 4) * 128)
    z = 2048 + v
    ba = np.array([6144 + d * 32 + ab * 16 + 4 * hg + h for d in range(2) for ab in range(2) for h in range(4)])
    return np.concatenate([q, k, v, z, ba])


def _cst():
    c = np.zeros((128, 384), np.float32)
    c[:, 0:128] = np.eye(128, dtype=np.float32)
    p = np.arange(64)[:, None]
    f = np.arange(64)[None, :]
    c[0:64, 128:192] = (p <= f)
    c[0:64, 192:256] = (p >= f)
    c[0:64, 256:320] = np.where(f < p, 0.0, NEG)
    c[0:64, 320:384] = np.where(f > p, 0.0, NEG)
    return c


def prep_l1(inp, hT_all, b, hg):
    cols = _l1_cols(hg)
    conv = inp["gdn_conv"][0][:, cols[:1024]]
    ab = np.zeros((16, 2), np.float32)
    for d in range(2):
        ab[d * 8 + 4:d * 8 + 8, 0] = inp["gdn_dt_bias"][0][d, 4 * hg:4 * hg + 4]
        ab[d * 8 + 4:d * 8 + 8, 1] = inp["gdn_a_log"][0][d, 4 * hg:4 * hg + 4]
    return {
        "hT": hT_all,
        "cc": np.ascontiguousarray(np.stack([_fm(inp["c"][b], 8), _fm(inp["c_ctx"], 8)], axis=-1)),
        "wmod": np.ascontiguousarray(inp["w_mod"][1]),
        "bmodT": _fm(inp["b_mod"][1], 48),
        "w_g": np.ascontiguousarray(inp["gdn_w_in"][0][:, cols]),
        "convT": np.ascontiguousarray(conv.T.reshape(8, 128, 5).transpose(1, 0, 2)),
        "abT": ab,
        "normT": np.ascontiguousarray(inp["gdn_norm"][0].reshape(128, 1).astype(np.float32)),
        "cst": _cst(),
    }


def prep_l2(inp, mixT, resT, b):
    return {
        "mixT": mixT, "resT": resT,
        "cc": np.ascontiguousarray(np.stack([_fm(inp["c"][b], 8), _fm(inp["c_ctx"], 8)], axis=-1)),
        "wmod": np.ascontiguousarray(inp["w_mod"][1]),
        "bmodT": _fm(inp["b_mod"][1], 48),
        "lnT": np.ascontiguousarray(np.stack([_fm(inp[k][1], 8) for k in ("ln1_g", "ln1_b", "ln2_g", "ln2_b")], axis=1)),
        "w_o": np.ascontiguousarray(inp["gdn_w_out"][0]),
        "w_fi": np.ascontiguousarray(inp["w_ffn_in"][1]),
        "w_fo": np.ascontiguousarray(inp["w_ffn_out"][1]),
    }


def _bf(a):
    return a.view(ml_dtypes.bfloat16) if a.dtype.kind == "V" else a


def kernel_unfused(**inp):
    inp = {k: np.asarray(v) for k, v in inp.items()}
    cores = list(range(8))
    tab = _rope_tables()
    r0 = run_bass_kernel_spmd(build_l0(), [prep_l0(inp, c // 4, c % 4, tab) for c in cores], core_ids=cores).results
    hT = []
    for b in range(2):
        parts = [r0[4 * b]["outT"][:, :CTX]] + [r0[4 * b + q]["outT"][:, CTX:] for q in range(4)]
        hT.append(np.ascontiguousarray(np.concatenate(parts, axis=1)))
    r1 = run_bass_kernel_spmd(build_l1(), [prep_l1(inp, hT[c // 4], c // 4, c % 4) for c in cores], core_ids=cores).results
    maps = []
    for c in cores:
        b, q = c // 4, c % 4
        sl = slice(CTX + q * NQ, CTX + (q + 1) * NQ)
        mix = np.concatenate([_bf(r1[4 * b + hg]["yT"])[:, sl] for hg in range(4)], axis=0)
        maps.append(prep_l2(inp, np.ascontiguousarray(mix), np.ascontiguousarray(hT[b][:, sl]), b))
    r2 = run_bass_kernel_spmd(build_l2(), maps, core_ids=cores).results
    out = np.empty((2, SEQ, D), np.float32)
    for c in cores:
        b, q = c // 4, c % 4
        out[b, q * NQ:(q + 1) * NQ, :] = r2[c]["outT"].T
    return out


GROUPS = [[0, 1, 2, 3], [4, 5, 6, 7]]


def collective(fw, kind, src, dst, op=None):
    tok = Tok()
    tok.sem = fw.getsem()
    key = tok.sem
    waits = fw._waits("pool", [src.tok], [dst.tok])
    fw.cnt[key] += 1
    v = fw.cnt[key]
    sa, da = src.ap, dst.ap
    aop = ALU.bypass if op is None else op
    fw.lists["pool"].append((waits, lambda e: e.collective_compute(kind, aop, replica_groups=GROUPS,
                                                                 ins=[sa.opt()], outs=[da.opt()]), key, 1))
    dst.tok.w = {key: v}
    dst.tok.r = {}
    src.tok.r[key] = v


def build_fused():
    nc = bass.Bass("TRN2", target_bir_lowering=False)
    with ExitStack() as st:
        fw = FW(nc, st)
        HC = 2048
        h2own = fw.dram("h2own", [D, NO], F32)
        hsend = [[fw.dram("hsend_%d_%d" % (k, c), [128, HC], F32) for c in range(2)] for k in range(8)]
        hgat = [[fw.dram("hgat_%d_%d" % (k, c), [4 * 128, HC], F32) for c in range(2)] for k in range(8)]
        ppart = [fw.dram("ppart_%d" % i, [4 * 256, NQ], F32) for i in range(4)]
        ysum = [fw.dram("ysum_%d" % i, [256, NQ], F32) for i in range(4)]
        EI = "ExternalInput"
        e1 = {"w_g": fw.dram("b_w_g", [D, 1552], F32, EI), "w_g_b": fw.dram("b_w_g_b", [128, 8, 1552], BF16),
              "w_op": fw.dram("b_w_op", [512, D], F32, EI), "w_op_b": fw.dram("b_w_op_b", [128, 4, D], BF16)}
        e2 = {"w_fi": fw.dram("c_w_fi", [D, 2 * DFF], F32, EI), "w_fi_b": fw.dram("c_w_fi_b", [128, 8, 2 * DFF], BF16),
              "w_fo": fw.dram("c_w_fo", [DFF, D], F32, EI), "w_fo_b": fw.dram("c_w_fo_b", [128, 22, D], BF16)}
        xjobs = [(e1["w_g"], e1["w_g_b"], D, 1552), (e1["w_op"], e1["w_op_b"], 512, D),
                 (e2["w_fi"], e2["w_fi_b"], D, 2 * DFF), (e2["w_fo"], e2["w_fo_b"], DFF, D)]
        emit_l0(fw, pre="a_", outT=h2own, extra_jobs=xjobs)
        with fw.phase():
            bounce = fw.ring("bnc", 2, [128, 8, 512], F32)
            sv = h2own.re("(k p) t -> p k t", p=128)
            for i in range(NQ // 512):
                b_ = bounce.next()
                fw.dma("sp", b_, sv[:, :, CTX + i * 512:CTX + (i + 1) * 512])
                c, o2 = (i * 512) // HC, (i * 512) % HC
                for k in range(8):
                    fw.dma("pool", hsend[k][c][:, o2:o2 + 512], b_[:, k, :])
        for k in range(8):
            for c in range(2):
                collective(fw, "AllGather", hsend[k][c], hgat[k][c])
        ctxv = h2own.re("(k p) t -> p k t", p=128)

        def hsrc(t0, n):
            if t0 < CTX:
                return ctxv[:, :, t0:t0 + n]
            q, off = (t0 - CTX) // NQ, (t0 - CTX) % NQ
            c, o2 = off // HC, off % HC
            return [hgat[k][c][q * 128:(q + 1) * 128, o2:o2 + n] for k in range(8)]

        def pdst(oc, t0, n):
            q, off = (t0 - CTX) // NQ, (t0 - CTX) % NQ
            r0 = q * 256 + (oc % 2) * 128
            return ppart[oc // 2][r0:r0 + 128, off:off + n]

        modT1 = emit_l1(fw, pre="b_", hsrc=hsrc, pdst=pdst, ext=e1)
        for i in range(4):
            collective(fw, "ReduceScatter", ppart[i], ysum[i], op=ALU.add)
        emit_l2(fw, pre="c_", resT=h2own[:, CTX:], modT=modT1, ydirect=ysum, ext=e2)
        fw.finish()
    return nc


def kernel(**inp):
    inp = {k: np.asarray(v) for k, v in inp.items()}
    cores = list(range(8))
    tab = _rope_tables()
    maps = []
    for c in cores:
        b, r = c // 4, c % 4
        m = {"a_" + k: v for k, v in prep_l0(inp, b, r, tab).items()}
        l1 = prep_l1(inp, None, b, r)
        l1.pop("hT")
        m.update({"b_" + k: v for k, v in l1.items()})
        l2 = prep_l2(inp, None, None, b)
        for k in ("mixT", "resT", "cc", "wmod", "bmodT", "w_o"):
            l2.pop(k)
        m.update({"c_" + k: v for k, v in l2.items()})
        m["b_w_op"] = np.ascontiguousarray(inp["gdn_w_out"][0][r * 512:(r + 1) * 512])
        maps.append(m)
    res = run_bass_kernel_spmd(build_fused(), maps, core_ids=cores).results
    out = np.empty((2, SEQ, D), np.float32)
    for c in cores:
        b, q = c // 4, c % 4
        out[b, q * NQ:(q + 1) * NQ, :] = res[c]["c_outT"].T
    return out
```

```python
import math
import numpy as np
import ml_dtypes
from contextlib import ExitStack
import concourse.bass as bass
import concourse.mybir as mybir
from concourse.bass_utils import run_bass_kernel_spmd

F32 = mybir.dt.float32
BF16 = mybir.dt.bfloat16
ALU = mybir.AluOpType
AF = mybir.ActivationFunctionType

D = 1024
SEQ = 16384
CTX = 256
NT = SEQ + CTX
NQ = SEQ // 4
NO = NQ + CTX
DFF = 2816
ALPHA = (2.0 * 2) ** 0.25
LN_EPS = 1e-5
RMS_EPS = 1e-6
MLA_SCALE = 192 ** -0.5


class Tok:
    __slots__ = ("w", "r", "sem")

    def __init__(self):
        self.w = {}
        self.r = {}
        self.sem = None


class B:
    def __init__(self, ap, tok=None, sb=True):
        self.ap = ap
        self.tok = tok if tok is not None else Tok()
        self.sb = sb

    def __getitem__(self, idx):
        return B(self.ap[idx], self.tok, self.sb)

    def re(self, pat, **kw):
        return B(self.ap.rearrange(pat, **kw), self.tok, self.sb)


class Ring:
    def __init__(self, bufs):
        self.bufs = bufs
        self.i = 0

    def next(self):
        b = self.bufs[self.i % len(self.bufs)]
        self.i += 1
        return b


class FW:
    ENG = ("pe", "act", "dve", "pool", "sp")

    def __init__(self, nc, stack):
        self.nc = nc
        self.outer = stack
        self.stack = stack
        self.lists = {e: [] for e in self.ENG}
        self.sems = {}
        self.cnt = {}
        self.seen = {e: {} for e in self.ENG}
        self.free_sems = []
        self.phase_sems = []
        self.nsem = 0
        self.uid = 0
        self.consts = {}
        self.ekey = {}
        for e in ("pe", "act", "dve", "pool"):
            self.ekey[e] = self._newsem("E_" + e)

    def _newsem(self, key):
        self.sems[key] = self.outer.enter_context(self.nc.semaphore(key))
        self.cnt[key] = 0
        self.nsem += 1
        return key

    def getsem(self):
        if self.free_sems:
            k = self.free_sems.pop()
        else:
            k = self._newsem("S%d" % self.nsem)
        self.phase_sems.append(k)
        return k

    def const(self, val):
        key = float(val)
        if key not in self.consts:
            assert self.stack is self.outer, "create consts before phases"
            saved = self.stack
            self.stack = self.outer
            c = self.sb("const", [128, 1], F32)
            self.stack = saved
            self.memset("dve", c, key)
            self.consts[key] = c
        return self.consts[key]

    def name(self, n):
        self.uid += 1
        return "%s_%d" % (n, self.uid)

    def sb(self, name, shape, dt):
        return B(self.stack.enter_context(self.nc.sbuf_tensor(self.name(name), list(shape), dt))[:])

    def ps(self, name, shape, dt=F32):
        return B(self.stack.enter_context(self.nc.psum_tensor(self.name(name), list(shape), dt))[:])

    def dram(self, name, shape, dt, kind="Internal"):
        return B(self.nc.dram_tensor(name, list(shape), dt, kind=kind).ap(), sb=False)

    def ring(self, name, n, shape, dt, ps=False):
        return Ring([(self.ps if ps else self.sb)(name, shape, dt) for _ in range(n)])

    def _waits(self, eng, reads, writes):
        deps = {}
        for t in reads:
            for k, v in t.w.items():
                deps[k] = max(deps.get(k, 0), v)
        for t in writes:
            for k, v in t.w.items():
                deps[k] = max(deps.get(k, 0), v)
            for k, v in t.r.items():
                deps[k] = max(deps.get(k, 0), v)
        out = []
        seen = self.seen[eng]
        for k, v in deps.items():
            if eng == "pe" and k.startswith("E_pe"):
                continue
            if seen.get(k, 0) >= v:
                continue
            seen[k] = v
            out.append((k, v))
        return out

    def op(self, eng, fn, reads=(), writes=()):
        reads = [b.tok for b in reads if isinstance(b, B)]
        writes = [b.tok for b in writes]
        waits = self._waits(eng, reads, writes)
        key = self.ekey[eng]
        if self.cnt[key] >= 6000:
            key = self.ekey[eng] = self._newsem("E_%s_%d" % (eng, self.nsem))
        self.cnt[key] += 1
        v = self.cnt[key]
        self.lists[eng].append((waits, fn, key, 1))
        for t in reads:
            t.r[key] = v
        for t in writes:
            t.w = {key: v}
            t.r = {}

    def dma(self, q, out, in_, **kw):
        sbside = out if out.sb else in_
        t = sbside.tok
        if t.sem is None:
            t.sem = self.getsem()
        key = t.sem
        reads = [in_.tok]
        writes = [out.tok]
        waits = self._waits(q, reads, writes)
        self.cnt[key] += 16
        v = self.cnt[key]
        oa, ia = out.ap, in_.ap
        self.lists[q].append((waits, lambda e: e.dma_start(out=oa, in_=ia, **kw), key, 16))
        for tk in reads:
            tk.r[key] = v
        for tk in writes:
            tk.w = dict(tk.w)
            tk.w[key] = v
            tk.r = {}

    def barrier(self):
        snap = {k: v for k, v in self.cnt.items() if v > 0}
        for e in self.ENG:
            seen = self.seen[e]
            waits = []
            for k, v in snap.items():
                if e == "pe" and k.startswith("E_pe"):
                    continue
                if seen.get(k, 0) >= v:
                    continue
                seen[k] = v
                waits.append((k, v))
            if waits:
                self.lists[e].append((waits, None, None, 0))

    class _Phase:
        def __init__(self, fw):
            self.fw = fw

        def __enter__(self):
            self.st = ExitStack()
            self.st.__enter__()
            self.prev = self.fw.stack
            self.fw.stack = self.st
            self.fw.phase_sems = []
            return self

        def __exit__(self, *a):
            self.fw.barrier()
            self.fw.free_sems.extend(self.fw.phase_sems)
            self.fw.phase_sems = []
            self.fw.stack = self.prev
            return self.st.__exit__(*a)

    def phase(self):
        return FW._Phase(self)

    def finish(self):
        self.barrier()
        sems = self.sems
        lists = self.lists
        with self.nc.Block() as block:
            def run(e, lst):
                for waits, fn, key, inc in lst:
                    for k, v in waits:
                        e.wait_ge(sems[k], v)
                    if fn is not None:
                        ins = fn(e)
                        if ins is not None and key is not None:
                            ins.then_inc(sems[key], inc)

            @block.sync
            def _(e):
                run(e, lists["sp"])

            @block.tensor
            def _(e):
                run(e, lists["pe"])

            @block.scalar
            def _(e):
                run(e, lists["act"])

            @block.vector
            def _(e):
                run(e, lists["dve"])

            @block.gpsimd
            def _(e):
                run(e, lists["pool"])

    def mm(self, out, lhsT, rhs, start=True, stop=True):
        o, l, r = out.ap, lhsT.ap, rhs.ap
        self.op("pe", lambda e: e.matmul(o, l, r, start=start, stop=stop), reads=[lhsT, rhs], writes=[out])

    def act(self, out, in_, func, bias=0.0, scale=1.0):
        o, i = out.ap, in_.ap
        b = bias.ap if isinstance(bias, B) else bias
        self.op("act", lambda e: e.activation(o, i, func, bias=b, scale=scale), reads=[in_, bias], writes=[out])

    def ts(self, eng, out, in0, s1, s2, op0, op1=None):
        o, i = out.ap, in0.ap
        a1 = s1.ap if isinstance(s1, B) else s1
        a2 = s2.ap if isinstance(s2, B) else s2
        if op1 is None:
            self.op(eng, lambda e: e.tensor_scalar(o, i, a1, None, op0), reads=[in0, s1], writes=[out])
        else:
            self.op(eng, lambda e: e.tensor_scalar(o, i, a1, a2, op0, op1), reads=[in0, s1, s2], writes=[out])

    def tt(self, eng, out, in0, in1, op):
        o, a, b = out.ap, in0.ap, in1.ap
        self.op(eng, lambda e: e.tensor_tensor(o, a, b, op), reads=[in0, in1], writes=[out])

    def stt(self, eng, out, in0, s, in1, op0, op1):
        o, a, b = out.ap, in0.ap, in1.ap
        sa = s.ap if isinstance(s, B) else s
        self.op(eng, lambda e: e.scalar_tensor_tensor(o, a, sa, b, op0, op1), reads=[in0, s, in1], writes=[out])

    def cp(self, eng, out, in_):
        o, i = out.ap, in_.ap
        if eng == "act":
            self.op(eng, lambda e: e.copy(o, i), reads=[in_], writes=[out])
        else:
            self.op(eng, lambda e: e.tensor_copy(o, i), reads=[in_], writes=[out])

    def memset(self, eng, out, val):
        o = out.ap
        self.op(eng, lambda e: e.memset(o, val), writes=[out])

    def recip(self, out, in_):
        o, i = out.ap, in_.ap
        self.op("dve", lambda e: e.reciprocal(o, i), reads=[in_], writes=[out])


class _Pre:
    def __init__(self, fw, pre):
        self._fw = fw
        self._pre = pre

    def __getattr__(self, k):
        return getattr(self._fw, k)

    def dram(self, name, shape, dt, kind="Internal"):
        return self._fw.dram(self._pre + name, shape, dt, kind)


def conv_weight(fw, src, dst, K, N, eng_cycle=("dve", "pool", "act")):
    kc = K // 128
    with fw.phase():
        CB = 2048
        fr = fw.ring("wcf", 2, [128, CB], F32)
        br = fw.ring("wcb", 2, [128, CB], BF16)
        i = 0
        for k in range(kc):
            for c0 in range(0, N, CB):
                n = min(CB, N - c0)
                f = fr.next()
                b = br.next()
                fw.dma("sp", f[:, :n], src[k * 128:(k + 1) * 128, c0:c0 + n])
                fw.cp(eng_cycle[i % len(eng_cycle)], b[:, :n], f[:, :n])
                fw.dma("pool", dst[:, k, c0:c0 + n], b[:, :n])
                i += 1


def mod_phase(fw, cc, wmod, bmodT, modT):
    with fw.phase():
        ccs = fw.sb("ccs", [128, 8, 2], F32)
        bm = fw.sb("bm", [128, 48], F32)
        fw.dma("sp", ccs, cc)
        fw.dma("sp", bm, bmodT)
        fw.act(ccs, ccs, AF.Silu)
        modp = fw.ps("modp", [128, 48, 2], F32)
        wr = fw.ring("wm", 2, [128, 8, 1024], F32)
        wv = wmod.re("(k p) o -> p k o", p=128)
        for blk in range(6):
            w = wr.next()
            fw.dma("sp", w, wv[:, :, blk * 1024:(blk + 1) * 1024])
            for o8 in range(8):
                oc = blk * 8 + o8
                for k in range(8):
                    fw.mm(modp[:, oc, :], w[:, k, o8 * 128:(o8 + 1) * 128], ccs[:, k, :], k == 0, k == 7)
        for j in range(2):
            fw.tt("dve", modT[:, :, j], modp[:, :, j], bm, ALU.add)


def layer_norm_fm(fw, r, n, onesm, pstat, tmp, outs):
    sq, mean, rstd, d = tmp
    fw.tt("pool", sq[:, :, :n], r[:, :, :n], r[:, :, :n], ALU.mult)
    pm = pstat[:, 0, :n]
    pq = pstat[:, 1, :n]
    for oc in range(8):
        fw.mm(pm, onesm, r[:, oc, :n], oc == 0, oc == 7)
    for oc in range(8):
        fw.mm(pq, onesm, sq[:, oc, :n], oc == 0, oc == 7)
    fw.cp("act", mean[:, :n], pm)
    fw.tt("dve", rstd[:, :n], mean[:, :n], mean[:, :n], ALU.mult)
    fw.tt("dve", rstd[:, :n], pq, rstd[:, :n], ALU.subtract)
    fw.act(rstd[:, :n], rstd[:, :n], AF.Ln, bias=fw.const(LN_EPS), scale=1.0)
    fw.act(rstd[:, :n], rstd[:, :n], AF.Exp, scale=-0.5)
    for oc in range(8):
        fw.tt("dve", d[:, oc, :n], r[:, oc, :n], mean[:, :n], ALU.subtract)
        fw.tt("pool", d[:, oc, :n], d[:, oc, :n], rstd[:, :n], ALU.mult)
        for (dst, gf, bf, eng) in outs:
            fw.ts(eng, dst[:, oc, :n], d[:, oc, :n], gf(oc), bf(oc), ALU.mult, ALU.add)


def post_phase(fw, mixT, KH, resT, wo_b, wfi_b, wfo_b, modT, lnT, outT, chunks, mixload=None, ydirect=None):
    with fw.phase():
        N = 256
        if ydirect is None:
            wo = fw.sb("wo", [128, KH, 1024], BF16)
        wfi = fw.sb("wfi", [128, 8, 2 * DFF], BF16)
        wfor = fw.ring("wfo", 3, [128, 1024], BF16)
        if ydirect is None:
            fw.dma("sp", wo, wo_b)
        for k in range(8):
            fw.dma("sp", wfi[:, k, :], wfi_b[:, k, :])
        onesm = fw.sb("onesm", [128, 128], F32)
        fw.memset("dve", onesm, 1.0 / 1024)
        G2 = fw.sb("G2", [128, 8, 2], F32)
        B2 = fw.sb("B2", [128, 8, 2], F32)
        for j in range(2):
            fw.ts("dve", G2[:, :, j], modT[:, 32:40, j], 1.0, None, ALU.add)
            fw.tt("dve", B2[:, :, j], G2[:, :, j], lnT[:, 1, :], ALU.mult)
            fw.tt("dve", B2[:, :, j], B2[:, :, j], modT[:, 24:32, j], ALU.add)
            fw.tt("dve", G2[:, :, j], G2[:, :, j], lnT[:, 0, :], ALU.mult)
        mixv = mixT.re("(h p) t -> p h t", p=128) if (mixload is None and ydirect is None) else None
        ydv = [y_.re("(k p) t -> p k t", p=128) for y_ in ydirect] if ydirect is not None else None
        resv = resT.re("(k p) t -> p k t", p=128)
        outv = outT.re("(k p) t -> p k t", p=128)
        mr = fw.ring("mx", 2, [128, KH, N], BF16) if ydirect is None else fw.ring("yd", 2, [128, 8, N], F32)
        xr = fw.ring("xs", 1, [128, 8, N], F32)
        rr = fw.ring("r", 1, [128, 8, N], F32)
        h1r = fw.ring("h1", 1, [128, 8, N], F32)
        u2r = fw.ring("u2", 1, [128, 8, N], BF16)
        dd = fw.sb("dd", [128, 8, N], F32)
        sq = dd
        mean = fw.sb("mean", [128, N], F32)
        rstd = fw.sb("rstd", [128, N], F32)
        sgr = fw.ring("sg", 3, [128, N], F32)
        ar = fw.ring("a", 3, [128, N], BF16)
        pfa = fw.ps("pfa", [128, 8, N], F32)
        pgu = fw.ring("pgu", 2, [128, 2, N], F32, ps=True)
        py = fw.ps("py", [128, 2, N], F32)
        pstat = fw.ps("pst", [128, 2, N], F32)
        tmp = (sq, mean, rstd, dd)
        for (t0, n, j) in chunks:
            ms = mr.next()
            xs = xr.next()
            if ydirect is not None:
                for i4 in range(4):
                    fw.dma("sp", ms[:, 2 * i4:2 * i4 + 2, :n], ydv[i4][:, :, t0:t0 + n])
            elif mixload is None:
                fw.dma("sp", ms[:, :, :n], mixv[:, :, t0:t0 + n])
            else:
                mixload(ms, t0, n)
            fw.dma("sp", xs[:, :, :n], resv[:, :, t0:t0 + n])
            fw.ts("pool", xs[:, :, :n], xs[:, :, :n], ALPHA, None, ALU.mult)
            r = rr.next()
            for oc in range(8):
                if ydirect is not None:
                    p = ms[:, oc, :n]
                else:
                    p = py[:, oc % 2, :n]
                    for h in range(KH):
                        fw.mm(p, wo[:, h, oc * 128:(oc + 1) * 128], ms[:, h, :n], h == 0, h == KH - 1)
                fw.stt("dve", r[:, oc, :n], p, modT[:, 16 + oc, j:j + 1], xs[:, oc, :n], ALU.mult, ALU.add)
            h1 = h1r.next()
            u2 = u2r.next()
            layer_norm_fm(fw, r, n, onesm, pstat, tmp, [
                (h1, lambda oc: lnT[:, 0, oc:oc + 1], lambda oc: lnT[:, 1, oc:oc + 1], "dve"),
                (u2, lambda oc: G2[:, oc, j:j + 1], lambda oc: B2[:, oc, j:j + 1], "pool"),
            ])
            def gateup(m):
                pp = pgu.next()
                for k in range(8):
                    fw.mm(pp[:, 0, :n], wfi[:, k, m * 128:(m + 1) * 128], u2[:, k, :n], k == 0, k == 7)
                for k in range(8):
                    fw.mm(pp[:, 1, :n], wfi[:, k, DFF + m * 128:DFF + (m + 1) * 128], u2[:, k, :n], k == 0, k == 7)
                sg = sgr.next()
                a = ar.next()
                fw.act(sg[:, :n], pp[:, 0, :n], AF.Silu)
                fw.tt("dve", a[:, :n], sg[:, :n], pp[:, 1, :n], ALU.mult)
                return a

            def down(m, a):
                wf = wfor.next()
                fw.dma("sp", wf, wfo_b[:, m, :])
                for oc in range(8):
                    fw.mm(pfa[:, oc, :n], wf[:, oc * 128:(oc + 1) * 128], a[:, :n], m == 0 and oc % 2 == 0, m == 21)

            prev = gateup(0)
            for m in range(1, 22):
                cur = gateup(m)
                down(m - 1, prev)
                prev = cur
            down(21, prev)
            fw.ts("pool", h1[:, :, :n], h1[:, :, :n], ALPHA, None, ALU.mult)
            r2 = rr.next()
            for oc in range(8):
                fw.stt("dve", r2[:, oc, :n], pfa[:, oc, :n], modT[:, 40 + oc, j:j + 1], h1[:, oc, :n], ALU.mult, ALU.add)
            h2 = xs
            layer_norm_fm(fw, r2, n, onesm, pstat, tmp, [
                (h2, lambda oc: lnT[:, 2, oc:oc + 1], lambda oc: lnT[:, 3, oc:oc + 1], "dve"),
            ])
            fw.dma("pool", outv[:, :, t0:t0 + n], h2[:, :, :n])


def build_l0(stop_after=None, dbg=False):
    nc = bass.Bass("TRN2", target_bir_lowering=False)
    with ExitStack() as st:
        fw = FW(nc, st)
        emit_l0(fw, stop_after=stop_after, dbg=dbg)
        fw.finish()
    return nc


def emit_l0(fw, pre="", outT=None, stop_after=None, dbg=False):
    if True:
        _d = fw.dram
        fw = _Pre(fw, pre)
        EI = "ExternalInput"
        SK = "ExternalOutput" if dbg else "Internal"
        xT = fw.dram("xT", [D, NT], F32, EI)
        xTo = fw.dram("xTo", [D, NO], F32, EI)
        cc = fw.dram("cc", [128, 8, 2], F32, EI)
        wmod = fw.dram("wmod", [D, 6 * D], F32, EI)
        bmodT = fw.dram("bmodT", [128, 48], F32, EI)
        lnTd = fw.dram("lnT", [128, 4, 8], F32, EI)
        w_in = fw.dram("w_in", [D, 896], F32, EI)
        qnT = fw.dram("qnT", [128, 4], F32, EI)
        kvnT = fw.dram("kvnT", [128, 2], F32, EI)
        w_q = fw.dram("w_q", [512, 3072], F32, EI)
        w_kv = fw.dram("w_kv", [256, 2048], F32, EI)
        w_o = fw.dram("w_o", [D, D], F32, EI)
        w_fi = fw.dram("w_fi", [D, 2 * DFF], F32, EI)
        w_fo = fw.dram("w_fo", [DFF, D], F32, EI)
        ropeK = fw.dram("ropeK", [64, 2, NT], F32, EI)
        ropeQ = fw.dram("ropeQ", [128, 2, NO], F32, EI)
        if outT is None:
            outT = fw.dram("outT", [D, NO], F32, "ExternalOutput")
        w_in_b = fw.dram("w_in_b", [128, 8, 896], BF16)
        w_q_b = fw.dram("w_q_b", [128, 4, 3072], BF16)
        w_kv_b = fw.dram("w_kv_b", [128, 2, 2048], BF16)
        w_o_b = fw.dram("w_o_b", [128, 8, D], BF16)
        w_fi_b = fw.dram("w_fi_b", [128, 8, 2 * DFF], BF16)
        w_fo_b = fw.dram("w_fo_b", [128, 22, D], BF16)
        KT = fw.dram("KT", [8, 128, NT], BF16, SK)
        VV = fw.dram("VV", [8, 128, NT // 128, 128], BF16, SK)
        KR = fw.dram("KR", [64, NT], BF16, SK)
        QN = fw.dram("QN", [8, 128, NO], BF16, SK)
        QR = fw.dram("QR", [8, 128, NO], BF16, SK)
        OT = fw.dram("OT", [D, NO], BF16, SK)

        modT = fw.sb("modT", [128, 48, 2], F32)
        lnT = fw.sb("lnTs", [128, 4, 8], F32)
        sc0 = fw.sb("sc0", [128, 8, 2], F32)
        fw.dma("sp", lnT, lnTd)
        fw.const(LN_EPS)
        fw.const(RMS_EPS)

        mod_phase(fw, cc, wmod, bmodT, modT)
        for j in range(2):
            fw.ts("dve", sc0[:, :, j], modT[:, 8:16, j], 1.0, None, ALU.add)
        conv_weight(fw, w_in, w_in_b, D, 896)
        conv_weight(fw, w_q, w_q_b, 512, 3072)
        conv_weight(fw, w_kv, w_kv_b, 256, 2048)
        conv_weight(fw, w_o, w_o_b, D, D)
        conv_weight(fw, w_fi, w_fi_b, D, 2 * DFF)
        conv_weight(fw, w_fo, w_fo_b, DFF, D)

        with fw.phase():
            win = fw.sb("win", [128, 8, 384], BF16)
            wkv = fw.sb("wkv", [128, 2, 2048], BF16)
            fw.dma("sp", win, w_in_b[:, :, 512:896])
            fw.dma("sp", wkv, w_kv_b)
            kvn = fw.sb("kvn", [128, 2], F32)
            fw.dma("sp", kvn, kvnT)
            ones = fw.sb("ones", [128, 128], BF16)
            fw.memset("dve", ones, 1.0)
            xr = fw.ring("xr", 2, [128, 8, 512], F32)
            ur = fw.ring("ur", 2, [128, 8, 512], BF16)
            cr = fw.ring("ckv", 2, [128, 2, 512], BF16)
            sqb = fw.sb("sqb", [128, 2, 512], BF16)
            rs = fw.sb("rs", [128, 512], F32)
            tr = fw.ring("tbl", 2, [64, 2, 512], F32)
            t1 = fw.sb("t1", [64, 512], F32)
            t2 = fw.sb("t2", [64, 512], F32)
            krr = fw.ring("krs", 2, [64, 512], BF16)
            ksr = fw.ring("kst", 3, [128, 512], BF16)
            vsr = fw.ring("vst", 3, [128, 512], BF16)
            pl = fw.ps("pl", [128, 2, 512], F32)
            pr = fw.ps("pr", [64, 2, 512], F32)
            pss = fw.ps("pss", [128, 512], F32)
            pkv = fw.ring("pkv", 3, [128, 512], F32, ps=True)
            xv = xT.re("(k p) t -> p k t", p=128)
            chunks = [(0, 256, 1)] + [(256 + 512 * i, 512, 0) for i in range(32)]
            ei = 0
            for (t0, n, j) in chunks:
                xs = xr.next()
                fw.dma("sp", xs[:, :, :n], xv[:, :, t0:t0 + n])
                tb = tr.next()
                fw.dma("sp", tb[:, :, :n], ropeK[:, :, t0:t0 + n])
                us = ur.next()
                for k in range(8):
                    fw.ts("dve" if k % 2 == 0 else "pool", us[:, k, :n], xs[:, k, :n], sc0[:, k, j:j + 1],
                          modT[:, k, j:j + 1], ALU.mult, ALU.add)
                for m in range(2):
                    for k in range(8):
                        fw.mm(pl[:, m, :n], win[:, k, m * 128:(m + 1) * 128], us[:, k, :n], k == 0, k == 7)
                for m in range(2):
                    for k in range(8):
                        fw.mm(pr[:, m, :n], win[:, k, 256 + m * 64:256 + (m + 1) * 64], us[:, k, :n], k == 0, k == 7)
                for m in range(2):
                    fw.act(sqb[:, m, :n], pl[:, m, :n], AF.Square)
                for m in range(2):
                    fw.mm(pss[:, :n], ones, sqb[:, m, :n], m == 0, m == 1)
                fw.act(rs[:, :n], pss[:, :n], AF.Ln, bias=fw.const(RMS_EPS), scale=1.0 / 256)
                fw.act(rs[:, :n], rs[:, :n], AF.Exp, scale=-0.5)
                cs = cr.next()
                for m in range(2):
                    fw.stt("dve", cs[:, m, :n], pl[:, m, :n], kvn[:, m:m + 1], rs[:, :n], ALU.mult, ALU.mult)
                fw.tt("dve", t1[:, :n], pr[:, 0, :n], tb[:, 0, :n], ALU.mult)
                fw.tt("dve", t2[:, :n], pr[:, 1, :n], tb[:, 1, :n], ALU.mult)
                krs = krr.next()
                fw.tt("pool", krs[:, :n], t1[:, :n], t2[:, :n], ALU.add)
                fw.dma("pool", KR[:, t0:t0 + n], krs[:, :n])
                for h in range(8):
                    pk = pkv.next()
                    for m in range(2):
                        fw.mm(pk[:, :n], wkv[:, m, h * 256:h * 256 + 128], cs[:, m, :n], m == 0, m == 1)
                    ks = ksr.next()
                    fw.cp("act" if ei % 2 == 0 else "dve", ks[:, :n], pk[:, :n])
                    ei += 1
                    fw.dma("pool", KT[h][:, t0:t0 + n], ks[:, :n])
                    pv = pkv.next()
                    for tq in range(n // 128):
                        for m in range(2):
                            fw.mm(pv[:, tq * 128:(tq + 1) * 128], cs[:, m, tq * 128:(tq + 1) * 128],
                                  wkv[:, m, h * 256 + 128:h * 256 + 256], m == 0, m == 1)
                    vs = vsr.next()
                    fw.cp("act" if ei % 2 == 0 else "dve", vs[:, :n], pv[:, :n])
                    ei += 1
                    fw.dma("pool", VV[h][:, t0 // 128:(t0 + n) // 128, :], vs[:, :n].re("p (t d) -> p t d", d=128))
        if stop_after == "P1":
            return

        with fw.phase():
            win = fw.sb("winq", [128, 8, 512], BF16)
            wq = fw.sb("wq", [128, 4, 3072], BF16)
            fw.dma("sp", win, w_in_b[:, :, 0:512])
            fw.dma("sp", wq, w_q_b)
            qn = fw.sb("qn", [128, 4], F32)
            fw.dma("sp", qn, qnT)
            ones = fw.sb("ones", [128, 128], BF16)
            fw.memset("dve", ones, 1.0)
            xr = fw.ring("xr", 2, [128, 8, 512], F32)
            ur = fw.ring("ur", 2, [128, 8, 512], BF16)
            cr = fw.ring("cq", 2, [128, 4, 512], BF16)
            sqb = fw.sb("sqb", [128, 4, 512], BF16)
            rs = fw.sb("rs", [128, 512], F32)
            tr = fw.ring("tbl", 2, [128, 2, 512], F32)
            t1 = fw.sb("t1", [128, 512], F32)
            t2 = fw.sb("t2", [128, 512], F32)
            qnr = fw.ring("qns", 3, [128, 512], BF16)
            qrr = fw.ring("qrs", 3, [128, 512], BF16)
            pq = fw.ps("pq", [128, 4, 512], F32)
            pss = fw.ps("pss", [128, 512], F32)
            pqo = fw.ring("pqo", 3, [128, 512], F32, ps=True)
            xv = xTo.re("(k p) t -> p k t", p=128)
            chunks = [(0, 256, 1)] + [(256 + 512 * i, 512, 0) for i in range(8)]
            ei = 0
            for (t0, n, j) in chunks:
                xs = xr.next()
                fw.dma("sp", xs[:, :, :n], xv[:, :, t0:t0 + n])
                tb = tr.next()
                fw.dma("sp", tb[:, :, :n], ropeQ[:, :, t0:t0 + n])
                us = ur.next()
                for k in range(8):
                    fw.ts("dve" if k % 2 == 0 else "pool", us[:, k, :n], xs[:, k, :n], sc0[:, k, j:j + 1],
                          modT[:, k, j:j + 1], ALU.mult, ALU.add)
                for m in range(4):
                    for k in range(8):
                        fw.mm(pq[:, m, :n], win[:, k, m * 128:(m + 1) * 128], us[:, k, :n], k == 0, k == 7)
                for m in range(4):
                    fw.act(sqb[:, m, :n], pq[:, m, :n], AF.Square)
                for m in range(4):
                    fw.mm(pss[:, :n], ones, sqb[:, m, :n], m == 0, m == 3)
                fw.act(rs[:, :n], pss[:, :n], AF.Ln, bias=fw.const(RMS_EPS), scale=1.0 / 512)
                fw.act(rs[:, :n], rs[:, :n], AF.Exp, scale=-0.5)
                cs = cr.next()
                for m in range(4):
                    fw.stt("dve", cs[:, m, :n], pq[:, m, :n], qn[:, m:m + 1], rs[:, :n], ALU.mult, ALU.mult)
                for h in range(8):
                    pn = pqo.next()
                    for k in range(4):
                        fw.mm(pn[:, :n], wq[:, k, h * 384:h * 384 + 128], cs[:, k, :n], k == 0, k == 3)
                    qs = qnr.next()
                    fw.cp("act", qs[:, :n], pn[:, :n])
                    fw.dma("pool", QN[h][:, t0:t0 + n], qs[:, :n])
                    pa = pqo.next()
                    for k in range(4):
                        fw.mm(pa[:, :n], wq[:, k, h * 384 + 128:h * 384 + 256], cs[:, k, :n], k == 0, k == 3)
                    pb = pqo.next()
                    for k in range(4):
                        fw.mm(pb[:, :n], wq[:, k, h * 384 + 256:h * 384 + 384], cs[:, k, :n], k == 0, k == 3)
                    fw.tt("dve", t1[:, :n], pa[:, :n], tb[:, 0, :n], ALU.mult)
                    fw.tt("dve", t2[:, :n], pb[:, :n], tb[:, 1, :n], ALU.mult)
                    qr_ = qrr.next()
                    fw.tt("pool", qr_[:, :n], t1[:, :n], t2[:, :n], ALU.add)
                    fw.dma("pool", QR[h][:, t0:t0 + n], qr_[:, :n])
        if stop_after == "P1b":
            return

        with fw.phase():
            NKT = NT // 128
            HALF = NKT // 2
            krp = fw.sb("krp", [128, HALF * 128], BF16)
            fw.dma("sp", krp[0:64, :], KR[:, 0:HALF * 128])
            fw.dma("sp", krp[64:128, :], KR[:, HALF * 128:NT])
            onesf = fw.sb("onesf", [128, 128], F32)
            fw.memset("dve", onesf, 1.0)
            kring = fw.ring("kb", 2, [128, NT], BF16)
            vring = fw.ring("vb", 2, [128, NKT, 128], BF16)
            qnr = fw.ring("qnb", 2, [128, 512], BF16)
            qrr = fw.ring("qrb", 2, [128, 512], BF16)
            ptr = fw.ring("pt", 7, [128, 512], BF16)
            acc0r = fw.ring("acc0", 2, [128, 512], F32)
            acc1r = fw.ring("acc1", 2, [128, 512], F32)
            rec = fw.sb("rec", [128, 512], F32)
            otr = fw.ring("ots", 2, [128, 512], BF16)
            pss_ = fw.ring("ps_s", 5, [128, 512], F32, ps=True)
            pso = fw.ring("ps_o", 2, [128, 512], F32, ps=True)
            psum_ = fw.ps("ps_sum", [128, 512], F32)
            qchunks = [(0, 256, 2)] + [(256 + 512 * i, 512, NKT) for i in range(8)]
            for h in range(8):
                kb = kring.next()
                vb = vring.next()
                for c in range(5):
                    fw.dma("sp", kb[:, c * 3328:(c + 1) * 3328], KT[h][:, c * 3328:(c + 1) * 3328])
                for c in range(5):
                    fw.dma("sp", vb[:, c * 26:(c + 1) * 26, :], VV[h][:, c * 26:(c + 1) * 26, :])
                for (t0, n, nk) in qchunks:
                    qnb = qnr.next()
                    qrb = qrr.next()
                    fw.dma("sp", qnb[:, :n], QN[h][:, t0:t0 + n])
                    fw.dma("sp", qrb[:, :n], QR[h][:, t0:t0 + n])
                    po = pso.next()
                    acc0 = acc0r.next()
                    acc1 = acc1r.next()

                    if nk == NKT:
                        units = [[i_, i_ + HALF] for i_ in range(HALF)]
                    else:
                        units = [[i_] for i_ in range(nk)]
                    order = [t_ for u_ in units for t_ in u_]
                    first_t, last_t = order[0], order[-1]
                    cnt = {"dve": 0, "pool": 0, "i": 0}

                    def qk_unit(unit):
                        pss = [pss_.next() for _ in unit]
                        for ps, jt in zip(pss, unit):
                            fw.mm(ps[:, :n], kb[:, jt * 128:(jt + 1) * 128], qnb[:, :n], True, False)
                        for ps, jt in zip(pss, unit):
                            hf, jj = jt // HALF, jt % HALF
                            fw.mm(ps[:, :n], krp[hf * 64:(hf + 1) * 64, jj * 128:(jj + 1) * 128],
                                  qrb[hf * 64:(hf + 1) * 64, :n], False, True)
                        res = []
                        for ps, jt in zip(pss, unit):
                            p = ptr.next()
                            fw.act(p[:, :n], ps[:, :n], AF.Exp, scale=MLA_SCALE)
                            idx = cnt["i"]
                            cnt["i"] += 1
                            eng, acc = ("pool", acc1) if (idx % 4 == 3 and nk >= 4) else ("dve", acc0)
                            if cnt[eng] == 0:
                                fw.cp(eng, acc[:, :n], p[:, :n])
                            else:
                                fw.tt(eng, acc[:, :n], acc[:, :n], p[:, :n], ALU.add)
                            cnt[eng] += 1
                            res.append((jt, p))
                        return res

                    def pv(jt, p):
                        fw.mm(po[:, :n], vb[:, jt, :], p[:, :n], jt == first_t, jt == last_t)

                    LA = 1 if nk == NKT else 3
                    pend = []
                    for unit in units:
                        pend.append(qk_unit(unit))
                        if len(pend) > LA:
                            for (j0_, p0_) in pend.pop(0):
                                pv(j0_, p0_)
                    for res_ in pend:
                        for (j0_, p0_) in res_:
                            pv(j0_, p0_)
                    fw.mm(psum_[:, :n], onesf, acc0[:, :n], True, nk < 4)
                    if nk >= 4:
                        fw.mm(psum_[:, :n], onesf, acc1[:, :n], False, True)
                    fw.recip(rec[:, :n], psum_[:, :n])
                    ots = otr.next()
                    fw.tt("dve", ots[:, :n], po[:, :n], rec[:, :n], ALU.mult)
                    fw.dma("pool", OT[h * 128:(h + 1) * 128, t0:t0 + n], ots[:, :n])
        if stop_after == "P2":
            return

        chunks = [(0, 256, 1)] + [(256 + 256 * i, 256, 0) for i in range(16)]
        post_phase(fw, OT, 8, xTo, w_o_b, w_fi_b, w_fo_b, modT, lnT, outT, chunks)


def _fm(v, nchunk):
    return np.ascontiguousarray(np.asarray(v, np.float32).reshape(nchunk, 128).T)


def _rope_tables():
    rows = SEQ // 64
    row = np.repeat(np.arange(rows, dtype=np.float32), 64)
    col = np.tile(np.arange(64, dtype=np.float32), rows)
    inv = (np.float32(10000.0) ** (-(2.0 * np.arange(16, dtype=np.float32)) / np.float32(32))).astype(np.float32)
    ang = np.concatenate([row[:, None] * inv, col[:, None] * inv], axis=-1).astype(np.float32)
    cos, sin = np.cos(ang).astype(np.float32), np.sin(ang).astype(np.float32)
    r = np.arange(64)
    a, half, f = r // 32, (r % 32) // 16, r % 16
    cosT = cos[:, a * 16 + f].T
    sinT = (sin[:, a * 16 + f] * np.where(half == 0, -1.0, 1.0).astype(np.float32)).T
    tab = np.zeros((64, 2, NT), np.float32)
    tab[:, 0, :CTX] = 1.0
    tab[:, 0, CTX:] = cosT
    tab[:, 1, CTX:] = sinT
    return tab


_SWAP = np.array([(r // 32) * 32 + (1 - (r % 32) // 16) * 16 + r % 16 for r in range(64)])


def prep_l0(inp, b, qtr, tab):
    x, c, ctx, c_ctx = inp["x"], inp["c"], inp["ctx"], inp["c_ctx"]
    allx = np.concatenate([ctx[b], x[b]], axis=0)
    own = np.concatenate([np.arange(CTX), CTX + qtr * NQ + np.arange(NQ)])
    w_in = inp["mla_w_in"][0]
    w_in_ext = np.concatenate([w_in, w_in[:, 768 + _SWAP]], axis=1)
    wq = inp["mla_w_q_up"][0].reshape(512, 8, 192)
    rope_cols = wq[:, :, 128:]
    wq_ext = np.concatenate([wq[:, :, :128], rope_cols, rope_cols, rope_cols[:, :, _SWAP], rope_cols[:, :, _SWAP]], axis=2)
    tq = tab[:, :, own]
    return {
        "xT": np.ascontiguousarray(allx.T),
        "xTo": np.ascontiguousarray(allx[own].T),
        "cc": np.ascontiguousarray(np.stack([_fm(c[b], 8), _fm(c_ctx, 8)], axis=-1)),
        "wmod": np.ascontiguousarray(inp["w_mod"][0]),
        "bmodT": _fm(inp["b_mod"][0], 48),
        "lnT": np.ascontiguousarray(np.stack([_fm(inp[k][0], 8) for k in ("ln1_g", "ln1_b", "ln2_g", "ln2_b")], axis=1)),
        "w_in": np.ascontiguousarray(w_in_ext),
        "qnT": _fm(inp["mla_q_norm"][0], 4),
        "kvnT": _fm(inp["mla_kv_norm"][0], 2),
        "w_q": np.ascontiguousarray(wq_ext.reshape(512, 3072)),
        "w_kv": np.ascontiguousarray(inp["mla_w_kv_up"][0]),
        "w_o": np.ascontiguousarray(inp["mla_w_out"][0]),
        "w_fi": np.ascontiguousarray(inp["w_ffn_in"][0]),
        "w_fo": np.ascontiguousarray(inp["w_ffn_out"][0]),
        "ropeK": tab,
        "ropeQ": np.ascontiguousarray(np.concatenate([tq, tq], axis=0)),
    }


NTP = NT + 8
NEG = -30000.0


def _pc(t):
    return t + 2 if t < CTX else t + 6


def build_l1(stop_after=None, dbg=False, nlat=SEQ):
    nc = bass.Bass("TRN2", target_bir_lowering=False)
    with ExitStack() as st:
        fw = FW(nc, st)
        emit_l1(fw, stop_after=stop_after, dbg=dbg, nlat=nlat)
        fw.finish()
    return nc


def emit_l1(fw, pre="", hsrc=None, ydst=None, stop_after=None, dbg=False, nlat=SEQ, pdst=None):
    NT = CTX + nlat
    NTP = NT + 8
    if True:
        fw = _Pre(fw, pre)
        EI = "ExternalInput"
        SK = "ExternalOutput" if dbg else "Internal"
        if hsrc is None:
            hT = fw.dram("hT", [D, NT], F32, EI)
            xv_ = hT.re("(k p) t -> p k t", p=128)
            hsrc = lambda t0, n: xv_[:, :, t0:t0 + n]
        cc = fw.dram("cc", [128, 8, 2], F32, EI)
        wmod = fw.dram("wmod", [D, 6 * D], F32, EI)
        bmodT = fw.dram("bmodT", [128, 48], F32, EI)
        w_g = fw.dram("w_g", [D, 1552], F32, EI)
        convT = fw.dram("convT", [128, 8, 5], F32, EI)
        abT = fw.dram("abT", [16, 2], F32, EI)
        normT = fw.dram("normT", [128, 1], F32, EI)
        cst = fw.dram("cst", [128, 384], F32, EI)
        if ydst is None and pdst is None:
            yT = fw.dram("yT", [512, NT], BF16, "ExternalOutput")
            ydst = lambda hv, t0, n: yT[hv * 128:(hv + 1) * 128, t0:t0 + n]
        if pdst is not None:
            w_op = fw.dram("w_op", [512, D], F32, EI)
            w_op_b = fw.dram("w_op_b", [128, 4, D], BF16)
        w_g_b = fw.dram("w_g_b", [128, 8, 1552], BF16)
        PR = fw.dram("PR", [8, 128, NTP], BF16)
        ZS = fw.dram("ZS", [4, 128, NT], BF16, SK)
        BG = fw.dram("BG", [16, 2, NT], F32, SK)
        QKV = fw.dram("QKV", [8, 128, NT], BF16, SK)
        OD = fw.dram("OD", [2, 4, 128, NT], F32, SK)

        modT = fw.sb("modT", [128, 48, 2], F32)
        sc0 = fw.sb("sc0", [128, 8, 2], F32)
        cs = fw.sb("cst", [128, 384], F32)
        fw.dma("sp", cs, cst)
        identf = cs[:, 0:128]
        fw.const(RMS_EPS)
        fw.const(1.0)
        identb = fw.sb("identb", [128, 128], BF16)
        fw.cp("dve", identb, identf)
        onesf = fw.sb("onesf", [128, 128], F32)
        fw.memset("dve", onesf, 1.0)

        mod_phase(fw, cc, wmod, bmodT, modT)
        for j in range(2):
            fw.ts("dve", sc0[:, :, j], modT[:, 8:16, j], 1.0, None, ALU.add)
        conv_weight(fw, w_g, w_g_b, D, 1552)
        if pdst is not None:
            conv_weight(fw, w_op, w_op_b, 512, D)

        chunks = [(0, 256, 1)] + [(256 + 512 * i, 512, 0) for i in range(nlat // 512)]
        with fw.phase():
            wg = fw.sb("wg", [128, 8, 1552], BF16)
            fw.dma("sp", wg, w_g_b)
            ab = fw.sb("ab", [16, 2], F32)
            fw.dma("sp", ab, abT)
            negA = fw.sb("negA", [16, 1], F32)
            fw.act(negA, ab[:, 1:2], AF.Exp)
            fw.ts("dve", negA, negA, -1.0, None, ALU.mult)
            zt = fw.sb("zt", [128, 4], BF16)
            fw.memset("dve", zt, 0.0)
            for ch in range(8):
                fw.dma("pool", PR[ch][:, 0:2], zt[:, 0:2])
                fw.dma("pool", PR[ch][:, 258:262], zt[:, 0:4])
                fw.dma("pool", PR[ch][:, NTP - 2:NTP], zt[:, 0:2])
            xr = fw.ring("xr", 2, [128, 8, 512], F32)
            ur = fw.ring("ur", 2, [128, 8, 512], BF16)
            sr = fw.ring("st", 3, [128, 512], BF16)
            zr = fw.ring("zs", 3, [128, 512], BF16)
            bgs = fw.ring("bgs", 2, [16, 2, 512], F32)
            et = fw.sb("et", [16, 512], F32)
            pp = fw.ring("pp", 6, [128, 512], F32, ps=True)
            pba = fw.ps("pba", [16, 512], F32)
            ei = 0
            for (t0, n, j) in chunks:
                xs = xr.next()
                src_ = hsrc(t0, n)
                if isinstance(src_, list):
                    for k in range(8):
                        fw.dma("sp", xs[:, k, :n], src_[k])
                else:
                    fw.dma("sp", xs[:, :, :n], src_)
                us = ur.next()
                for k in range(8):
                    fw.ts("dve" if k % 2 == 0 else "pool", us[:, k, :n], xs[:, k, :n], sc0[:, k, j:j + 1],
                          modT[:, k, j:j + 1], ALU.mult, ALU.add)
                for mt in range(12):
                    p = pp.next()
                    for k in range(8):
                        fw.mm(p[:, :n], wg[:, k, mt * 128:(mt + 1) * 128], us[:, k, :n], k == 0, k == 7)
                    if mt < 8:
                        s = sr.next()
                        fw.cp("act" if ei % 2 == 0 else "dve", s[:, :n], p[:, :n])
                        ei += 1
                        fw.dma("pool", PR[mt][:, _pc(t0):_pc(t0) + n], s[:, :n])
                    else:
                        z = zr.next()
                        fw.act(z[:, :n], p[:, :n], AF.Silu)
                        fw.dma("pool", ZS[mt - 8][:, t0:t0 + n], z[:, :n])
                for k in range(8):
                    fw.mm(pba[:, :n], wg[:, k, 1536:1552], us[:, k, :n], k == 0, k == 7)
                bg = bgs.next()
                fw.act(bg[:, 0, :n], pba[:, :n], AF.Sigmoid)
                fw.act(et[:, :n], pba[:, :n], AF.Exp, bias=ab[:, 0:1])
                fw.act(et[:, :n], et[:, :n], AF.Ln, bias=fw.const(1.0)[0:16, :])
                fw.ts("dve", bg[:, 1, :n], et[:, :n], negA, None, ALU.mult)
                fw.dma("pool", BG[:, :, t0:t0 + n], bg[:, :, :n])
        with fw.phase():
            cw = fw.sb("cw", [128, 8, 5], F32)
            fw.dma("sp", cw, convT)
            dg = fw.sb("dg", [128, 40, 128], BF16)
            for ch in range(8):
                for jj in range(5):
                    fw.ts("dve" if (ch + jj) % 2 == 0 else "pool", dg[:, ch * 5 + jj, :], identb, cw[:, ch, jj:jj + 1], None, ALU.mult)
            pr = fw.ring("prr", 4, [128, 516], BF16)
            a1 = fw.ring("a1", 3, [128, 512], F32)
            sq = fw.sb("sq", [128, 512], F32)
            rs = fw.sb("rs", [128, 512], F32)
            ob = fw.ring("ob", 3, [128, 512], BF16)
            pss = fw.ring("pss", 2, [128, 512], F32, ps=True)
            pcv = fw.ring("pcv", 3, [128, 512], F32, ps=True)
            for (t0, n, j) in chunks:
                for ch in range(8):
                    x = pr.next()
                    fw.dma("sp", x[:, :n + 4], PR[ch][:, _pc(t0) - 2:_pc(t0) + n + 2])
                    pc_ = pcv.next()
                    for jj in range(5):
                        fw.mm(pc_[:, :n], dg[:, ch * 5 + jj, :], x[:, jj:jj + n], jj == 0, jj == 4)
                    s = a1.next()
                    fw.act(s[:, :n], pc_[:, :n], AF.Silu)
                    o = ob.next()
                    if ch < 4:
                        fw.tt("pool", sq[:, :n], s[:, :n], s[:, :n], ALU.mult)
                        ps_ = pss.next()
                        fw.mm(ps_[:, :n], onesf, sq[:, :n], True, True)
                        fw.act(rs[:, :n], ps_[:, :n], AF.Ln, bias=fw.const(RMS_EPS))
                        fw.act(rs[:, :n], rs[:, :n], AF.Exp, scale=-0.5)
                        if ch < 2:
                            fw.stt("dve", o[:, :n], s[:, :n], 128 ** -0.5, rs[:, :n], ALU.mult, ALU.mult)
                        else:
                            fw.tt("dve", o[:, :n], s[:, :n], rs[:, :n], ALU.mult)
                    else:
                        fw.cp("pool", o[:, :n], s[:, :n])
                    fw.dma("pool", QKV[ch][:, t0:t0 + n], o[:, :n])
        if stop_after == "G1":
            return

        with fw.phase():
            tri = [cs[0:64, 128:192], cs[0:64, 192:256]]
            negS = [cs[0:64, 256:320], cs[0:64, 320:384]]
            id64 = cs[0:64, 0:64]
            GM = 8
            def mk_shared(i):
                return dict(
                    fm=[fw.sb("fm%d" % i, [128, GM * 64], BF16) for _ in range(8)],
                    kt=[fw.sb("kt%d" % i, [64, GM, 128], BF16) for _ in range(2)],
                    vt=[fw.sb("vt%d" % i, [64, GM, 128], BF16) for _ in range(4)],
                    kk=[fw.sb("kk%d" % i, [64, GM, 64], F32) for _ in range(2)],
                    at=[fw.sb("at%d" % i, [64, GM, 64], F32) for _ in range(2)],
                    bgf=fw.sb("bgf%d" % i, [16, 2, GM * 64], F32),
                    bT=fw.sb("bT%d" % i, [64, GM, 16], F32),
                    gT=fw.sb("gT%d" % i, [64, GM, 16], F32),
                    gc=[fw.sb("gc%d_%d" % (i, d), [64, GM, 16], F32) for d in range(2)],
                )
            shared = [mk_shared(0), mk_shared(1)]
            prob = {}
            for hv in range(4):
                for d in range(2):
                    prob[(hv, d)] = dict(
                        U=fw.sb("U", [64, GM, 128], BF16), Kd=fw.sb("Kd", [64, GM, 128], BF16),
                        WT=fw.sb("WT", [128, GM, 64], BF16), QdT=fw.sb("QdT", [128, GM, 64], BF16),
                        AT=fw.sb("ATb", [64, GM, 64], BF16), gl=fw.sb("gl", [128, GM], F32),
                        S=fw.sb("S", [128, 128], F32), Sb=fw.sb("Sb", [128, 128], BF16),
                        oo=fw.sb("oo", [128, GM, 64], F32))
                    fw.memset("dve", prob[(hv, d)]["S"], 0.0)
                    fw.memset("pool", prob[(hv, d)]["Sb"], 0.0)
            T = {nm: fw.sb(nm, [64, GM, 64], F32) for nm in ("E1", "E2", "diff", "m1", "DmS", "DmST")}
            NSET = 2
            TS = [{nm: fw.sb(nm, [64, GM, 64], F32) for nm in ("Tt", "X", "Y", "X2", "Y2")} for _ in range(NSET)]
            TtB = fw.sb("TtB", [64, GM, 64], BF16)
            Rv = fw.sb("Rv", [64, GM, 128], BF16)
            Rk = fw.sb("Rk", [64, GM, 128], BF16)
            eg = fw.sb("eg", [128, GM, 64], F32)
            c64 = {nm: fw.sb(nm, [64, GM], F32) for nm in ("egc", "kco", "kdc")}
            vnr = fw.ring("vn", 4, [64, 128], BF16)
            pf = fw.ring("pf", 6, [128, 512], F32, ps=True)
            pb = fw.ring("pb", 2, [128, 1024], BF16, ps=True)

            def b3(x, G, n):
                return B(x.ap.unsqueeze(2).to_broadcast([x.ap.shape[0], G, n]), x.tok)

            def m3(x, G):
                return B(x.ap.unsqueeze(1).to_broadcast([x.ap.shape[0], G, x.ap.shape[1]]), x.tok)

            def prep_group(sh, c0, G):
                t0, n = c0 * 64, G * 64
                for ch in range(8):
                    fw.dma("sp", sh["fm"][ch][:, :n], QKV[ch][:, t0:t0 + n])
                fw.dma("sp", sh["bgf"][:, :, :n], BG[:, :, t0:t0 + n])
                for idx, (src, dst) in enumerate([(2, sh["kt"][0]), (3, sh["kt"][1])] +
                                                 [(4 + v, sh["vt"][v]) for v in range(4)]):
                    p = pb.next()
                    pv = p[0:64, 0:G * 128].re("p (g d) -> p g d", d=128)
                    for g in range(G):
                        fw.op("pe", (lambda o, i: (lambda e: e.transpose(o, i, identb.ap)))(pv[:, g, :].ap, sh["fm"][src][:, g * 64:(g + 1) * 64].ap),
                              reads=[sh["fm"][src], identb], writes=[p])
                    fw.cp("act" if idx % 2 == 0 else "dve", dst[:, :G, :], pv)
                for pl_, dst in ((0, sh["bT"]), (1, sh["gT"])):
                    p = pf.next()
                    pv = p[0:64, 0:G * 16].re("p (g r) -> p g r", r=16)
                    for g in range(G):
                        fw.op("pe", (lambda o, i: (lambda e: e.transpose(o, i, identf[0:16, 0:16].ap)))(pv[:, g, :].ap, sh["bgf"][:, pl_, g * 64:(g + 1) * 64].ap),
                              reads=[sh["bgf"], cs], writes=[p])
                    fw.cp("dve", dst[:, :G, :], pv)
                for d in range(2):
                    p = pf.next()
                    fw.mm(p[0:64, 0:G * 16], tri[d], sh["gT"][:, :G, :].re("p g r -> p (g r)"), True, True)
                    fw.cp("act", sh["gc"][d][:, :G, :], p[0:64, 0:G * 16].re("p (g r) -> p g r", r=16))
                for kh in range(2):
                    KT, QT = sh["fm"][2 + kh], sh["fm"][kh]
                    p = pf.next()
                    for g in range(G):
                        fw.mm(p[0:64, g * 64:(g + 1) * 64], KT[:, g * 64:(g + 1) * 64], KT[:, g * 64:(g + 1) * 64], True, True)
                    fw.cp("act", sh["kk"][kh][:, :G, :], p[0:64, 0:n].re("p (g j) -> p g j", j=64))
                    p = pf.next()
                    for g in range(G):
                        fw.mm(p[0:64, g * 64:(g + 1) * 64], KT[:, g * 64:(g + 1) * 64], QT[:, g * 64:(g + 1) * 64], True, True)
                    fw.cp("dve", sh["at"][kh][:, :G, :], p[0:64, 0:n].re("p (g j) -> p g j", j=64))

            def prep_problem(sh, G, hv, d, ts):
                P = prob[(hv, d)]
                kh = hv // 2
                n = G * 64
                beta = sh["bT"][:, :G, d * 8 + hv]
                gc = sh["gc"][d][:, :G, d * 8 + 4 + hv]
                t = {k: v[:, :G, :] for k, v in T.items()}
                t.update({k: v[:, :G, :] for k, v in TS[ts].items()})
                idG = m3(id64, G)
                fw.tt("dve", t["E1"], idG, b3(gc, G, 64), ALU.mult)
                fw.tt("pool", t["E2"], idG, b3(beta, G, 64), ALU.mult)
                pg = pf.next()
                fw.mm(pg[:, :n], onesf[0:64, :], t["E1"].re("p g j -> p (g j)"), True, True)
                pbt = pf.next()
                fw.mm(pbt[0:64, :n], onesf[0:64, 0:64], t["E2"].re("p g j -> p (g j)"), True, True)
                gcrow = pg[0:64, :n].re("p (g j) -> p g j", j=64)
                brow = pbt[0:64, :n].re("p (g j) -> p g j", j=64)
                last = 63 if d == 0 else 0
                fw.stt("dve", t["diff"], gcrow, -1.0, b3(gc, G, 64), ALU.mult, ALU.add)
                fw.stt("dve", t["m1"], t["diff"], 0.0, m3(negS[d], G), ALU.min, ALU.add)
                fw.act(t["DmS"], t["m1"], AF.Exp)
                fw.ts("pool", t["m1"], t["diff"], -1.0, 0.0, ALU.mult, ALU.min)
                fw.tt("pool", t["m1"], t["m1"], m3(negS[1 - d], G), ALU.add)
                fw.act(t["DmST"], t["m1"], AF.Exp)
                fw.stt("dve", t["X"], sh["kk"][kh][:, :G, :], -1.0, t["DmS"], ALU.mult, ALU.mult)
                fw.tt("dve", t["X"], t["X"], b3(beta, G, 64), ALU.mult)
                fw.tt("pool", t["Y"], sh["kk"][kh][:, :G, :], t["DmST"], ALU.mult)
                fw.stt("dve", t["Y"], t["Y"], -1.0, brow, ALU.mult, ALU.mult)
                fw.tt("pool", t["m1"], t["DmST"], idG, ALU.add)
                fw.tt("pool", P["AT"][:, :G, :], sh["at"][kh][:, :G, :], t["m1"], ALU.mult)
                fw.tt("pool", t["Tt"], t["Y"], idG, ALU.add)
                fw.act(eg[:, :G, :], pg[:, :n].re("p (g j) -> p g j", j=64), AF.Exp)
                fw.cp("dve", P["gl"][:, :G], eg[:, :G, last])
                fw.tt("dve", c64["kdc"][:, :G], gcrow[:, :, last], gc, ALU.subtract)
                fw.act(c64["kdc"][:, :G], c64["kdc"][:, :G], AF.Exp)
                fw.tt("pool", P["QdT"][:, :G, :], sh["fm"][kh][:, :n].re("p (g j) -> p g j", j=64), eg[:, :G, :], ALU.mult)
                fw.tt("pool", P["Kd"][:, :G, :], sh["kt"][kh][:, :G, :], b3(c64["kdc"][:, :G], G, 128), ALU.mult)
                X, Y, X2, Y2 = t["X"], t["Y"], t["X2"], t["Y2"]
                for lv in range(5):
                    px = pf.next()
                    for g in range(G):
                        fw.mm(px[0:64, g * 64:(g + 1) * 64], Y[:, g, :], X[:, g, :], True, True)
                    fw.cp("act", X2, px[0:64, :n].re("p (g j) -> p g j", j=64))
                    if lv < 4:
                        py_ = pf.next()
                        for g in range(G):
                            fw.mm(py_[0:64, g * 64:(g + 1) * 64], X[:, g, :], Y[:, g, :], True, True)
                        fw.cp("dve", Y2, py_[0:64, :n].re("p (g j) -> p g j", j=64))
                    ptt = pf.next()
                    for g in range(G):
                        fw.mm(ptt[0:64, g * 64:(g + 1) * 64], X2[:, g, :], t["Tt"][:, g, :], True, True)
                    fw.tt("dve", t["Tt"], t["Tt"], ptt[0:64, :n].re("p (g j) -> p g j", j=64), ALU.add)
                    X, X2 = X2, X
                    Y, Y2 = Y2, Y
                    yield
                fw.act(c64["egc"][:, :G], gc, AF.Exp)
                fw.tt("dve", c64["kco"][:, :G], c64["egc"][:, :G], beta, ALU.mult)
                fw.tt("dve", Rv[:, :G, :], sh["vt"][hv][:, :G, :], b3(beta, G, 128), ALU.mult)
                fw.tt("dve", Rk[:, :G, :], sh["kt"][kh][:, :G, :], b3(c64["kco"][:, :G], G, 128), ALU.mult)
                fw.cp("pool", TtB[:, :G, :], t["Tt"])
                for hf in range(0, G, 4):
                    g1 = min(G, hf + 4)
                    pu = pf.next()
                    for g in range(hf, g1):
                        fw.mm(pu[0:64, (g - hf) * 128:(g - hf + 1) * 128], TtB[:, g, :], Rv[:, g, :], True, True)
                    fw.cp("act", P["U"][:, hf:g1, :], pu[0:64, 0:(g1 - hf) * 128].re("p (g j) -> p g j", j=128))
                pw = pf.next()
                for g in range(G):
                    fw.mm(pw[:, g * 64:(g + 1) * 64], Rk[:, g, :], TtB[:, g, :], True, True)
                fw.cp("dve", P["WT"][:, :G, :], pw[:, :n].re("p (g j) -> p g j", j=64))

            def scan_step(hv, d, g):
                P = prob[(hv, d)]
                p1 = pf.next()
                fw.mm(p1[0:64, 0:128], P["WT"][:, g, :], P["Sb"], True, True)
                vn = vnr.next()
                fw.tt("dve", vn, P["U"][:, g, :], p1[0:64, 0:128], ALU.subtract)
                po = pf.next()
                fw.mm(po[:, 0:64], P["Sb"], P["QdT"][:, g, :], True, False)
                fw.mm(po[:, 0:64], vn, P["AT"][:, g, :], False, True)
                fw.cp("act", P["oo"][:, g, :], po[:, 0:64])
                ps_ = pf.next()
                fw.mm(ps_[:, 0:128], P["Kd"][:, g, :], vn, True, True)
                fw.stt("dve", P["S"], P["S"], P["gl"][:, g:g + 1], ps_[:, 0:128], ALU.mult, ALU.add)
                fw.cp("pool", P["Sb"], P["S"])

            groups = [(0, 4)] + [(4 + 8 * k, 8) for k in range(nlat // 512)]
            NG = len(groups)
            for it in range(NG):
                gf = groups[it]
                gb = groups[0] if it == 0 else groups[NG - it]
                todo = [(gf, (0,)), (gb, (1,))] if gf != gb else [(gf, (0, 1))]
                plist = []
                for si, ((c0, G), dirs) in enumerate(todo):
                    sh = shared[si]
                    prep_group(sh, c0, G)
                    for hv in range(4):
                        for d in dirs:
                            plist.append((sh, G, hv, d))
                for i0_ in range(0, len(plist), NSET):
                    gens = [prep_problem(*a, ts) for ts, a in enumerate(plist[i0_:i0_ + NSET])]
                    while gens:
                        for g_ in list(gens):
                            try:
                                next(g_)
                            except StopIteration:
                                gens.remove(g_)
                for s in range(8):
                    for hv in range(4):
                        for d in range(2):
                            c0, G = gf if d == 0 else gb
                            if s >= G:
                                continue
                            g = s if d == 0 else G - 1 - s
                            scan_step(hv, d, g)
                for d, (c0, G) in ((0, gf), (1, gb)):
                    for hv in range(4):
                        fw.dma("pool", OD[d][hv][:, c0 * 64:(c0 + G) * 64],
                               prob[(hv, d)]["oo"][:, :G, :].re("p g j -> p (g j)"))
        if stop_after == "G2":
            return

        with fw.phase():
            nw = fw.sb("nw", [128, 1], F32)
            fw.dma("sp", nw, normT)
            ofr = fw.ring("of", 2, [128, 512], F32)
            obr = fw.ring("obb", 2, [128, 512], F32)
            zr = fw.ring("zz", 2, [128, 512], BF16)
            sq = fw.sb("sq", [128, 512], F32)
            rs = fw.sb("rs", [128, 512], F32)
            yr = fw.ring("yy", 8, [128, 512], BF16)
            pss = fw.ring("pss", 2, [128, 512], F32, ps=True)
            if pdst is not None:
                wop = fw.sb("wop", [128, 4, D], BF16)
                fw.dma("sp", wop, w_op_b)
                ppr = fw.ring("ppr", 3, [128, 512], F32, ps=True)
                pst = fw.ring("pst", 3, [128, 512], F32)
            for (t0, n, j) in chunks:
                ys = []
                for hv in range(4):
                    a, b_, z = ofr.next(), obr.next(), zr.next()
                    fw.dma("sp", a[:, :n], OD[0][hv][:, t0:t0 + n])
                    fw.dma("sp", b_[:, :n], OD[1][hv][:, t0:t0 + n])
                    fw.dma("sp", z[:, :n], ZS[hv][:, t0:t0 + n])
                    fw.tt("dve", a[:, :n], a[:, :n], b_[:, :n], ALU.add)
                    fw.tt("pool", sq[:, :n], a[:, :n], a[:, :n], ALU.mult)
                    p = pss.next()
                    fw.mm(p[:, :n], onesf, sq[:, :n], True, True)
                    fw.act(rs[:, :n], p[:, :n], AF.Ln, bias=fw.const(RMS_EPS), scale=1.0 / 128)
                    fw.act(rs[:, :n], rs[:, :n], AF.Exp, scale=-0.5)
                    fw.stt("dve", a[:, :n], a[:, :n], nw[:, 0:1], rs[:, :n], ALU.mult, ALU.mult)
                    y = yr.next()
                    fw.tt("pool", y[:, :n], a[:, :n], z[:, :n], ALU.mult)
                    ys.append(y)
                    if ydst is not None:
                        fw.dma("pool", ydst(hv, t0, n), y[:, :n])
                if pdst is not None and t0 >= CTX:
                    for oc in range(8):
                        pp_ = ppr.next()
                        for hv in range(4):
                            fw.mm(pp_[:, :n], wop[:, hv, oc * 128:(oc + 1) * 128], ys[hv][:, :n], hv == 0, hv == 3)
                        sg_ = pst.next()
                        fw.cp("act" if oc % 2 == 0 else "dve", sg_[:, :n], pp_[:, :n])
                        fw.dma("pool", pdst(oc, t0, n), sg_[:, :n])
        return modT


def build_l2():
    nc = bass.Bass("TRN2", target_bir_lowering=False)
    with ExitStack() as st:
        fw = FW(nc, st)
        emit_l2(fw)
        fw.finish()
    return nc


def emit_l2(fw, pre="", mixT=None, resT=None, modT=None, mixload=None, ydirect=None):
    if True:
        fw = _Pre(fw, pre)
        EI = "ExternalInput"
        if mixT is None and ydirect is None:
            mixT = fw.dram("mixT", [2048, NQ], BF16, EI)
            resT = fw.dram("resT", [D, NQ], F32, EI)
        if modT is None:
            cc = fw.dram("cc", [128, 8, 2], F32, EI)
            wmod = fw.dram("wmod", [D, 6 * D], F32, EI)
            bmodT = fw.dram("bmodT", [128, 48], F32, EI)
        lnTd = fw.dram("lnT", [128, 4, 8], F32, EI)
        if ydirect is None:
            w_o = fw.dram("w_o", [2048, D], F32, EI)
        w_fi = fw.dram("w_fi", [D, 2 * DFF], F32, EI)
        w_fo = fw.dram("w_fo", [DFF, D], F32, EI)
        outT = fw.dram("outT", [D, NQ], F32, "ExternalOutput")
        w_o_b = fw.dram("w_o_b", [128, 16, D], BF16)
        w_fi_b = fw.dram("w_fi_b", [128, 8, 2 * DFF], BF16)
        w_fo_b = fw.dram("w_fo_b", [128, 22, D], BF16)
        have_mod = modT is not None
        if not have_mod:
            modT = fw.sb("modT", [128, 48, 2], F32)
        lnT = fw.sb("lnTs", [128, 4, 8], F32)
        fw.dma("sp", lnT, lnTd)
        fw.const(LN_EPS)
        if not have_mod:
            mod_phase(fw, cc, wmod, bmodT, modT)
        if ydirect is None:
            conv_weight(fw, w_o, w_o_b, 2048, D)
        conv_weight(fw, w_fi, w_fi_b, D, 2 * DFF)
        conv_weight(fw, w_fo, w_fo_b, DFF, D)
        chunks = [(256 * i, 256, 0) for i in range(16)]
        post_phase(fw, mixT, 16, resT, w_o_b, w_fi_b, w_fo_b, modT, lnT, outT, chunks, mixload=mixload, ydirect=ydirect)


def _l1_cols(hg):
    q = np.arange(2 * hg * 128, (2 * hg + 2) * 128)
    k = 1024 + q
    v = 2048 + np.arange(4 * hg * 128, (4 * hg + 4) * 128)
    z = 2048 + v
    ba = np.array([6144 + d * 32 + ab * 16 + 4 * hg + h for d in range(2) for ab in range(2) for h in range(4)])
    return np.concatenate([q, k, v, z, ba])


def _cst():
    c = np.zeros((128, 384), np.float32)
    c[:, 0:128] = np.eye(128, dtype=np.float32)
    p = np.arange(64)[:, None]
    f = np.arange(64)[None, :]
    c[0:64, 128:192] = (p <= f)
    c[0:64, 192:256] = (p >= f)
    c[0:64, 256:320] = np.where(f < p, 0.0, NEG)
    c[0:64, 320:384] = np.where(f > p, 0.0, NEG)
    return c


def prep_l1(inp, hT_all, b, hg):
    cols = _l1_cols(hg)
    conv = inp["gdn_conv"][0][:, cols[:1024]]
    ab = np.zeros((16, 2), np.float32)
    for d in range(2):
        ab[d * 8 + 4:d * 8 + 8, 0] = inp["gdn_dt_bias"][0][d, 4 * hg:4 * hg + 4]
        ab[d * 8 + 4:d * 8 + 8, 1] = inp["gdn_a_log"][0][d, 4 * hg:4 * hg + 4]
    return {
        "hT": hT_all,
        "cc": np.ascontiguousarray(np.stack([_fm(inp["c"][b], 8), _fm(inp["c_ctx"], 8)], axis=-1)),
        "wmod": np.ascontiguousarray(inp["w_mod"][1]),
        "bmodT": _fm(inp["b_mod"][1], 48),
        "w_g": np.ascontiguousarray(inp["gdn_w_in"][0][:, cols]),
        "convT": np.ascontiguousarray(conv.T.reshape(8, 128, 5).transpose(1, 0, 2)),
        "abT": ab,
        "normT": np.ascontiguousarray(inp["gdn_norm"][0].reshape(128, 1).astype(np.float32)),
        "cst": _cst(),
    }


def prep_l2(inp, mixT, resT, b):
    return {
        "mixT": mixT, "resT": resT,
        "cc": np.ascontiguousarray(np.stack([_fm(inp["c"][b], 8), _fm(inp["c_ctx"], 8)], axis=-1)),
        "wmod": np.ascontiguousarray(inp["w_mod"][1]),
        "bmodT": _fm(inp["b_mod"][1], 48),
        "lnT": np.ascontiguousarray(np.stack([_fm(inp[k][1], 8) for k in ("ln1_g", "ln1_b", "ln2_g", "ln2_b")], axis=1)),
        "w_o": np.ascontiguousarray(inp["gdn_w_out"][0]),
        "w_fi": np.ascontiguousarray(inp["w_ffn_in"][1]),
        "w_fo": np.ascontiguousarray(inp["w_ffn_out"][1]),
    }


def _bf(a):
    return a.view(ml_dtypes.bfloat16) if a.dtype.kind == "V" else a


def kernel_unfused(**inp):
    inp = {k: np.asarray(v) for k, v in inp.items()}
    cores = list(range(8))
    tab = _rope_tables()
    r0 = run_bass_kernel_spmd(build_l0(), [prep_l0(inp, c // 4, c % 4, tab) for c in cores], core_ids=cores).results
    hT = []
    for b in range(2):
        parts = [r0[4 * b]["outT"][:, :CTX]] + [r0[4 * b + q]["outT"][:, CTX:] for q in range(4)]
        hT.append(np.ascontiguousarray(np.concatenate(parts, axis=1)))
    r1 = run_bass_kernel_spmd(build_l1(), [prep_l1(inp, hT[c // 4], c // 4, c % 4) for c in cores], core_ids=cores).results
    maps = []
    for c in cores:
        b, q = c // 4, c % 4
        sl = slice(CTX + q * NQ, CTX + (q + 1) * NQ)
        mix = np.concatenate([_bf(r1[4 * b + hg]["yT"])[:, sl] for hg in range(4)], axis=0)
        maps.append(prep_l2(inp, np.ascontiguousarray(mix), np.ascontiguousarray(hT[b][:, sl]), b))
    r2 = run_bass_kernel_spmd(build_l2(), maps, core_ids=cores).results
    out = np.empty((2, SEQ, D), np.float32)
    for c in cores:
        b, q = c // 4, c % 4
        out[b, q * NQ:(q + 1) * NQ, :] = r2[c]["outT"].T
    return out


GROUPS = [[0, 1, 2, 3], [4, 5, 6, 7]]


def collective(fw, kind, src, dst, op=None):
    tok = Tok()
    tok.sem = fw.getsem()
    key = tok.sem
    waits = fw._waits("pool", [src.tok], [dst.tok])
    fw.cnt[key] += 1
    v = fw.cnt[key]
    sa, da = src.ap, dst.ap
    aop = ALU.bypass if op is None else op
    fw.lists["pool"].append((waits, lambda e: e.collective_compute(kind, aop, replica_groups=GROUPS,
                                                                 ins=[sa.opt()], outs=[da.opt()]), key, 1))
    dst.tok.w = {key: v}
    dst.tok.r = {}
    src.tok.r[key] = v


def build_fused():
    nc = bass.Bass("TRN2", target_bir_lowering=False)
    with ExitStack() as st:
        fw = FW(nc, st)
        HC = 2048
        h2own = fw.dram("h2own", [D, NO], F32)
        hsend = [[fw.dram("hsend_%d_%d" % (k, c), [128, HC], F32) for c in range(2)] for k in range(8)]
        hgat = [[fw.dram("hgat_%d_%d" % (k, c), [4 * 128, HC], F32) for c in range(2)] for k in range(8)]
        ppart = [fw.dram("ppart_%d" % i, [4 * 256, NQ], F32) for i in range(4)]
        ysum = [fw.dram("ysum_%d" % i, [256, NQ], F32) for i in range(4)]
        emit_l0(fw, pre="a_", outT=h2own)
        with fw.phase():
            bounce = fw.ring("bnc", 2, [128, 8, 512], F32)
            sv = h2own.re("(k p) t -> p k t", p=128)
            for i in range(NQ // 512):
                b_ = bounce.next()
                fw.dma("sp", b_, sv[:, :, CTX + i * 512:CTX + (i + 1) * 512])
                c, o2 = (i * 512) // HC, (i * 512) % HC
                for k in range(8):
                    fw.dma("pool", hsend[k][c][:, o2:o2 + 512], b_[:, k, :])
        for k in range(8):
            for c in range(2):
                collective(fw, "AllGather", hsend[k][c], hgat[k][c])
        ctxv = h2own.re("(k p) t -> p k t", p=128)

        def hsrc(t0, n):
            if t0 < CTX:
                return ctxv[:, :, t0:t0 + n]
            q, off = (t0 - CTX) // NQ, (t0 - CTX) % NQ
            c, o2 = off // HC, off % HC
            return [hgat[k][c][q * 128:(q + 1) * 128, o2:o2 + n] for k in range(8)]

        def pdst(oc, t0, n):
            q, off = (t0 - CTX) // NQ, (t0 - CTX) % NQ
            r0 = q * 256 + (oc % 2) * 128
            return ppart[oc // 2][r0:r0 + 128, off:off + n]

        modT1 = emit_l1(fw, pre="b_", hsrc=hsrc, pdst=pdst)
        for i in range(4):
            collective(fw, "ReduceScatter", ppart[i], ysum[i], op=ALU.add)
        emit_l2(fw, pre="c_", resT=h2own[:, CTX:], modT=modT1, ydirect=ysum)
        fw.finish()
    return nc


def kernel(**inp):
    inp = {k: np.asarray(v) for k, v in inp.items()}
    cores = list(range(8))
    tab = _rope_tables()
    maps = []
    for c in cores:
        b, r = c // 4, c % 4
        m = {"a_" + k: v for k, v in prep_l0(inp, b, r, tab).items()}
        l1 = prep_l1(inp, None, b, r)
        l1.pop("hT")
        m.update({"b_" + k: v for k, v in l1.items()})
        l2 = prep_l2(inp, None, None, b)
        for k in ("mixT", "resT", "cc", "wmod", "bmodT", "w_o"):
            l2.pop(k)
        m.update({"c_" + k: v for k, v in l2.items()})
        m["b_w_op"] = np.ascontiguousarray(inp["gdn_w_out"][0][r * 512:(r + 1) * 512])
        maps.append(m)
    res = run_bass_kernel_spmd(build_fused(), maps, core_ids=cores).results
    out = np.empty((2, SEQ, D), np.float32)
    for c in cores:
        b, q = c // 4, c % 4
        out[b, q * NQ:(q + 1) * NQ, :] = res[c]["c_outT"].T
    return out
```

```python
import math
import numpy as np
import ml_dtypes
from contextlib import ExitStack
import concourse.bass as bass
import concourse.mybir as mybir
from concourse.bass_utils import run_bass_kernel_spmd

F32 = mybir.dt.float32
BF16 = mybir.dt.bfloat16
ALU = mybir.AluOpType
AF = mybir.ActivationFunctionType

D = 1024
SEQ = 16384
CTX = 256
NT = SEQ + CTX
NQ = SEQ // 4
NO = NQ + CTX
DFF = 2816
ALPHA = (2.0 * 2) ** 0.25
LN_EPS = 1e-5
RMS_EPS = 1e-6
MLA_SCALE = 192 ** -0.5


class Tok:
    __slots__ = ("w", "r", "sem")

    def __init__(self):
        self.w = {}
        self.r = {}
        self.sem = None


class B:
    def __init__(self, ap, tok=None, sb=True):
        self.ap = ap
        self.tok = tok if tok is not None else Tok()
        self.sb = sb

    def __getitem__(self, idx):
        return B(self.ap[idx], self.tok, self.sb)

    def re(self, pat, **kw):
        return B(self.ap.rearrange(pat, **kw), self.tok, self.sb)


class Ring:
    def __init__(self, bufs):
        self.bufs = bufs
        self.i = 0

    def next(self):
        b = self.bufs[self.i % len(self.bufs)]
        self.i += 1
        return b


class FW:
    ENG = ("pe", "act", "dve", "pool", "sp")

    def __init__(self, nc, stack):
        self.nc = nc
        self.outer = stack
        self.stack = stack
        self.lists = {e: [] for e in self.ENG}
        self.sems = {}
        self.cnt = {}
        self.seen = {e: {} for e in self.ENG}
        self.free_sems = []
        self.phase_sems = []
        self.nsem = 0
        self.uid = 0
        self.consts = {}
        self.ekey = {}
        for e in ("pe", "act", "dve", "pool"):
            self.ekey[e] = self._newsem("E_" + e)

    def _newsem(self, key):
        self.sems[key] = self.outer.enter_context(self.nc.semaphore(key))
        self.cnt[key] = 0
        self.nsem += 1
        return key

    def getsem(self):
        if self.free_sems:
            k = self.free_sems.pop()
        else:
            k = self._newsem("S%d" % self.nsem)
        self.phase_sems.append(k)
        return k

    def const(self, val):
        key = float(val)
        if key not in self.consts:
            assert self.stack is self.outer, "create consts before phases"
            saved = self.stack
            self.stack = self.outer
            c = self.sb("const", [128, 1], F32)
            self.stack = saved
            self.memset("dve", c, key)
            self.consts[key] = c
        return self.consts[key]

    def name(self, n):
        self.uid += 1
        return "%s_%d" % (n, self.uid)

    def sb(self, name, shape, dt):
        return B(self.stack.enter_context(self.nc.sbuf_tensor(self.name(name), list(shape), dt))[:])

    def ps(self, name, shape, dt=F32):
        return B(self.stack.enter_context(self.nc.psum_tensor(self.name(name), list(shape), dt))[:])

    def dram(self, name, shape, dt, kind="Internal"):
        return B(self.nc.dram_tensor(name, list(shape), dt, kind=kind).ap(), sb=False)

    def ring(self, name, n, shape, dt, ps=False):
        return Ring([(self.ps if ps else self.sb)(name, shape, dt) for _ in range(n)])

    def _waits(self, eng, reads, writes):
        deps = {}
        for t in reads:
            for k, v in t.w.items():
                deps[k] = max(deps.get(k, 0), v)
        for t in writes:
            for k, v in t.w.items():
                deps[k] = max(deps.get(k, 0), v)
            for k, v in t.r.items():
                deps[k] = max(deps.get(k, 0), v)
        out = []
        seen = self.seen[eng]
        for k, v in deps.items():
            if eng == "pe" and k.startswith("E_pe"):
                continue
            if seen.get(k, 0) >= v:
                continue
            seen[k] = v
            out.append((k, v))
        return out

    def op(self, eng, fn, reads=(), writes=()):
        reads = [b.tok for b in reads if isinstance(b, B)]
        writes = [b.tok for b in writes]
        waits = self._waits(eng, reads, writes)
        key = self.ekey[eng]
        if self.cnt[key] >= 6000:
            key = self.ekey[eng] = self._newsem("E_%s_%d" % (eng, self.nsem))
        self.cnt[key] += 1
        v = self.cnt[key]
        self.lists[eng].append((waits, fn, key, 1))
        for t in reads:
            t.r[key] = v
        for t in writes:
            t.w = {key: v}
            t.r = {}

    def dma(self, q, out, in_, **kw):
        sbside = out if out.sb else in_
        t = sbside.tok
        if t.sem is None:
            t.sem = self.getsem()
        key = t.sem
        reads = [in_.tok]
        writes = [out.tok]
        waits = self._waits(q, reads, writes)
        self.cnt[key] += 16
        v = self.cnt[key]
        oa, ia = out.ap, in_.ap
        self.lists[q].append((waits, lambda e: e.dma_start(out=oa, in_=ia, **kw), key, 16))
        for tk in reads:
            tk.r[key] = v
        for tk in writes:
            tk.w = dict(tk.w)
            tk.w[key] = v
            tk.r = {}

    def barrier(self):
        snap = {k: v for k, v in self.cnt.items() if v > 0}
        for e in self.ENG:
            seen = self.seen[e]
            waits = []
            for k, v in snap.items():
                if e == "pe" and k.startswith("E_pe"):
                    continue
                if seen.get(k, 0) >= v:
                    continue
                seen[k] = v
                waits.append((k, v))
            if waits:
                self.lists[e].append((waits, None, None, 0))

    class _Phase:
        def __init__(self, fw):
            self.fw = fw

        def __enter__(self):
            self.st = ExitStack()
            self.st.__enter__()
            self.prev = self.fw.stack
            self.fw.stack = self.st
            self.fw.phase_sems = []
            return self

        def __exit__(self, *a):
            self.fw.barrier()
            self.fw.free_sems.extend(self.fw.phase_sems)
            self.fw.phase_sems = []
            self.fw.stack = self.prev
            return self.st.__exit__(*a)

    def phase(self):
        return FW._Phase(self)

    def finish(self):
        self.barrier()
        sems = self.sems
        lists = self.lists
        with self.nc.Block() as block:
            def run(e, lst):
                for waits, fn, key, inc in lst:
                    for k, v in waits:
                        e.wait_ge(sems[k], v)
                    if fn is not None:
                        ins = fn(e)
                        if ins is not None and key is not None:
                            ins.then_inc(sems[key], inc)

            @block.sync
            def _(e):
                run(e, lists["sp"])

            @block.tensor
            def _(e):
                run(e, lists["pe"])

            @block.scalar
            def _(e):
                run(e, lists["act"])

            @block.vector
            def _(e):
                run(e, lists["dve"])

            @block.gpsimd
            def _(e):
                run(e, lists["pool"])

    def mm(self, out, lhsT, rhs, start=True, stop=True):
        o, l, r = out.ap, lhsT.ap, rhs.ap
        self.op("pe", lambda e: e.matmul(o, l, r, start=start, stop=stop), reads=[lhsT, rhs], writes=[out])

    def act(self, out, in_, func, bias=0.0, scale=1.0):
        o, i = out.ap, in_.ap
        b = bias.ap if isinstance(bias, B) else bias
        sc = scale.ap if isinstance(scale, B) else scale
        self.op("act", lambda e: e.activation(o, i, func, bias=b, scale=sc), reads=[in_, bias, scale], writes=[out])

    def ts(self, eng, out, in0, s1, s2, op0, op1=None):
        o, i = out.ap, in0.ap
        a1 = s1.ap if isinstance(s1, B) else s1
        a2 = s2.ap if isinstance(s2, B) else s2
        if op1 is None:
            self.op(eng, lambda e: e.tensor_scalar(o, i, a1, None, op0), reads=[in0, s1], writes=[out])
        else:
            self.op(eng, lambda e: e.tensor_scalar(o, i, a1, a2, op0, op1), reads=[in0, s1, s2], writes=[out])

    def tt(self, eng, out, in0, in1, op):
        o, a, b = out.ap, in0.ap, in1.ap
        self.op(eng, lambda e: e.tensor_tensor(o, a, b, op), reads=[in0, in1], writes=[out])

    def stt(self, eng, out, in0, s, in1, op0, op1):
        o, a, b = out.ap, in0.ap, in1.ap
        sa = s.ap if isinstance(s, B) else s
        self.op(eng, lambda e: e.scalar_tensor_tensor(o, a, sa, b, op0, op1), reads=[in0, s, in1], writes=[out])

    def cp(self, eng, out, in_):
        o, i = out.ap, in_.ap
        if eng == "act":
            self.op(eng, lambda e: e.copy(o, i), reads=[in_], writes=[out])
        else:
            self.op(eng, lambda e: e.tensor_copy(o, i), reads=[in_], writes=[out])

    def memset(self, eng, out, val):
        o = out.ap
        self.op(eng, lambda e: e.memset(o, val), writes=[out])

    def recip(self, out, in_):
        o, i = out.ap, in_.ap
        self.op("dve", lambda e: e.reciprocal(o, i), reads=[in_], writes=[out])


class _Pre:
    def __init__(self, fw, pre):
        self._fw = fw
        self._pre = pre

    def __getattr__(self, k):
        return getattr(self._fw, k)

    def dram(self, name, shape, dt, kind="Internal"):
        return self._fw.dram(self._pre + name, shape, dt, kind)


def conv_weight(fw, src, dst, K, N, eng_cycle=("dve", "pool", "act")):
    kc = K // 128
    with fw.phase():
        CB = 2048
        fr = fw.ring("wcf", 2, [128, CB], F32)
        br = fw.ring("wcb", 2, [128, CB], BF16)
        i = 0
        for k in range(kc):
            for c0 in range(0, N, CB):
                n = min(CB, N - c0)
                f = fr.next()
                b = br.next()
                fw.dma("sp", f[:, :n], src[k * 128:(k + 1) * 128, c0:c0 + n])
                fw.cp(eng_cycle[i % len(eng_cycle)], b[:, :n], f[:, :n])
                fw.dma("pool", dst[:, k, c0:c0 + n], b[:, :n])
                i += 1


def mod_phase(fw, cc, wmod, bmodT, modT):
    with fw.phase():
        ccs = fw.sb("ccs", [128, 8, 2], F32)
        bm = fw.sb("bm", [128, 48], F32)
        fw.dma("sp", ccs, cc)
        fw.dma("sp", bm, bmodT)
        fw.act(ccs, ccs, AF.Silu)
        modp = fw.ps("modp", [128, 48, 2], F32)
        wr = fw.ring("wm", 2, [128, 8, 1024], F32)
        wv = wmod.re("(k p) o -> p k o", p=128)
        for blk in range(6):
            w = wr.next()
            fw.dma("sp", w, wv[:, :, blk * 1024:(blk + 1) * 1024])
            for o8 in range(8):
                oc = blk * 8 + o8
                for k in range(8):
                    fw.mm(modp[:, oc, :], w[:, k, o8 * 128:(o8 + 1) * 128], ccs[:, k, :], k == 0, k == 7)
        for j in range(2):
            fw.tt("dve", modT[:, :, j], modp[:, :, j], bm, ALU.add)


def layer_norm_fm(fw, r, n, onesm, pstat, tmp, outs):
    sq, mean, rstd, d = tmp
    fw.act(sq[:, :, :n], r[:, :, :n], AF.Square)
    pm = pstat[:, 0, :n]
    pq = pstat[:, 1, :n]
    for oc in range(8):
        fw.mm(pm, onesm, r[:, oc, :n], oc == 0, oc == 7)
    for oc in range(8):
        fw.mm(pq, onesm, sq[:, oc, :n], oc == 0, oc == 7)
    fw.cp("act", mean[:, :n], pm)
    fw.tt("dve", rstd[:, :n], mean[:, :n], mean[:, :n], ALU.mult)
    fw.tt("dve", rstd[:, :n], pq, rstd[:, :n], ALU.subtract)
    fw.act(rstd[:, :n], rstd[:, :n], AF.Ln, bias=fw.const(LN_EPS), scale=1.0)
    fw.act(rstd[:, :n], rstd[:, :n], AF.Exp, scale=-0.5)
    for oc in range(8):
        fw.tt("dve", d[:, oc, :n], r[:, oc, :n], mean[:, :n], ALU.subtract)
        fw.tt("pool", d[:, oc, :n], d[:, oc, :n], rstd[:, :n], ALU.mult)
        for (dst, gf, bf, eng) in outs:
            if eng == "pool":
                fw.act(dst[:, oc, :n], d[:, oc, :n], AF.Identity, bias=bf(oc), scale=gf(oc))
            else:
                fw.ts(eng, dst[:, oc, :n], d[:, oc, :n], gf(oc), bf(oc), ALU.mult, ALU.add)


def post_phase(fw, mixT, KH, resT, wo_b, wfi_b, wfo_b, modT, lnT, outT, chunks, mixload=None, ydirect=None):
    with fw.phase():
        N = 256
        if ydirect is None:
            wo = fw.sb("wo", [128, KH, 1024], BF16)
        wfi = fw.sb("wfi", [128, 8, 2 * DFF], BF16)
        wfor = fw.ring("wfo", 3, [128, 1024], BF16)
        if ydirect is None:
            fw.dma("sp", wo, wo_b)
        for k in range(8):
            fw.dma("sp", wfi[:, k, :], wfi_b[:, k, :])
        onesm = fw.sb("onesm", [128, 128], F32)
        fw.memset("dve", onesm, 1.0 / 1024)
        G2 = fw.sb("G2", [128, 8, 2], F32)
        B2 = fw.sb("B2", [128, 8, 2], F32)
        for j in range(2):
            fw.ts("dve", G2[:, :, j], modT[:, 32:40, j], 1.0, None, ALU.add)
            fw.tt("dve", B2[:, :, j], G2[:, :, j], lnT[:, 1, :], ALU.mult)
            fw.tt("dve", B2[:, :, j], B2[:, :, j], modT[:, 24:32, j], ALU.add)
            fw.tt("dve", G2[:, :, j], G2[:, :, j], lnT[:, 0, :], ALU.mult)
        mixv = mixT.re("(h p) t -> p h t", p=128) if (mixload is None and ydirect is None) else None
        ydv = [y_.re("(k p) t -> p k t", p=128) for y_ in ydirect] if ydirect is not None else None
        resv = resT.re("(k p) t -> p k t", p=128)
        outv = outT.re("(k p) t -> p k t", p=128)
        mr = fw.ring("mx", 2, [128, KH, N], BF16) if ydirect is None else fw.ring("yd", 2, [128, 8, N], F32)
        xr = fw.ring("xs", 1, [128, 8, N], F32)
        rr = fw.ring("r", 1, [128, 8, N], F32)
        h1r = fw.ring("h1", 1, [128, 8, N], F32)
        u2r = fw.ring("u2", 1, [128, 8, N], BF16)
        dd = fw.sb("dd", [128, 8, N], F32)
        sq = dd
        mean = fw.sb("mean", [128, N], F32)
        rstd = fw.sb("rstd", [128, N], F32)
        sgr = fw.ring("sg", 3, [128, N], F32)
        ar = fw.ring("a", 3, [128, N], BF16)
        pfa = fw.ps("pfa", [128, 8, N], F32)
        pgu = fw.ring("pgu", 2, [128, 2, N], F32, ps=True)
        py = fw.ps("py", [128, 2, N], F32)
        pstat = fw.ps("pst", [128, 2, N], F32)
        tmp = (sq, mean, rstd, dd)
        for (t0, n, j) in chunks:
            ms = mr.next()
            xs = xr.next()
            if ydirect is not None:
                for i4 in range(4):
                    fw.dma("sp", ms[:, 2 * i4:2 * i4 + 2, :n], ydv[i4][:, :, t0:t0 + n])
            elif mixload is None:
                fw.dma("sp", ms[:, :, :n], mixv[:, :, t0:t0 + n])
            else:
                mixload(ms, t0, n)
            fw.dma("sp", xs[:, :, :n], resv[:, :, t0:t0 + n])
            fw.act(xs[:, :, :n], xs[:, :, :n], AF.Identity, scale=ALPHA)
            r = rr.next()
            for oc in range(8):
                if ydirect is not None:
                    p = ms[:, oc, :n]
                else:
                    p = py[:, oc % 2, :n]
                    for h in range(KH):
                        fw.mm(p, wo[:, h, oc * 128:(oc + 1) * 128], ms[:, h, :n], h == 0, h == KH - 1)
                fw.stt("dve", r[:, oc, :n], p, modT[:, 16 + oc, j:j + 1], xs[:, oc, :n], ALU.mult, ALU.add)
            h1 = h1r.next()
            u2 = u2r.next()
            layer_norm_fm(fw, r, n, onesm, pstat, tmp, [
                (h1, lambda oc: lnT[:, 0, oc:oc + 1], lambda oc: lnT[:, 1, oc:oc + 1], "dve"),
                (u2, lambda oc: G2[:, oc, j:j + 1], lambda oc: B2[:, oc, j:j + 1], "pool"),
            ])
            def gateup(m):
                pp = pgu.next()
                for k in range(8):
                    fw.mm(pp[:, 0, :n], wfi[:, k, m * 128:(m + 1) * 128], u2[:, k, :n], k == 0, k == 7)
                for k in range(8):
                    fw.mm(pp[:, 1, :n], wfi[:, k, DFF + m * 128:DFF + (m + 1) * 128], u2[:, k, :n], k == 0, k == 7)
                sg = sgr.next()
                a = ar.next()
                fw.act(sg[:, :n], pp[:, 0, :n], AF.Silu)
                fw.tt("dve", a[:, :n], sg[:, :n], pp[:, 1, :n], ALU.mult)
                return a

            def down(m, a):
                wf = wfor.next()
                fw.dma("sp", wf, wfo_b[:, m, :])
                for oc in range(8):
                    fw.mm(pfa[:, oc, :n], wf[:, oc * 128:(oc + 1) * 128], a[:, :n], m == 0 and oc % 2 == 0, m == 21)

            prev = gateup(0)
            for m in range(1, 22):
                cur = gateup(m)
                down(m - 1, prev)
                prev = cur
            down(21, prev)
            fw.act(h1[:, :, :n], h1[:, :, :n], AF.Identity, scale=ALPHA)
            r2 = rr.next()
            for oc in range(8):
                fw.stt("dve", r2[:, oc, :n], pfa[:, oc, :n], modT[:, 40 + oc, j:j + 1], h1[:, oc, :n], ALU.mult, ALU.add)
            h2 = xs
            layer_norm_fm(fw, r2, n, onesm, pstat, tmp, [
                (h2, lambda oc: lnT[:, 2, oc:oc + 1], lambda oc: lnT[:, 3, oc:oc + 1], "dve"),
            ])
            fw.dma("pool", outv[:, :, t0:t0 + n], h2[:, :, :n])


def build_l0(stop_after=None, dbg=False):
    nc = bass.Bass("TRN2", target_bir_lowering=False)
    with ExitStack() as st:
        fw = FW(nc, st)
        emit_l0(fw, stop_after=stop_after, dbg=dbg)
        fw.finish()
    return nc


def emit_l0(fw, pre="", outT=None, stop_after=None, dbg=False):
    if True:
        _d = fw.dram
        fw = _Pre(fw, pre)
        EI = "ExternalInput"
        SK = "ExternalOutput" if dbg else "Internal"
        xT = fw.dram("xT", [D, NT], F32, EI)
        xTo = fw.dram("xTo", [D, NO], F32, EI)
        cc = fw.dram("cc", [128, 8, 2], F32, EI)
        wmod = fw.dram("wmod", [D, 6 * D], F32, EI)
        bmodT = fw.dram("bmodT", [128, 48], F32, EI)
        lnTd = fw.dram("lnT", [128, 4, 8], F32, EI)
        w_in = fw.dram("w_in", [D, 896], F32, EI)
        qnT = fw.dram("qnT", [128, 4], F32, EI)
        kvnT = fw.dram("kvnT", [128, 2], F32, EI)
        w_q = fw.dram("w_q", [512, 3072], F32, EI)
        w_kv = fw.dram("w_kv", [256, 2048], F32, EI)
        w_o = fw.dram("w_o", [D, D], F32, EI)
        w_fi = fw.dram("w_fi", [D, 2 * DFF], F32, EI)
        w_fo = fw.dram("w_fo", [DFF, D], F32, EI)
        ropeK = fw.dram("ropeK", [64, 2, NT], F32, EI)
        ropeQ = fw.dram("ropeQ", [128, 2, NO], F32, EI)
        if outT is None:
            outT = fw.dram("outT", [D, NO], F32, "ExternalOutput")
        w_in_b = fw.dram("w_in_b", [128, 8, 896], BF16)
        w_q_b = fw.dram("w_q_b", [128, 4, 3072], BF16)
        w_kv_b = fw.dram("w_kv_b", [128, 2, 2048], BF16)
        w_o_b = fw.dram("w_o_b", [128, 8, D], BF16)
        w_fi_b = fw.dram("w_fi_b", [128, 8, 2 * DFF], BF16)
        w_fo_b = fw.dram("w_fo_b", [128, 22, D], BF16)
        KT = fw.dram("KT", [8, 128, NT], BF16, SK)
        VV = fw.dram("VV", [8, 128, NT // 128, 128], BF16, SK)
        KR = fw.dram("KR", [64, NT], BF16, SK)
        QN = fw.dram("QN", [8, 128, NO], BF16, SK)
        QR = fw.dram("QR", [8, 128, NO], BF16, SK)
        OT = fw.dram("OT", [D, NO], BF16, SK)

        modT = fw.sb("modT", [128, 48, 2], F32)
        lnT = fw.sb("lnTs", [128, 4, 8], F32)
        sc0 = fw.sb("sc0", [128, 8, 2], F32)
        fw.dma("sp", lnT, lnTd)
        fw.const(LN_EPS)
        fw.const(RMS_EPS)

        mod_phase(fw, cc, wmod, bmodT, modT)
        for j in range(2):
            fw.ts("dve", sc0[:, :, j], modT[:, 8:16, j], 1.0, None, ALU.add)
        conv_weight(fw, w_in, w_in_b, D, 896)
        conv_weight(fw, w_q, w_q_b, 512, 3072)
        conv_weight(fw, w_kv, w_kv_b, 256, 2048)
        conv_weight(fw, w_o, w_o_b, D, D)
        conv_weight(fw, w_fi, w_fi_b, D, 2 * DFF)
        conv_weight(fw, w_fo, w_fo_b, DFF, D)

        with fw.phase():
            win = fw.sb("win", [128, 8, 384], BF16)
            wkv = fw.sb("wkv", [128, 2, 2048], BF16)
            fw.dma("sp", win, w_in_b[:, :, 512:896])
            fw.dma("sp", wkv, w_kv_b)
            kvn = fw.sb("kvn", [128, 2], F32)
            fw.dma("sp", kvn, kvnT)
            ones = fw.sb("ones", [128, 128], BF16)
            fw.memset("dve", ones, 1.0)
            xr = fw.ring("xr", 2, [128, 8, 512], F32)
            ur = fw.ring("ur", 2, [128, 8, 512], BF16)
            cr = fw.ring("ckv", 2, [128, 2, 512], BF16)
            sqb = fw.sb("sqb", [128, 2, 512], BF16)
            rs = fw.sb("rs", [128, 512], F32)
            tr = fw.ring("tbl", 2, [64, 2, 512], F32)
            t1 = fw.sb("t1", [64, 512], F32)
            t2 = fw.sb("t2", [64, 512], F32)
            krr = fw.ring("krs", 2, [64, 512], BF16)
            ksr = fw.ring("kst", 3, [128, 512], BF16)
            vsr = fw.ring("vst", 3, [128, 512], BF16)
            pl = fw.ps("pl", [128, 2, 512], F32)
            pr = fw.ps("pr", [64, 2, 512], F32)
            pss = fw.ps("pss", [128, 512], F32)
            pkv = fw.ring("pkv", 3, [128, 512], F32, ps=True)
            xv = xT.re("(k p) t -> p k t", p=128)
            chunks = [(0, 256, 1)] + [(256 + 512 * i, 512, 0) for i in range(32)]
            ei = 0
            for (t0, n, j) in chunks:
                xs = xr.next()
                fw.dma("sp", xs[:, :, :n], xv[:, :, t0:t0 + n])
                tb = tr.next()
                fw.dma("sp", tb[:, :, :n], ropeK[:, :, t0:t0 + n])
                us = ur.next()
                for k in range(8):
                    fw.ts("dve" if k % 2 == 0 else "pool", us[:, k, :n], xs[:, k, :n], sc0[:, k, j:j + 1],
                          modT[:, k, j:j + 1], ALU.mult, ALU.add)
                for m in range(2):
                    for k in range(8):
                        fw.mm(pl[:, m, :n], win[:, k, m * 128:(m + 1) * 128], us[:, k, :n], k == 0, k == 7)
                for m in range(2):
                    for k in range(8):
                        fw.mm(pr[:, m, :n], win[:, k, 256 + m * 64:256 + (m + 1) * 64], us[:, k, :n], k == 0, k == 7)
                for m in range(2):
                    fw.act(sqb[:, m, :n], pl[:, m, :n], AF.Square)
                for m in range(2):
                    fw.mm(pss[:, :n], ones, sqb[:, m, :n], m == 0, m == 1)
                fw.act(rs[:, :n], pss[:, :n], AF.Ln, bias=fw.const(RMS_EPS), scale=1.0 / 256)
                fw.act(rs[:, :n], rs[:, :n], AF.Exp, scale=-0.5)
                cs = cr.next()
                for m in range(2):
                    fw.stt("dve", cs[:, m, :n], pl[:, m, :n], kvn[:, m:m + 1], rs[:, :n], ALU.mult, ALU.mult)
                fw.tt("dve", t1[:, :n], pr[:, 0, :n], tb[:, 0, :n], ALU.mult)
                fw.tt("dve", t2[:, :n], pr[:, 1, :n], tb[:, 1, :n], ALU.mult)
                krs = krr.next()
                fw.tt("pool", krs[:, :n], t1[:, :n], t2[:, :n], ALU.add)
                fw.dma("pool", KR[:, t0:t0 + n], krs[:, :n])
                for h in range(8):
                    pk = pkv.next()
                    for m in range(2):
                        fw.mm(pk[:, :n], wkv[:, m, h * 256:h * 256 + 128], cs[:, m, :n], m == 0, m == 1)
                    ks = ksr.next()
                    fw.cp("act" if ei % 2 == 0 else "dve", ks[:, :n], pk[:, :n])
                    ei += 1
                    fw.dma("pool", KT[h][:, t0:t0 + n], ks[:, :n])
                    pv = pkv.next()
                    for tq in range(n // 128):
                        for m in range(2):
                            fw.mm(pv[:, tq * 128:(tq + 1) * 128], cs[:, m, tq * 128:(tq + 1) * 128],
                                  wkv[:, m, h * 256 + 128:h * 256 + 256], m == 0, m == 1)
                    vs = vsr.next()
                    fw.cp("act" if ei % 2 == 0 else "dve", vs[:, :n], pv[:, :n])
                    ei += 1
                    fw.dma("pool", VV[h][:, t0 // 128:(t0 + n) // 128, :], vs[:, :n].re("p (t d) -> p t d", d=128))
        if stop_after == "P1":
            return

        with fw.phase():
            win = fw.sb("winq", [128, 8, 512], BF16)
            wq = fw.sb("wq", [128, 4, 3072], BF16)
            fw.dma("sp", win, w_in_b[:, :, 0:512])
            fw.dma("sp", wq, w_q_b)
            qn = fw.sb("qn", [128, 4], F32)
            fw.dma("sp", qn, qnT)
            ones = fw.sb("ones", [128, 128], BF16)
            fw.memset("dve", ones, 1.0)
            xr = fw.ring("xr", 2, [128, 8, 512], F32)
            ur = fw.ring("ur", 2, [128, 8, 512], BF16)
            cr = fw.ring("cq", 2, [128, 4, 512], BF16)
            sqb = fw.sb("sqb", [128, 4, 512], BF16)
            rs = fw.sb("rs", [128, 512], F32)
            tr = fw.ring("tbl", 2, [128, 2, 512], F32)
            t1 = fw.sb("t1", [128, 512], F32)
            t2 = fw.sb("t2", [128, 512], F32)
            qnr = fw.ring("qns", 3, [128, 512], BF16)
            qrr = fw.ring("qrs", 3, [128, 512], BF16)
            pq = fw.ps("pq", [128, 4, 512], F32)
            pss = fw.ps("pss", [128, 512], F32)
            pqo = fw.ring("pqo", 3, [128, 512], F32, ps=True)
            xv = xTo.re("(k p) t -> p k t", p=128)
            chunks = [(0, 256, 1)] + [(256 + 512 * i, 512, 0) for i in range(8)]
            ei = 0
            for (t0, n, j) in chunks:
                xs = xr.next()
                fw.dma("sp", xs[:, :, :n], xv[:, :, t0:t0 + n])
                tb = tr.next()
                fw.dma("sp", tb[:, :, :n], ropeQ[:, :, t0:t0 + n])
                us = ur.next()
                for k in range(8):
                    fw.ts("dve" if k % 2 == 0 else "pool", us[:, k, :n], xs[:, k, :n], sc0[:, k, j:j + 1],
                          modT[:, k, j:j + 1], ALU.mult, ALU.add)
                for m in range(4):
                    for k in range(8):
                        fw.mm(pq[:, m, :n], win[:, k, m * 128:(m + 1) * 128], us[:, k, :n], k == 0, k == 7)
                for m in range(4):
                    fw.act(sqb[:, m, :n], pq[:, m, :n], AF.Square)
                for m in range(4):
                    fw.mm(pss[:, :n], ones, sqb[:, m, :n], m == 0, m == 3)
                fw.act(rs[:, :n], pss[:, :n], AF.Ln, bias=fw.const(RMS_EPS), scale=1.0 / 512)
                fw.act(rs[:, :n], rs[:, :n], AF.Exp, scale=-0.5)
                cs = cr.next()
                for m in range(4):
                    fw.stt("dve", cs[:, m, :n], pq[:, m, :n], qn[:, m:m + 1], rs[:, :n], ALU.mult, ALU.mult)
                for h in range(8):
                    pn = pqo.next()
                    for k in range(4):
                        fw.mm(pn[:, :n], wq[:, k, h * 384:h * 384 + 128], cs[:, k, :n], k == 0, k == 3)
                    qs = qnr.next()
                    fw.cp("act", qs[:, :n], pn[:, :n])
                    fw.dma("pool", QN[h][:, t0:t0 + n], qs[:, :n])
                    pa = pqo.next()
                    for k in range(4):
                        fw.mm(pa[:, :n], wq[:, k, h * 384 + 128:h * 384 + 256], cs[:, k, :n], k == 0, k == 3)
                    pb = pqo.next()
                    for k in range(4):
                        fw.mm(pb[:, :n], wq[:, k, h * 384 + 256:h * 384 + 384], cs[:, k, :n], k == 0, k == 3)
                    fw.tt("dve", t1[:, :n], pa[:, :n], tb[:, 0, :n], ALU.mult)
                    fw.tt("dve", t2[:, :n], pb[:, :n], tb[:, 1, :n], ALU.mult)
                    qr_ = qrr.next()
                    fw.tt("pool", qr_[:, :n], t1[:, :n], t2[:, :n], ALU.add)
                    fw.dma("pool", QR[h][:, t0:t0 + n], qr_[:, :n])
        if stop_after == "P1b":
            return

        with fw.phase():
            NKT = NT // 128
            HALF = NKT // 2
            krp = fw.sb("krp", [128, HALF * 128], BF16)
            fw.dma("sp", krp[0:64, :], KR[:, 0:HALF * 128])
            fw.dma("sp", krp[64:128, :], KR[:, HALF * 128:NT])
            onesf = fw.sb("onesf", [128, 128], F32)
            fw.memset("dve", onesf, 1.0)
            kring = fw.ring("kb", 2, [128, NT], BF16)
            vring = fw.ring("vb", 2, [128, NKT, 128], BF16)
            qnr = fw.ring("qnb", 2, [128, 512], BF16)
            qrr = fw.ring("qrb", 2, [128, 512], BF16)
            ptr = fw.ring("pt", 7, [128, 512], BF16)
            acc0r = fw.ring("acc0", 2, [128, 512], F32)
            acc1r = fw.ring("acc1", 2, [128, 512], F32)
            rec = fw.sb("rec", [128, 512], F32)
            otr = fw.ring("ots", 2, [128, 512], BF16)
            pss_ = fw.ring("ps_s", 5, [128, 512], F32, ps=True)
            pso = fw.ring("ps_o", 2, [128, 512], F32, ps=True)
            psum_ = fw.ps("ps_sum", [128, 512], F32)
            qchunks = [(0, 256, 2)] + [(256 + 512 * i, 512, NKT) for i in range(8)]
            for h in range(8):
                kb = kring.next()
                vb = vring.next()
                for c in range(5):
                    fw.dma("sp", kb[:, c * 3328:(c + 1) * 3328], KT[h][:, c * 3328:(c + 1) * 3328])
                for c in range(5):
                    fw.dma("sp", vb[:, c * 26:(c + 1) * 26, :], VV[h][:, c * 26:(c + 1) * 26, :])
                for (t0, n, nk) in qchunks:
                    qnb = qnr.next()
                    qrb = qrr.next()
                    fw.dma("sp", qnb[:, :n], QN[h][:, t0:t0 + n])
                    fw.dma("sp", qrb[:, :n], QR[h][:, t0:t0 + n])
                    po = pso.next()
                    acc0 = acc0r.next()
                    acc1 = acc1r.next()

                    if nk == NKT:
                        units = [[i_, i_ + HALF] for i_ in range(HALF)]
                    else:
                        units = [[i_] for i_ in range(nk)]
                    order = [t_ for u_ in units for t_ in u_]
                    first_t, last_t = order[0], order[-1]
                    cnt = {"dve": 0, "pool": 0, "i": 0}

                    def qk_unit(unit):
                        pss = [pss_.next() for _ in unit]
                        for ps, jt in zip(pss, unit):
                            fw.mm(ps[:, :n], kb[:, jt * 128:(jt + 1) * 128], qnb[:, :n], True, False)
                        for ps, jt in zip(pss, unit):
                            hf, jj = jt // HALF, jt % HALF
                            fw.mm(ps[:, :n], krp[hf * 64:(hf + 1) * 64, jj * 128:(jj + 1) * 128],
                                  qrb[hf * 64:(hf + 1) * 64, :n], False, True)
                        res = []
                        for ps, jt in zip(pss, unit):
                            p = ptr.next()
                            fw.act(p[:, :n], ps[:, :n], AF.Exp, scale=MLA_SCALE)
                            idx = cnt["i"]
                            cnt["i"] += 1
                            eng, acc = ("pool", acc1) if (idx % 4 == 3 and nk >= 4) else ("dve", acc0)
                            if cnt[eng] == 0:
                                fw.cp(eng, acc[:, :n], p[:, :n])
                            else:
                                fw.tt(eng, acc[:, :n], acc[:, :n], p[:, :n], ALU.add)
                            cnt[eng] += 1
                            res.append((jt, p))
                        return res

                    def pv(jt, p):
                        fw.mm(po[:, :n], vb[:, jt, :], p[:, :n], jt == first_t, jt == last_t)

                    LA = 1 if nk == NKT else 3
                    pend = []
                    for unit in units:
                        pend.append(qk_unit(unit))
                        if len(pend) > LA:
                            for (j0_, p0_) in pend.pop(0):
                                pv(j0_, p0_)
                    for res_ in pend:
                        for (j0_, p0_) in res_:
                            pv(j0_, p0_)
                    fw.mm(psum_[:, :n], onesf, acc0[:, :n], True, nk < 4)
                    if nk >= 4:
                        fw.mm(psum_[:, :n], onesf, acc1[:, :n], False, True)
                    fw.recip(rec[:, :n], psum_[:, :n])
                    ots = otr.next()
                    fw.tt("dve", ots[:, :n], po[:, :n], rec[:, :n], ALU.mult)
                    fw.dma("pool", OT[h * 128:(h + 1) * 128, t0:t0 + n], ots[:, :n])
        if stop_after == "P2":
            return

        chunks = [(0, 256, 1)] + [(256 + 256 * i, 256, 0) for i in range(16)]
        post_phase(fw, OT, 8, xTo, w_o_b, w_fi_b, w_fo_b, modT, lnT, outT, chunks)


def _fm(v, nchunk):
    return np.ascontiguousarray(np.asarray(v, np.float32).reshape(nchunk, 128).T)


def _rope_tables():
    rows = SEQ // 64
    row = np.repeat(np.arange(rows, dtype=np.float32), 64)
    col = np.tile(np.arange(64, dtype=np.float32), rows)
    inv = (np.float32(10000.0) ** (-(2.0 * np.arange(16, dtype=np.float32)) / np.float32(32))).astype(np.float32)
    ang = np.concatenate([row[:, None] * inv, col[:, None] * inv], axis=-1).astype(np.float32)
    cos, sin = np.cos(ang).astype(np.float32), np.sin(ang).astype(np.float32)
    r = np.arange(64)
    a, half, f = r // 32, (r % 32) // 16, r % 16
    cosT = cos[:, a * 16 + f].T
    sinT = (sin[:, a * 16 + f] * np.where(half == 0, -1.0, 1.0).astype(np.float32)).T
    tab = np.zeros((64, 2, NT), np.float32)
    tab[:, 0, :CTX] = 1.0
    tab[:, 0, CTX:] = cosT
    tab[:, 1, CTX:] = sinT
    return tab


_SWAP = np.array([(r // 32) * 32 + (1 - (r % 32) // 16) * 16 + r % 16 for r in range(64)])


def prep_l0(inp, b, qtr, tab):
    x, c, ctx, c_ctx = inp["x"], inp["c"], inp["ctx"], inp["c_ctx"]
    allx = np.concatenate([ctx[b], x[b]], axis=0)
    own = np.concatenate([np.arange(CTX), CTX + qtr * NQ + np.arange(NQ)])
    w_in = inp["mla_w_in"][0]
    w_in_ext = np.concatenate([w_in, w_in[:, 768 + _SWAP]], axis=1)
    wq = inp["mla_w_q_up"][0].reshape(512, 8, 192)
    rope_cols = wq[:, :, 128:]
    wq_ext = np.concatenate([wq[:, :, :128], rope_cols, rope_cols, rope_cols[:, :, _SWAP], rope_cols[:, :, _SWAP]], axis=2)
    tq = tab[:, :, own]
    return {
        "xT": np.ascontiguousarray(allx.T),
        "xTo": np.ascontiguousarray(allx[own].T),
        "cc": np.ascontiguousarray(np.stack([_fm(c[b], 8), _fm(c_ctx, 8)], axis=-1)),
        "wmod": np.ascontiguousarray(inp["w_mod"][0]),
        "bmodT": _fm(inp["b_mod"][0], 48),
        "lnT": np.ascontiguousarray(np.stack([_fm(inp[k][0], 8) for k in ("ln1_g", "ln1_b", "ln2_g", "ln2_b")], axis=1)),
        "w_in": np.ascontiguousarray(w_in_ext),
        "qnT": _fm(inp["mla_q_norm"][0], 4),
        "kvnT": _fm(inp["mla_kv_norm"][0], 2),
        "w_q": np.ascontiguousarray(wq_ext.reshape(512, 3072)),
        "w_kv": np.ascontiguousarray(inp["mla_w_kv_up"][0]),
        "w_o": np.ascontiguousarray(inp["mla_w_out"][0]),
        "w_fi": np.ascontiguousarray(inp["w_ffn_in"][0]),
        "w_fo": np.ascontiguousarray(inp["w_ffn_out"][0]),
        "ropeK": tab,
        "ropeQ": np.ascontiguousarray(np.concatenate([tq, tq], axis=0)),
    }


NTP = NT + 8
NEG = -30000.0


def _pc(t):
    return t + 2 if t < CTX else t + 6


def build_l1(stop_after=None, dbg=False, nlat=SEQ):
    nc = bass.Bass("TRN2", target_bir_lowering=False)
    with ExitStack() as st:
        fw = FW(nc, st)
        emit_l1(fw, stop_after=stop_after, dbg=dbg, nlat=nlat)
        fw.finish()
    return nc


def emit_l1(fw, pre="", hsrc=None, ydst=None, stop_after=None, dbg=False, nlat=SEQ, pdst=None):
    NT = CTX + nlat
    NTP = NT + 8
    if True:
        fw = _Pre(fw, pre)
        EI = "ExternalInput"
        SK = "ExternalOutput" if dbg else "Internal"
        if hsrc is None:
            hT = fw.dram("hT", [D, NT], F32, EI)
            xv_ = hT.re("(k p) t -> p k t", p=128)
            hsrc = lambda t0, n: xv_[:, :, t0:t0 + n]
        cc = fw.dram("cc", [128, 8, 2], F32, EI)
        wmod = fw.dram("wmod", [D, 6 * D], F32, EI)
        bmodT = fw.dram("bmodT", [128, 48], F32, EI)
        w_g = fw.dram("w_g", [D, 1552], F32, EI)
        convT = fw.dram("convT", [128, 8, 5], F32, EI)
        abT = fw.dram("abT", [16, 2], F32, EI)
        normT = fw.dram("normT", [128, 1], F32, EI)
        cst = fw.dram("cst", [128, 384], F32, EI)
        if ydst is None and pdst is None:
            yT = fw.dram("yT", [512, NT], BF16, "ExternalOutput")
            ydst = lambda hv, t0, n: yT[hv * 128:(hv + 1) * 128, t0:t0 + n]
        if pdst is not None:
            w_op = fw.dram("w_op", [512, D], F32, EI)
            w_op_b = fw.dram("w_op_b", [128, 4, D], BF16)
        w_g_b = fw.dram("w_g_b", [128, 8, 1552], BF16)
        PR = fw.dram("PR", [8, 128, NTP], BF16)
        ZS = fw.dram("ZS", [4, 128, NT], BF16, SK)
        BG = fw.dram("BG", [16, 2, NT], F32, SK)
        QKV = fw.dram("QKV", [8, 128, NT], BF16, SK)
        OD = fw.dram("OD", [2, 4, 128, NT], F32, SK)

        modT = fw.sb("modT", [128, 48, 2], F32)
        sc0 = fw.sb("sc0", [128, 8, 2], F32)
        cs = fw.sb("cst", [128, 384], F32)
        fw.dma("sp", cs, cst)
        identf = cs[:, 0:128]
        fw.const(RMS_EPS)
        fw.const(1.0)
        identb = fw.sb("identb", [128, 128], BF16)
        fw.cp("dve", identb, identf)
        onesf = fw.sb("onesf", [128, 128], F32)
        fw.memset("dve", onesf, 1.0)

        mod_phase(fw, cc, wmod, bmodT, modT)
        for j in range(2):
            fw.ts("dve", sc0[:, :, j], modT[:, 8:16, j], 1.0, None, ALU.add)
        conv_weight(fw, w_g, w_g_b, D, 1552)
        if pdst is not None:
            conv_weight(fw, w_op, w_op_b, 512, D)

        chunks = [(0, 256, 1)] + [(256 + 512 * i, 512, 0) for i in range(nlat // 512)]
        with fw.phase():
            wg = fw.sb("wg", [128, 8, 1552], BF16)
            fw.dma("sp", wg, w_g_b)
            ab = fw.sb("ab", [16, 2], F32)
            fw.dma("sp", ab, abT)
            negA = fw.sb("negA", [16, 1], F32)
            fw.act(negA, ab[:, 1:2], AF.Exp)
            fw.ts("dve", negA, negA, -1.0, None, ALU.mult)
            zt = fw.sb("zt", [128, 4], BF16)
            fw.memset("dve", zt, 0.0)
            for ch in range(8):
                fw.dma("pool", PR[ch][:, 0:2], zt[:, 0:2])
                fw.dma("pool", PR[ch][:, 258:262], zt[:, 0:4])
                fw.dma("pool", PR[ch][:, NTP - 2:NTP], zt[:, 0:2])
            xr = fw.ring("xr", 2, [128, 8, 512], F32)
            ur = fw.ring("ur", 2, [128, 8, 512], BF16)
            sr = fw.ring("st", 3, [128, 512], BF16)
            zr = fw.ring("zs", 3, [128, 512], BF16)
            bgs = fw.ring("bgs", 2, [16, 2, 512], F32)
            et = fw.sb("et", [16, 512], F32)
            pp = fw.ring("pp", 6, [128, 512], F32, ps=True)
            pba = fw.ps("pba", [16, 512], F32)
            ei = 0
            for (t0, n, j) in chunks:
                xs = xr.next()
                src_ = hsrc(t0, n)
                if isinstance(src_, list):
                    for k in range(8):
                        fw.dma("sp", xs[:, k, :n], src_[k])
                else:
                    fw.dma("sp", xs[:, :, :n], src_)
                us = ur.next()
                for k in range(8):
                    fw.ts("dve" if k % 2 == 0 else "pool", us[:, k, :n], xs[:, k, :n], sc0[:, k, j:j + 1],
                          modT[:, k, j:j + 1], ALU.mult, ALU.add)
                for mt in range(12):
                    p = pp.next()
                    for k in range(8):
                        fw.mm(p[:, :n], wg[:, k, mt * 128:(mt + 1) * 128], us[:, k, :n], k == 0, k == 7)
                    if mt < 8:
                        s = sr.next()
                        fw.cp("act" if ei % 2 == 0 else "dve", s[:, :n], p[:, :n])
                        ei += 1
                        fw.dma("pool", PR[mt][:, _pc(t0):_pc(t0) + n], s[:, :n])
                    else:
                        z = zr.next()
                        fw.act(z[:, :n], p[:, :n], AF.Silu)
                        fw.dma("pool", ZS[mt - 8][:, t0:t0 + n], z[:, :n])
                for k in range(8):
                    fw.mm(pba[:, :n], wg[:, k, 1536:1552], us[:, k, :n], k == 0, k == 7)
                bg = bgs.next()
                fw.act(bg[:, 0, :n], pba[:, :n], AF.Sigmoid)
                fw.act(et[:, :n], pba[:, :n], AF.Exp, bias=ab[:, 0:1])
                fw.act(et[:, :n], et[:, :n], AF.Ln, bias=fw.const(1.0)[0:16, :])
                fw.ts("dve", bg[:, 1, :n], et[:, :n], negA, None, ALU.mult)
                fw.dma("pool", BG[:, :, t0:t0 + n], bg[:, :, :n])
        with fw.phase():
            cw = fw.sb("cw", [128, 8, 5], F32)
            fw.dma("sp", cw, convT)
            dg = fw.sb("dg", [128, 40, 128], BF16)
            for ch in range(8):
                for jj in range(5):
                    fw.ts("dve" if (ch + jj) % 2 == 0 else "pool", dg[:, ch * 5 + jj, :], identb, cw[:, ch, jj:jj + 1], None, ALU.mult)
            pr = fw.ring("prr", 4, [128, 516], BF16)
            a1 = fw.ring("a1", 3, [128, 512], F32)
            sq = fw.sb("sq", [128, 512], F32)
            rs = fw.sb("rs", [128, 512], F32)
            ob = fw.ring("ob", 3, [128, 512], BF16)
            pss = fw.ring("pss", 2, [128, 512], F32, ps=True)
            pcv = fw.ring("pcv", 3, [128, 512], F32, ps=True)
            for (t0, n, j) in chunks:
                for ch in range(8):
                    x = pr.next()
                    fw.dma("sp", x[:, :n + 4], PR[ch][:, _pc(t0) - 2:_pc(t0) + n + 2])
                    pc_ = pcv.next()
                    for jj in range(5):
                        fw.mm(pc_[:, :n], dg[:, ch * 5 + jj, :], x[:, jj:jj + n], jj == 0, jj == 4)
                    s = a1.next()
                    fw.act(s[:, :n], pc_[:, :n], AF.Silu)
                    o = ob.next()
                    if ch < 4:
                        fw.tt("pool", sq[:, :n], s[:, :n], s[:, :n], ALU.mult)
                        ps_ = pss.next()
                        fw.mm(ps_[:, :n], onesf, sq[:, :n], True, True)
                        fw.act(rs[:, :n], ps_[:, :n], AF.Ln, bias=fw.const(RMS_EPS))
                        fw.act(rs[:, :n], rs[:, :n], AF.Exp, scale=-0.5)
                        if ch < 2:
                            fw.stt("dve", o[:, :n], s[:, :n], 128 ** -0.5, rs[:, :n], ALU.mult, ALU.mult)
                        else:
                            fw.tt("dve", o[:, :n], s[:, :n], rs[:, :n], ALU.mult)
                    else:
                        fw.cp("pool", o[:, :n], s[:, :n])
                    fw.dma("pool", QKV[ch][:, t0:t0 + n], o[:, :n])
        if stop_after == "G1":
            return

        with fw.phase():
            tri = [cs[0:64, 128:192], cs[0:64, 192:256]]
            negS = [cs[0:64, 256:320], cs[0:64, 320:384]]
            id64 = cs[0:64, 0:64]
            GM = 8
            def mk_shared(i):
                return dict(
                    fm=[fw.sb("fm%d" % i, [128, GM * 64], BF16) for _ in range(8)],
                    kt=[fw.sb("kt%d" % i, [64, GM, 128], BF16) for _ in range(2)],
                    vt=[fw.sb("vt%d" % i, [64, GM, 128], BF16) for _ in range(4)],
                    kk=[fw.sb("kk%d" % i, [64, GM, 64], F32) for _ in range(2)],
                    at=[fw.sb("at%d" % i, [64, GM, 64], F32) for _ in range(2)],
                    bgf=fw.sb("bgf%d" % i, [16, 2, GM * 64], F32),
                    bT=fw.sb("bT%d" % i, [64, GM, 16], F32),
                    gT=fw.sb("gT%d" % i, [64, GM, 16], F32),
                    gc=[fw.sb("gc%d_%d" % (i, d), [64, GM, 16], F32) for d in range(2)],
                )
            shared = [mk_shared(0), mk_shared(1)]
            prob = {}
            for hv in range(4):
                for d in range(2):
                    prob[(hv, d)] = dict(
                        U=fw.sb("U", [64, GM, 128], BF16), Kd=fw.sb("Kd", [64, GM, 128], BF16),
                        WT=fw.sb("WT", [128, GM, 64], BF16), QdT=fw.sb("QdT", [128, GM, 64], BF16),
                        AT=fw.sb("ATb", [64, GM, 64], BF16), gl=fw.sb("gl", [128, GM], F32),
                        S=fw.sb("S", [128, 128], F32), Sb=fw.sb("Sb", [128, 128], BF16),
                        oo=fw.sb("oo", [128, GM, 64], F32))
                    fw.memset("dve", prob[(hv, d)]["S"], 0.0)
                    fw.memset("pool", prob[(hv, d)]["Sb"], 0.0)
            T = {nm: fw.sb(nm, [64, GM, 64], F32) for nm in ("E1", "E2", "diff", "m1", "DmS", "DmST")}
            NSET = 2
            TS = [{nm: fw.sb(nm, [64, GM, 64], F32) for nm in ("Tt", "X", "Y", "X2", "Y2")} for _ in range(NSET)]
            TtB = fw.sb("TtB", [64, GM, 64], BF16)
            Rv = fw.sb("Rv", [64, GM, 128], BF16)
            Rk = fw.sb("Rk", [64, GM, 128], BF16)
            eg = fw.sb("eg", [128, GM, 64], F32)
            c64 = {nm: fw.sb(nm, [64, GM], F32) for nm in ("egc", "kco", "kdc")}
            vnr = fw.ring("vn", 4, [64, 128], BF16)
            pf = fw.ring("pf", 6, [128, 512], F32, ps=True)
            pb = fw.ring("pb", 2, [128, 1024], BF16, ps=True)

            def b3(x, G, n):
                return B(x.ap.unsqueeze(2).to_broadcast([x.ap.shape[0], G, n]), x.tok)

            def m3(x, G):
                return B(x.ap.unsqueeze(1).to_broadcast([x.ap.shape[0], G, x.ap.shape[1]]), x.tok)

            def prep_group(sh, c0, G):
                t0, n = c0 * 64, G * 64
                for ch in range(8):
                    fw.dma("sp", sh["fm"][ch][:, :n], QKV[ch][:, t0:t0 + n])
                fw.dma("sp", sh["bgf"][:, :, :n], BG[:, :, t0:t0 + n])
                for idx, (src, dst) in enumerate([(2, sh["kt"][0]), (3, sh["kt"][1])] +
                                                 [(4 + v, sh["vt"][v]) for v in range(4)]):
                    p = pb.next()
                    pv = p[0:64, 0:G * 128].re("p (g d) -> p g d", d=128)
                    for g in range(G):
                        fw.op("pe", (lambda o, i: (lambda e: e.transpose(o, i, identb.ap)))(pv[:, g, :].ap, sh["fm"][src][:, g * 64:(g + 1) * 64].ap),
                              reads=[sh["fm"][src], identb], writes=[p])
                    fw.cp("act" if idx % 2 == 0 else "dve", dst[:, :G, :], pv)
                for pl_, dst in ((0, sh["bT"]), (1, sh["gT"])):
                    p = pf.next()
                    pv = p[0:64, 0:G * 16].re("p (g r) -> p g r", r=16)
                    for g in range(G):
                        fw.op("pe", (lambda o, i: (lambda e: e.transpose(o, i, identf[0:16, 0:16].ap)))(pv[:, g, :].ap, sh["bgf"][:, pl_, g * 64:(g + 1) * 64].ap),
                              reads=[sh["bgf"], cs], writes=[p])
                    fw.cp("dve", dst[:, :G, :], pv)
                for d in range(2):
                    p = pf.next()
                    fw.mm(p[0:64, 0:G * 16], tri[d], sh["gT"][:, :G, :].re("p g r -> p (g r)"), True, True)
                    fw.cp("act", sh["gc"][d][:, :G, :], p[0:64, 0:G * 16].re("p (g r) -> p g r", r=16))
                for kh in range(2):
                    KT, QT = sh["fm"][2 + kh], sh["fm"][kh]
                    p = pf.next()
                    for g in range(G):
                        fw.mm(p[0:64, g * 64:(g + 1) * 64], KT[:, g * 64:(g + 1) * 64], KT[:, g * 64:(g + 1) * 64], True, True)
                    fw.cp("act", sh["kk"][kh][:, :G, :], p[0:64, 0:n].re("p (g j) -> p g j", j=64))
                    p = pf.next()
                    for g in range(G):
                        fw.mm(p[0:64, g * 64:(g + 1) * 64], KT[:, g * 64:(g + 1) * 64], QT[:, g * 64:(g + 1) * 64], True, True)
                    fw.cp("dve", sh["at"][kh][:, :G, :], p[0:64, 0:n].re("p (g j) -> p g j", j=64))

            def prep_problem(sh, G, hv, d, ts):
                P = prob[(hv, d)]
                kh = hv // 2
                n = G * 64
                beta = sh["bT"][:, :G, d * 8 + hv]
                gc = sh["gc"][d][:, :G, d * 8 + 4 + hv]
                t = {k: v[:, :G, :] for k, v in T.items()}
                t.update({k: v[:, :G, :] for k, v in TS[ts].items()})
                idG = m3(id64, G)
                fw.tt("dve", t["E1"], idG, b3(gc, G, 64), ALU.mult)
                fw.tt("pool", t["E2"], idG, b3(beta, G, 64), ALU.mult)
                pg = pf.next()
                fw.mm(pg[:, :n], onesf[0:64, :], t["E1"].re("p g j -> p (g j)"), True, True)
                pbt = pf.next()
                fw.mm(pbt[0:64, :n], onesf[0:64, 0:64], t["E2"].re("p g j -> p (g j)"), True, True)
                gcrow = pg[0:64, :n].re("p (g j) -> p g j", j=64)
                brow = pbt[0:64, :n].re("p (g j) -> p g j", j=64)
                last = 63 if d == 0 else 0
                fw.stt("dve", t["diff"], gcrow, -1.0, b3(gc, G, 64), ALU.mult, ALU.add)
                fw.stt("dve", t["m1"], t["diff"], 0.0, m3(negS[d], G), ALU.min, ALU.add)
                fw.act(t["DmS"], t["m1"], AF.Exp)
                fw.ts("pool", t["m1"], t["diff"], -1.0, 0.0, ALU.mult, ALU.min)
                fw.tt("pool", t["m1"], t["m1"], m3(negS[1 - d], G), ALU.add)
                fw.act(t["DmST"], t["m1"], AF.Exp)
                fw.stt("dve", t["X"], sh["kk"][kh][:, :G, :], -1.0, t["DmS"], ALU.mult, ALU.mult)
                fw.tt("dve", t["X"], t["X"], b3(beta, G, 64), ALU.mult)
                fw.tt("pool", t["Y"], sh["kk"][kh][:, :G, :], t["DmST"], ALU.mult)
                fw.stt("dve", t["Y"], t["Y"], -1.0, brow, ALU.mult, ALU.mult)
                fw.tt("pool", t["m1"], t["DmST"], idG, ALU.add)
                fw.tt("pool", P["AT"][:, :G, :], sh["at"][kh][:, :G, :], t["m1"], ALU.mult)
                fw.tt("pool", t["Tt"], t["Y"], idG, ALU.add)
                fw.act(eg[:, :G, :], pg[:, :n].re("p (g j) -> p g j", j=64), AF.Exp)
                fw.cp("dve", P["gl"][:, :G], eg[:, :G, last])
                fw.tt("dve", c64["kdc"][:, :G], gcrow[:, :, last], gc, ALU.subtract)
                fw.act(c64["kdc"][:, :G], c64["kdc"][:, :G], AF.Exp)
                fw.tt("pool", P["QdT"][:, :G, :], sh["fm"][kh][:, :n].re("p (g j) -> p g j", j=64), eg[:, :G, :], ALU.mult)
                fw.tt("pool", P["Kd"][:, :G, :], sh["kt"][kh][:, :G, :], b3(c64["kdc"][:, :G], G, 128), ALU.mult)
                X, Y, X2, Y2 = t["X"], t["Y"], t["X2"], t["Y2"]
                for lv in range(5):
                    px = pf.next()
                    for g in range(G):
                        fw.mm(px[0:64, g * 64:(g + 1) * 64], Y[:, g, :], X[:, g, :], True, True)
                    fw.cp("act", X2, px[0:64, :n].re("p (g j) -> p g j", j=64))
                    if lv < 4:
                        py_ = pf.next()
                        for g in range(G):
                            fw.mm(py_[0:64, g * 64:(g + 1) * 64], X[:, g, :], Y[:, g, :], True, True)
                        fw.cp("dve", Y2, py_[0:64, :n].re("p (g j) -> p g j", j=64))
                    ptt = pf.next()
                    for g in range(G):
                        fw.mm(ptt[0:64, g * 64:(g + 1) * 64], X2[:, g, :], t["Tt"][:, g, :], True, True)
                    fw.tt("dve", t["Tt"], t["Tt"], ptt[0:64, :n].re("p (g j) -> p g j", j=64), ALU.add)
                    X, X2 = X2, X
                    Y, Y2 = Y2, Y
                    yield
                fw.act(c64["egc"][:, :G], gc, AF.Exp)
                fw.tt("dve", c64["kco"][:, :G], c64["egc"][:, :G], beta, ALU.mult)
                fw.tt("dve", Rv[:, :G, :], sh["vt"][hv][:, :G, :], b3(beta, G, 128), ALU.mult)
                fw.tt("dve", Rk[:, :G, :], sh["kt"][kh][:, :G, :], b3(c64["kco"][:, :G], G, 128), ALU.mult)
                fw.cp("pool", TtB[:, :G, :], t["Tt"])
                for hf in range(0, G, 4):
                    g1 = min(G, hf + 4)
                    pu = pf.next()
                    for g in range(hf, g1):
                        fw.mm(pu[0:64, (g - hf) * 128:(g - hf + 1) * 128], TtB[:, g, :], Rv[:, g, :], True, True)
                    fw.cp("act", P["U"][:, hf:g1, :], pu[0:64, 0:(g1 - hf) * 128].re("p (g j) -> p g j", j=128))
                pw = pf.next()
                for g in range(G):
                    fw.mm(pw[:, g * 64:(g + 1) * 64], Rk[:, g, :], TtB[:, g, :], True, True)
                fw.cp("dve", P["WT"][:, :G, :], pw[:, :n].re("p (g j) -> p g j", j=64))

            def scan_step(hv, d, g):
                P = prob[(hv, d)]
                p1 = pf.next()
                fw.mm(p1[0:64, 0:128], P["WT"][:, g, :], P["Sb"], True, True)
                vn = vnr.next()
                fw.tt("dve", vn, P["U"][:, g, :], p1[0:64, 0:128], ALU.subtract)
                po = pf.next()
                fw.mm(po[:, 0:64], P["Sb"], P["QdT"][:, g, :], True, False)
                fw.mm(po[:, 0:64], vn, P["AT"][:, g, :], False, True)
                fw.cp("act", P["oo"][:, g, :], po[:, 0:64])
                ps_ = pf.next()
                fw.mm(ps_[:, 0:128], P["Kd"][:, g, :], vn, True, True)
                fw.stt("dve", P["S"], P["S"], P["gl"][:, g:g + 1], ps_[:, 0:128], ALU.mult, ALU.add)
                fw.cp("pool", P["Sb"], P["S"])

            groups = [(0, 4)] + [(4 + 8 * k, 8) for k in range(nlat // 512)]
            NG = len(groups)
            for it in range(NG):
                gf = groups[it]
                gb = groups[0] if it == 0 else groups[NG - it]
                todo = [(gf, (0,)), (gb, (1,))] if gf != gb else [(gf, (0, 1))]
                plist = []
                for si, ((c0, G), dirs) in enumerate(todo):
                    sh = shared[si]
                    prep_group(sh, c0, G)
                    for hv in range(4):
                        for d in dirs:
                            plist.append((sh, G, hv, d))
                for i0_ in range(0, len(plist), NSET):
                    gens = [prep_problem(*a, ts) for ts, a in enumerate(plist[i0_:i0_ + NSET])]
                    while gens:
                        for g_ in list(gens):
                            try:
                                next(g_)
                            except StopIteration:
                                gens.remove(g_)
                for s in range(8):
                    for hv in range(4):
                        for d in range(2):
                            c0, G = gf if d == 0 else gb
                            if s >= G:
                                continue
                            g = s if d == 0 else G - 1 - s
                            scan_step(hv, d, g)
                for d, (c0, G) in ((0, gf), (1, gb)):
                    for hv in range(4):
                        fw.dma("pool", OD[d][hv][:, c0 * 64:(c0 + G) * 64],
                               prob[(hv, d)]["oo"][:, :G, :].re("p g j -> p (g j)"))
        if stop_after == "G2":
            return

        with fw.phase():
            nw = fw.sb("nw", [128, 1], F32)
            fw.dma("sp", nw, normT)
            ofr = fw.ring("of", 2, [128, 512], F32)
            obr = fw.ring("obb", 2, [128, 512], F32)
            zr = fw.ring("zz", 2, [128, 512], BF16)
            sq = fw.sb("sq", [128, 512], F32)
            rs = fw.sb("rs", [128, 512], F32)
            yr = fw.ring("yy", 8, [128, 512], BF16)
            pss = fw.ring("pss", 2, [128, 512], F32, ps=True)
            if pdst is not None:
                wop = fw.sb("wop", [128, 4, D], BF16)
                fw.dma("sp", wop, w_op_b)
                ppr = fw.ring("ppr", 3, [128, 512], F32, ps=True)
                pst = fw.ring("pst", 3, [128, 512], F32)
            for (t0, n, j) in chunks:
                ys = []
                for hv in range(4):
                    a, b_, z = ofr.next(), obr.next(), zr.next()
                    fw.dma("sp", a[:, :n], OD[0][hv][:, t0:t0 + n])
                    fw.dma("sp", b_[:, :n], OD[1][hv][:, t0:t0 + n])
                    fw.dma("sp", z[:, :n], ZS[hv][:, t0:t0 + n])
                    fw.tt("dve", a[:, :n], a[:, :n], b_[:, :n], ALU.add)
                    fw.tt("pool", sq[:, :n], a[:, :n], a[:, :n], ALU.mult)
                    p = pss.next()
                    fw.mm(p[:, :n], onesf, sq[:, :n], True, True)
                    fw.act(rs[:, :n], p[:, :n], AF.Ln, bias=fw.const(RMS_EPS), scale=1.0 / 128)
                    fw.act(rs[:, :n], rs[:, :n], AF.Exp, scale=-0.5)
                    fw.stt("dve", a[:, :n], a[:, :n], nw[:, 0:1], rs[:, :n], ALU.mult, ALU.mult)
                    y = yr.next()
                    fw.tt("pool", y[:, :n], a[:, :n], z[:, :n], ALU.mult)
                    ys.append(y)
                    if ydst is not None:
                        fw.dma("pool", ydst(hv, t0, n), y[:, :n])
                if pdst is not None and t0 >= CTX:
                    for oc in range(8):
                        pp_ = ppr.next()
                        for hv in range(4):
                            fw.mm(pp_[:, :n], wop[:, hv, oc * 128:(oc + 1) * 128], ys[hv][:, :n], hv == 0, hv == 3)
                        sg_ = pst.next()
                        fw.cp("act" if oc % 2 == 0 else "dve", sg_[:, :n], pp_[:, :n])
                        fw.dma("pool", pdst(oc, t0, n), sg_[:, :n])
        return modT


def build_l2():
    nc = bass.Bass("TRN2", target_bir_lowering=False)
    with ExitStack() as st:
        fw = FW(nc, st)
        emit_l2(fw)
        fw.finish()
    return nc


def emit_l2(fw, pre="", mixT=None, resT=None, modT=None, mixload=None, ydirect=None):
    if True:
        fw = _Pre(fw, pre)
        EI = "ExternalInput"
        if mixT is None and ydirect is None:
            mixT = fw.dram("mixT", [2048, NQ], BF16, EI)
            resT = fw.dram("resT", [D, NQ], F32, EI)
        if modT is None:
            cc = fw.dram("cc", [128, 8, 2], F32, EI)
            wmod = fw.dram("wmod", [D, 6 * D], F32, EI)
            bmodT = fw.dram("bmodT", [128, 48], F32, EI)
        lnTd = fw.dram("lnT", [128, 4, 8], F32, EI)
        if ydirect is None:
            w_o = fw.dram("w_o", [2048, D], F32, EI)
        w_fi = fw.dram("w_fi", [D, 2 * DFF], F32, EI)
        w_fo = fw.dram("w_fo", [DFF, D], F32, EI)
        outT = fw.dram("outT", [D, NQ], F32, "ExternalOutput")
        w_o_b = fw.dram("w_o_b", [128, 16, D], BF16)
        w_fi_b = fw.dram("w_fi_b", [128, 8, 2 * DFF], BF16)
        w_fo_b = fw.dram("w_fo_b", [128, 22, D], BF16)
        have_mod = modT is not None
        if not have_mod:
            modT = fw.sb("modT", [128, 48, 2], F32)
        lnT = fw.sb("lnTs", [128, 4, 8], F32)
        fw.dma("sp", lnT, lnTd)
        fw.const(LN_EPS)
        if not have_mod:
            mod_phase(fw, cc, wmod, bmodT, modT)
        if ydirect is None:
            conv_weight(fw, w_o, w_o_b, 2048, D)
        conv_weight(fw, w_fi, w_fi_b, D, 2 * DFF)
        conv_weight(fw, w_fo, w_fo_b, DFF, D)
        chunks = [(256 * i, 256, 0) for i in range(16)]
        post_phase(fw, mixT, 16, resT, w_o_b, w_fi_b, w_fo_b, modT, lnT, outT, chunks, mixload=mixload, ydirect=ydirect)


def _l1_cols(hg):
    q = np.arange(2 * hg * 128, (2 * hg + 2) * 128)
    k = 1024 + q
    v = 2048 + np.arange(4 * hg * 128, (4 * hg + 4) * 128)
    z = 2048 + v
    ba = np.array([6144 + d * 32 + ab * 16 + 4 * hg + h for d in range(2) for ab in range(2) for h in range(4)])
    return np.concatenate([q, k, v, z, ba])


def _cst():
    c = np.zeros((128, 384), np.float32)
    c[:, 0:128] = np.eye(128, dtype=np.float32)
    p = np.arange(64)[:, None]
    f = np.arange(64)[None, :]
    c[0:64, 128:192] = (p <= f)
    c[0:64, 192:256] = (p >= f)
    c[0:64, 256:320] = np.where(f < p, 0.0, NEG)
    c[0:64, 320:384] = np.where(f > p, 0.0, NEG)
    return c


def prep_l1(inp, hT_all, b, hg):
    cols = _l1_cols(hg)
    conv = inp["gdn_conv"][0][:, cols[:1024]]
    ab = np.zeros((16, 2), np.float32)
    for d in range(2):
        ab[d * 8 + 4:d * 8 + 8, 0] = inp["gdn_dt_bias"][0][d, 4 * hg:4 * hg + 4]
        ab[d * 8 + 4:d * 8 + 8, 1] = inp["gdn_a_log"][0][d, 4 * hg:4 * hg + 4]
    return {
        "hT": hT_all,
        "cc": np.ascontiguousarray(np.stack([_fm(inp["c"][b], 8), _fm(inp["c_ctx"], 8)], axis=-1)),
        "wmod": np.ascontiguousarray(inp["w_mod"][1]),
        "bmodT": _fm(inp["b_mod"][1], 48),
        "w_g": np.ascontiguousarray(inp["gdn_w_in"][0][:, cols]),
        "convT": np.ascontiguousarray(conv.T.reshape(8, 128, 5).transpose(1, 0, 2)),
        "abT": ab,
        "normT": np.ascontiguousarray(inp["gdn_norm"][0].reshape(128, 1).astype(np.float32)),
        "cst": _cst(),
    }


def prep_l2(inp, mixT, resT, b):
    return {
        "mixT": mixT, "resT": resT,
        "cc": np.ascontiguousarray(np.stack([_fm(inp["c"][b], 8), _fm(inp["c_ctx"], 8)], axis=-1)),
        "wmod": np.ascontiguousarray(inp["w_mod"][1]),
        "bmodT": _fm(inp["b_mod"][1], 48),
        "lnT": np.ascontiguousarray(np.stack([_fm(inp[k][1], 8) for k in ("ln1_g", "ln1_b", "ln2_g", "ln2_b")], axis=1)),
        "w_o": np.ascontiguousarray(inp["gdn_w_out"][0]),
        "w_fi": np.ascontiguousarray(inp["w_ffn_in"][1]),
        "w_fo": np.ascontiguousarray(inp["w_ffn_out"][1]),
    }


def _bf(a):
    return a.view(ml_dtypes.bfloat16) if a.dtype.kind == "V" else a


def kernel_unfused(**inp):
    inp = {k: np.asarray(v) for k, v in inp.items()}
    cores = list(range(8))
    tab = _rope_tables()
    r0 = run_bass_kernel_spmd(build_l0(), [prep_l0(inp, c // 4, c % 4, tab) for c in cores], core_ids=cores).results
    hT = []
    for b in range(2):
        parts = [r0[4 * b]["outT"][:, :CTX]] + [r0[4 * b + q]["outT"][:, CTX:] for q in range(4)]
        hT.append(np.ascontiguousarray(np.concatenate(parts, axis=1)))
    r1 = run_bass_kernel_spmd(build_l1(), [prep_l1(inp, hT[c // 4], c // 4, c % 4) for c in cores], core_ids=cores).results
    maps = []
    for c in cores:
        b, q = c // 4, c % 4
        sl = slice(CTX + q * NQ, CTX + (q + 1) * NQ)
        mix = np.concatenate([_bf(r1[4 * b + hg]["yT"])[:, sl] for hg in range(4)], axis=0)
        maps.append(prep_l2(inp, np.ascontiguousarray(mix), np.ascontiguousarray(hT[b][:, sl]), b))
    r2 = run_bass_kernel_spmd(build_l2(), maps, core_ids=cores).results
    out = np.empty((2, SEQ, D), np.float32)
    for c in cores:
        b, q = c // 4, c % 4
        out[b, q * NQ:(q + 1) * NQ, :] = r2[c]["outT"].T
    return out


GROUPS = [[0, 1, 2, 3], [4, 5, 6, 7]]


def collective(fw, kind, src, dst, op=None):
    tok = Tok()
    tok.sem = fw.getsem()
    key = tok.sem
    waits = fw._waits("pool", [src.tok], [dst.tok])
    fw.cnt[key] += 1
    v = fw.cnt[key]
    sa, da = src.ap, dst.ap
    aop = ALU.bypass if op is None else op
    fw.lists["pool"].append((waits, lambda e: e.collective_compute(kind, aop, replica_groups=GROUPS,
                                                                 ins=[sa.opt()], outs=[da.opt()]), key, 1))
    dst.tok.w = {key: v}
    dst.tok.r = {}
    src.tok.r[key] = v


def build_fused():
    nc = bass.Bass("TRN2", target_bir_lowering=False)
    with ExitStack() as st:
        fw = FW(nc, st)
        HC = 2048
        h2own = fw.dram("h2own", [D, NO], F32)
        hsend = [[fw.dram("hsend_%d_%d" % (k, c), [128, HC], F32) for c in range(2)] for k in range(8)]
        hgat = [[fw.dram("hgat_%d_%d" % (k, c), [4 * 128, HC], F32) for c in range(2)] for k in range(8)]
        ppart = [fw.dram("ppart_%d" % i, [4 * 256, NQ], F32) for i in range(4)]
        ysum = [fw.dram("ysum_%d" % i, [256, NQ], F32) for i in range(4)]
        emit_l0(fw, pre="a_", outT=h2own)
        with fw.phase():
            bounce = fw.ring("bnc", 2, [128, 8, 512], F32)
            sv = h2own.re("(k p) t -> p k t", p=128)
            for i in range(NQ // 512):
                b_ = bounce.next()
                fw.dma("sp", b_, sv[:, :, CTX + i * 512:CTX + (i + 1) * 512])
                c, o2 = (i * 512) // HC, (i * 512) % HC
                for k in range(8):
                    fw.dma("pool", hsend[k][c][:, o2:o2 + 512], b_[:, k, :])
        for k in range(8):
            for c in range(2):
                collective(fw, "AllGather", hsend[k][c], hgat[k][c])
        ctxv = h2own.re("(k p) t -> p k t", p=128)

        def hsrc(t0, n):
            if t0 < CTX:
                return ctxv[:, :, t0:t0 + n]
            q, off = (t0 - CTX) // NQ, (t0 - CTX) % NQ
            c, o2 = off // HC, off % HC
            return [hgat[k][c][q * 128:(q + 1) * 128, o2:o2 + n] for k in range(8)]

        def pdst(oc, t0, n):
            q, off = (t0 - CTX) // NQ, (t0 - CTX) % NQ
            r0 = q * 256 + (oc % 2) * 128
            return ppart[oc // 2][r0:r0 + 128, off:off + n]

        modT1 = emit_l1(fw, pre="b_", hsrc=hsrc, pdst=pdst)
        for i in range(4):
            collective(fw, "ReduceScatter", ppart[i], ysum[i], op=ALU.add)
        emit_l2(fw, pre="c_", resT=h2own[:, CTX:], modT=modT1, ydirect=ysum)
        fw.finish()
    return nc


def kernel(**inp):
    inp = {k: np.asarray(v) for k, v in inp.items()}
    cores = list(range(8))
    tab = _rope_tables()
    maps = []
    for c in cores:
        b, r = c // 4, c % 4
        m = {"a_" + k: v for k, v in prep_l0(inp, b, r, tab).items()}
        l1 = prep_l1(inp, None, b, r)
        l1.pop("hT")
        m.update({"b_" + k: v for k, v in l1.items()})
        l2 = prep_l2(inp, None, None, b)
        for k in ("mixT", "resT", "cc", "wmod", "bmodT", "w_o"):
            l2.pop(k)
        m.update({"c_" + k: v for k, v in l2.items()})
        m["b_w_op"] = np.ascontiguousarray(inp["gdn_w_out"][0][r * 512:(r + 1) * 512])
        maps.append(m)
    res = run_bass_kernel_spmd(build_fused(), maps, core_ids=cores).results
    out = np.empty((2, SEQ, D), np.float32)
    for c in cores:
        b, q = c // 4, c % 4
        out[b, q * NQ:(q + 1) * NQ, :] = res[c]["c_outT"].T
    return out
```
